# Optimizing a Trainium2 kernel written in Bass

```python
import math
import jax, jax.numpy as jnp
from jax import lax
import numpy as np

D_MODEL = 2048
BATCH = 2
SEQ = 4096
DEPTH = 4

HEAD_DIM = 128
BLOCK = 128
EPS = 1e-6
A_PATTERNS = ((128, 1), (512, 4), (2048, 16))
A_GROUPS = 3
A_HEADS_PER_GROUP = 4
A_HEADS = A_GROUPS * A_HEADS_PER_GROUP
A_OUT = A_HEADS_PER_GROUP * HEAD_DIM
B_Q_HEADS = 8
B_KV_HEADS = 2
B_WINDOW = 128
B_OUT = B_Q_HEADS * HEAD_DIM
C_HEADS = 4
C_OUT = C_HEADS * 2 * HEAD_DIM
N_MEM = 256
M_HEADS = 4
M_OUT = M_HEADS * HEAD_DIM
N_BRANCH = 4
BRANCH_WIDTHS = (A_OUT, B_OUT, C_OUT, M_OUT)
BRANCH_TOTAL = A_OUT + B_OUT + C_OUT + M_OUT
QK_A_Q, QK_A_K, QK_B_Q, QK_B_K, QK_C_Q, QK_C_K, QK_M_Q, QK_M_K = 0, 1, 2, 3, 4, 5, 6, 7
N_QK_NORMS = 8
N_LAMBDA_VECS = 4
IN_SIZES = (A_HEADS * HEAD_DIM, A_HEADS * HEAD_DIM, A_HEADS * HEAD_DIM,
            B_Q_HEADS * HEAD_DIM, B_KV_HEADS * HEAD_DIM, B_KV_HEADS * HEAD_DIM,
            C_HEADS * 2 * HEAD_DIM, C_HEADS * 2 * HEAD_DIM, C_OUT,
            M_HEADS * HEAD_DIM,
            A_OUT, B_OUT, C_OUT, M_OUT,
            N_BRANCH * D_MODEL)
D_IN = sum(IN_SIZES)

kernel_name = "hybrid_dilated_swa_diff_memory_gated"


def rms_norm(x, g):
    xf = x.astype(jnp.float32)
    y = xf * lax.rsqrt(jnp.mean(xf * xf, axis=-1, keepdims=True) + EPS)
    return (y * g.astype(jnp.float32)).astype(x.dtype)


def alibi_slopes(n):
    return 2.0 ** (-8.0 * jnp.arange(1, n + 1, dtype=jnp.float32) / n)


def strided(x, d):
    b, s = x.shape[:2]
    x = x.reshape(b, s // d, d, *x.shape[2:])
    x = jnp.moveaxis(x, 2, 1)
    return x.reshape(b * d, s // d, *x.shape[3:])


def unstrided(x, b, d):
    x = x.reshape(b, d, x.shape[1], *x.shape[2:])
    x = jnp.moveaxis(x, 1, 2)
    return x.reshape(b, -1, *x.shape[3:])


def banded_attention(q, k, v, slopes, max_dist, dist_scale, sinks):
    n, L, hkv, g, hd = q.shape
    nb = -(-L // BLOCK)
    lp = nb * BLOCK
    pad = lp - L
    q = jnp.pad(q, ((0, 0), (0, pad), (0, 0), (0, 0), (0, 0)))
    k = jnp.pad(k, ((0, 0), (BLOCK, pad), (0, 0), (0, 0)))
    v = jnp.pad(v, ((0, 0), (BLOCK, pad), (0, 0), (0, 0)))
    qb = q.reshape(n, nb, BLOCK, hkv, g, hd)
    kb = k.reshape(n, nb + 1, BLOCK, hkv, hd)
    vb = v.reshape(n, nb + 1, BLOCK, hkv, hd)
    kw = jnp.concatenate([kb[:, :-1], kb[:, 1:]], axis=2)
    vw = jnp.concatenate([vb[:, :-1], vb[:, 1:]], axis=2)
    s = jnp.einsum('nbqhgd,nbkhd->nbhgqk', qb, kw).astype(jnp.float32) * (hd ** -0.5)
    dist = jnp.arange(BLOCK)[:, None] + BLOCK - jnp.arange(2 * BLOCK)[None, :]
    key_idx = (jnp.arange(nb)[:, None] - 1) * BLOCK + jnp.arange(2 * BLOCK)[None, :]
    valid = (dist >= 0)[None] & (dist <= max_dist)[None] & (key_idx >= 0)[:, None, :]
    bias = -slopes.astype(jnp.float32)[:, :, None, None] * (dist * dist_scale).astype(jnp.float32)
    s = jnp.where(valid[None, :, None, None], s + bias[None, None], -jnp.inf)
    m = jnp.max(s, axis=-1, keepdims=True)
    if sinks is not None:
        sink = sinks.astype(jnp.float32)[None, None, :, :, None, None]
        m = jnp.maximum(m, sink)
    e = jnp.exp(s - m)
    denom = jnp.sum(e, axis=-1, keepdims=True)
    if sinks is not None:
        denom = denom + jnp.exp(sink - m)
    p = e / denom
    lse = (m + jnp.log(denom))[..., 0]
    o = jnp.einsum('nbhgqk,nbkhd->nbqhgd', p.astype(v.dtype), vw)
    o = o.reshape(n, lp, hkv, g, hd)[:, :L]
    lse = jnp.transpose(lse, (0, 1, 4, 2, 3)).reshape(n, lp, hkv, g)[:, :L]
    return o, lse


def diff_attention(q, k, v, slopes, lam):
    b, s_len, h, _, hd = q.shape
    nb = s_len // BLOCK
    qb = jnp.moveaxis(q.reshape(b, nb, BLOCK, h, 2, hd), 1, 0)
    kpos = jnp.arange(s_len)
    sl = slopes.astype(jnp.float32)[None, :, None, None, None]

    def one_block(args):
        qi, i = args
        sc = jnp.einsum('bqhcd,bkhcd->bhcqk', qi, k).astype(jnp.float32) * (hd ** -0.5)
        dist = (i * BLOCK + jnp.arange(BLOCK))[:, None] - kpos[None, :]
        sc = jnp.where(dist >= 0, sc - sl * dist.astype(jnp.float32), -jnp.inf)
        p = jax.nn.softmax(sc, axis=-1)
        a = p[:, :, 0] - lam * p[:, :, 1]
        return jnp.einsum('bhqk,bkhe->bqhe', a.astype(v.dtype), v)

    o = lax.map(one_block, (qb, jnp.arange(nb)))
    return jnp.moveaxis(o, 0, 1).reshape(b, s_len, h, 2 * hd)


def setup_inputs(seed: int = 0) -> dict:
    key = jax.random.key(seed)
    ks = jax.random.split(key, 13)
    f = jnp.float32
    nrm = jax.random.normal
    x = nrm(ks[0], (BATCH, SEQ, D_MODEL), f)
    mem = nrm(ks[1], (BATCH, N_MEM, D_MODEL), f)
    norm_g = 1.0 + 0.02 * nrm(ks[2], (DEPTH, D_MODEL), f)
    w_in = nrm(ks[3], (DEPTH, D_MODEL, D_IN), f) * (D_MODEL ** -0.5)
    b_gate = 0.1 * nrm(ks[4], (DEPTH, N_BRANCH, D_MODEL), f)
    qk_gain = 1.0 + 0.02 * nrm(ks[5], (DEPTH, N_QK_NORMS, HEAD_DIM), f)
    sinks = 0.5 * nrm(ks[6], (DEPTH, B_Q_HEADS), f)
    lam = 0.1 * nrm(ks[7], (DEPTH, N_LAMBDA_VECS, HEAD_DIM), f)
    subln_g = 1.0 + 0.02 * nrm(ks[8], (DEPTH, 2 * HEAD_DIM), f)
    mem_norm_g = 1.0 + 0.02 * nrm(ks[9], (DEPTH, D_MODEL), f)
    w_mem_kv = nrm(ks[10], (DEPTH, D_MODEL, 2 * M_HEADS * HEAD_DIM), f) * (D_MODEL ** -0.5)
    row_scale = jnp.concatenate([jnp.full((w,), w ** -0.5, f) for w in BRANCH_WIDTHS])
    w_branch = nrm(ks[11], (DEPTH, BRANCH_TOTAL, D_MODEL), f) * row_scale[None, :, None]
    w_out = nrm(ks[12], (DEPTH, D_MODEL, D_MODEL), f) * (D_MODEL ** -0.5)
    return {"x": x, "mem": mem, "norm_g": norm_g, "w_in": w_in, "b_gate": b_gate,
            "qk_gain": qk_gain, "sinks": sinks, "lam": lam, "subln_g": subln_g,
            "mem_norm_g": mem_norm_g, "w_mem_kv": w_mem_kv, "w_branch": w_branch, "w_out": w_out}


def reference(x, mem, norm_g, w_in, b_gate, qk_gain, sinks, lam, subln_g, mem_norm_g, w_mem_kv, w_branch, w_out):
    bsz, s_len, _ = x.shape
    split_at = [int(c) for c in np.cumsum(IN_SIZES)[:-1]]
    branch_at = [int(c) for c in np.cumsum(BRANCH_WIDTHS)[:-1]]
    slopes_a = alibi_slopes(A_HEADS_PER_GROUP)[:, None]
    slopes_b = alibi_slopes(B_Q_HEADS).reshape(B_KV_HEADS, -1)
    slopes_c = alibi_slopes(C_HEADS)
    m_scale = HEAD_DIM ** -0.5
    for l in range(DEPTH):
        h = rms_norm(x, norm_g[l])
        proj = jnp.einsum('bsd,de->bse', h, w_in[l])
        (aq, ak, av, bq, bk, bv, cq, ck, cv, mq,
         za, zb, zc, zm, gates) = jnp.split(proj, split_at, axis=-1)
        gq = qk_gain[l]

        aq = rms_norm(aq.reshape(bsz, s_len, A_GROUPS, A_HEADS_PER_GROUP, HEAD_DIM), gq[QK_A_Q])
        ak = rms_norm(ak.reshape(bsz, s_len, A_GROUPS, A_HEADS_PER_GROUP, HEAD_DIM), gq[QK_A_K])
        av = av.reshape(bsz, s_len, A_GROUPS, A_HEADS_PER_GROUP, HEAD_DIM)
        outs, lses = [], []
        for g, (win, dil) in enumerate(A_PATTERNS):
            o, lse = banded_attention(strided(aq[:, :, g], dil)[:, :, :, None], strided(ak[:, :, g], dil),
                                      strided(av[:, :, g], dil), slopes_a, win // dil, dil, None)
            outs.append(unstrided(o[:, :, :, 0], bsz, dil))
            lses.append(unstrided(lse[..., 0], bsz, dil))
        alpha = jax.nn.softmax(jnp.stack(lses), axis=0)
        ya = jnp.sum(alpha[..., None].astype(x.dtype) * jnp.stack(outs), axis=0).reshape(bsz, s_len, A_OUT)

        bq = rms_norm(bq.reshape(bsz, s_len, B_KV_HEADS, B_Q_HEADS // B_KV_HEADS, HEAD_DIM), gq[QK_B_Q])
        bk = rms_norm(bk.reshape(bsz, s_len, B_KV_HEADS, HEAD_DIM), gq[QK_B_K])
        bv = bv.reshape(bsz, s_len, B_KV_HEADS, HEAD_DIM)
        yb, _ = banded_attention(bq, bk, bv, slopes_b, B_WINDOW - 1, 1,
                                 sinks[l].reshape(B_KV_HEADS, -1))
        yb = yb.reshape(bsz, s_len, B_OUT)

        lam_init = 0.8 - 0.6 * math.exp(-0.3 * l)
        lp = lam[l].astype(jnp.float32)
        lam_full = jnp.exp(jnp.sum(lp[0] * lp[1])) - jnp.exp(jnp.sum(lp[2] * lp[3])) + lam_init
        cq = rms_norm(cq.reshape(bsz, s_len, C_HEADS, 2, HEAD_DIM), gq[QK_C_Q])
        ck = rms_norm(ck.reshape(bsz, s_len, C_HEADS, 2, HEAD_DIM), gq[QK_C_K])
        cv = cv.reshape(bsz, s_len, C_HEADS, 2 * HEAD_DIM)
        yc = diff_attention(cq, ck, cv, slopes_c, lam_full)
        yc = (rms_norm(yc, subln_g[l]) * (1.0 - lam_init)).reshape(bsz, s_len, C_OUT)

        mn = rms_norm(mem, mem_norm_g[l])
        mk, mv = jnp.split(jnp.einsum('bmd,de->bme', mn, w_mem_kv[l]), 2, axis=-1)
        n_mem = mem.shape[1]
        mk = rms_norm(mk.reshape(bsz, n_mem, M_HEADS, HEAD_DIM), gq[QK_M_K])
        mv = mv.reshape(bsz, n_mem, M_HEADS, HEAD_DIM)
        mq = rms_norm(mq.reshape(bsz, s_len, M_HEADS, HEAD_DIM), gq[QK_M_Q])
        pm = jax.nn.softmax(jnp.einsum('bshd,bmhd->bhsm', mq, mk).astype(jnp.float32) * m_scale, axis=-1)
        ym = jnp.einsum('bhsm,bmhd->bshd', pm.astype(mv.dtype), mv).reshape(bsz, s_len, M_OUT)

        gate = jax.nn.sigmoid((gates + b_gate[l].reshape(-1)).astype(jnp.float32)).astype(x.dtype)
        gate = gate.reshape(bsz, s_len, N_BRANCH, D_MODEL)
        w_parts = jnp.split(w_branch[l], branch_at, axis=0)
        branches = (ya * jax.nn.silu(za), yb * jax.nn.silu(zb), yc * jax.nn.silu(zc), ym * jax.nn.silu(zm))
        merged = gate[:, :, 0] * jnp.einsum('bse,ed->bsd', branches[0], w_parts[0])
        for i in range(1, N_BRANCH):
            merged = merged + gate[:, :, i] * jnp.einsum('bse,ed->bsd', branches[i], w_parts[i])
        x = x + jnp.einsum('bsd,de->bse', merged, w_out[l])
    return x
```

```python
import contextlib
import math
import os
_ASK = os.environ.get('ASKIP', '')
import numpy as np
import ml_dtypes
import concourse.bass as bass
import concourse.mybir as mybir
from concourse.bass_utils import run_bass_kernel_spmd

F32 = mybir.dt.float32
BF16 = mybir.dt.bfloat16
AF = mybir.ActivationFunctionType
ALU = mybir.AluOpType
AX = mybir.AxisListType
EPS = 1e-6
NEG = -30000.0
D = 2048
T = 1024
KC = 16
NCORE = 8


class Buf:
    __slots__ = ("last_w", "readers", "dw")

    def __init__(self):
        self.last_w = None
        self.readers = []
        self.dw = []


class Op:
    __slots__ = ("eng", "fn", "deps", "signal", "seq", "idx", "dma", "sem", "val", "inc", "ring", "epoch")


class Prog:
    def __init__(self):
        self.ops = []
        self.ring = {"sp": 16, "pool": 8, "act": 2}
        self.fence = []
        self.since = []
        self.last = {}
        self.epoch = 0

    def op(self, eng, fn, reads=(), writes=(), dma=False, inc=16, ring=None):
        o = Op()
        o.ring = ring or eng
        o.epoch = self.epoch
        o.eng, o.fn, o.idx, o.signal, o.dma, o.seq, o.sem, o.val, o.inc = eng, fn, len(self.ops), False, dma, None, None, None, inc
        deps = {}
        for b in reads:
            if b.last_w is not None:
                deps[b.last_w.idx] = b.last_w
            for w in b.dw:
                deps[w.idx] = w
        for b in writes:
            if b.last_w is not None:
                deps[b.last_w.idx] = b.last_w
            if not dma:
                for w in b.dw:
                    deps[w.idx] = w
            for r in b.readers:
                deps[r.idx] = r
        for f in self.fence:
            deps[f.idx] = f
        o.deps = list(deps.values())
        for b in reads:
            b.readers.append(o)
        for b in writes:
            if dma:
                b.dw.append(o)
            else:
                b.last_w = o
                b.dw = []
            b.readers = []
        self.ops.append(o)
        if dma:
            self.since.append(o)
        else:
            self.last[eng] = o
        return o

    def barrier(self):
        self.fence = list(self.last.values()) + list(self.since)
        self.since = []

    def dma(self, eng, out, in_, reads=(), writes=()):
        return self.op(eng, lambda e: e.dma_start(out=out, in_=in_), reads, writes, dma=True)

    def emit(self, nc, final_ops=()):
        ops = self.ops
        engs = ["pe", "act", "dve", "pool", "sp"]
        for o in ops:
            for d in o.deps:
                if d.dma:
                    continue
                if d.eng == "pe" and o.eng == "pe" and not o.dma:
                    continue
                d.signal = True
        seqc = {}
        for o in ops:
            if not o.dma and o.signal:
                k = (o.eng, o.epoch)
                seqc[k] = seqc.get(k, 0) + 1
                o.seq = seqc[k]
        stack = contextlib.ExitStack()
        prog_sem = {k: stack.enter_context(nc.semaphore("prg_%s%d" % k)) for k in seqc}
        ring_sems = {q: [stack.enter_context(nc.semaphore("rg_%s%d" % (q, i))) for i in range(n)]
                     for q, n in self.ring.items()}
        ring_cnt = {q: [0] * n for q, n in self.ring.items()}
        ring_last = {q: [None] * n for q, n in self.ring.items()}
        ring_pos = {q: 0 for q in self.ring}
        for o in ops:
            if o.dma:
                q = o.ring
                i = ring_pos[q]
                ring_pos[q] = (i + 1) % self.ring[q]
                prev = ring_last[q][i]
                if prev is not None:
                    o.deps.append(prev)
                ring_cnt[q][i] += o.inc
                o.sem = ring_sems[q][i]
                o.val = ring_cnt[q][i]
                ring_last[q][i] = o
            elif o.signal:
                o.sem = prog_sem[(o.eng, o.epoch)]
                o.val = o.seq
        per_eng = {e: [o for o in ops if o.eng == e] for e in engs}
        block = stack.enter_context(nc.Block())

        def run(eng_name, handle):
            waited = {}
            for o in per_eng[eng_name]:
                need = {}
                for d in o.deps:
                    if (not d.dma) and d.eng == "pe" and eng_name == "pe" and not o.dma:
                        continue
                    k = id(d.sem)
                    if k not in need or need[k][1] < d.val:
                        need[k] = (d.sem, d.val)
                for k, (s, v) in need.items():
                    if waited.get(k, 0) >= v:
                        continue
                    handle.wait_ge(s, v)
                    waited[k] = v
                ins = o.fn(handle)
                if o.dma:
                    ins.then_inc(o.sem, o.inc)
                elif o.signal:
                    ins.then_inc(o.sem, 1)
            if eng_name == "sp":
                for o in final_ops:
                    handle.wait_ge(o.sem, o.val)

        block.tensor(lambda e: run("pe", e))
        block.scalar(lambda e: run("act", e))
        block.vector(lambda e: run("dve", e))
        block.gpsimd(lambda e: run("pool", e))
        block.sync(lambda e: run("sp", e))
        stack.close()


class TB:
    def __init__(self, t):
        self.t = t
        self.b = Buf()


class KB:
    def __init__(self):
        self.nc = bass.Bass("TRN2", target_bir_lowering=False)
        self.P = Prog()
        self.st = contextlib.ExitStack()
        self.n = 0

    def sb(self, shape, dt):
        self.n += 1
        return TB(self.st.enter_context(self.nc.sbuf_tensor("sb%d" % self.n, list(shape), dt)))

    def ps(self, shape, dt=F32):
        self.n += 1
        return TB(self.st.enter_context(self.nc.psum_tensor("ps%d" % self.n, list(shape), dt)))

    def din(self, name, shape, dt=F32):
        return self.nc.dram_tensor(name, list(shape), dt, kind="ExternalInput").ap()

    def dout(self, name, shape, dt=F32):
        return self.nc.dram_tensor(name, list(shape), dt, kind="ExternalOutput").ap()


def common_setup(kb, consts_d):
    P = kb.P
    c = kb.sb([128, 384], BF16)
    P.dma("sp", c.t[:], consts_d, writes=[c.b])
    kb.cst = c
    kb.ident = c.t[:, 0:128]
    kb.ones = c.t[:, 128:256]
    kb.ones128 = c.t[:, 256:384]
    kb.p0 = kb.ps([128, 512])
    kb.p1 = kb.ps([128, 512])
    kb.pss = kb.ps([128, 512])
    kb.s0 = kb.ps([128, 512])
    kb.s1 = kb.ps([128, 512])
    kb.ud0 = kb.ps([128, 512])
    kb.ud1 = kb.ps([128, 512])
    kb.ptr = kb.ps([128, 1024], BF16)
    kb.ring = [kb.sb([128, 8192], BF16) for _ in range(2)]
    kb.ring_i = 0
    kb.sq = [kb.sb([128, 512], BF16) for _ in range(2)]
    kb.rstd = [kb.sb([128, 512], F32) for _ in range(2)]
    kb.qn_i = 0
    kb.pp_i = 0
    kb.epsc = kb.sb([128, 1], F32)
    P.op("dve", lambda e: e.memset(kb.epsc.t[:], EPS), writes=[kb.epsc.b])


def next_ring(kb):
    r = kb.ring[kb.ring_i % len(kb.ring)]
    kb.ring_i += 1
    return r


def next_pp(kb):
    lst = getattr(kb, "pp_list", None) or (kb.p0, kb.p1)
    p = lst[kb.pp_i % len(lst)]
    kb.pp_i += 1
    return p


def load_w(kb, w_d, c0, nc_, nk=KC):
    r = next_ring(kb)
    view = r.t[:, 0:nk * nc_].rearrange("p (k c) -> p k c", k=nk)
    src = w_d[:, c0:c0 + nc_].rearrange("(k p) c -> p k c", p=128)
    kb.P.dma("pool", view, src, writes=[r.b])
    return view, r.b


def norm_to_hT(kb, x_d, ngb, hT, ntile, scr, xdep=()):
    P = kb.P
    xt, sqf, hb, ss, rs = scr
    for t in range(ntile):
        xs = xt[t % 2]
        P.dma("sp", xs.t[:], x_d[t * 128:(t + 1) * 128, :], reads=list(xdep), writes=[xs.b])
        P.op("act", lambda e, xs=xs: e.activation(out=sqf.t[:], in_=xs.t[:], func=AF.Square), reads=[xs.b], writes=[sqf.b])
        P.op("dve", lambda e: e.reduce_sum(out=ss.t[:], in_=sqf.t[:], axis=AX.X), reads=[sqf.b], writes=[ss.b])
        P.op("act", lambda e: e.activation(out=rs.t[:], in_=ss.t[:], func=AF.Ln, bias=kb.epsc.t[:, 0:1], scale=1.0 / D),
             reads=[ss.b, kb.epsc.b], writes=[rs.b])
        P.op("act", lambda e: e.activation(out=rs.t[:], in_=rs.t[:], func=AF.Exp, scale=-0.5), reads=[rs.b], writes=[rs.b])
        P.op("dve", lambda e, xs=xs: e.scalar_tensor_tensor(out=hb.t[:], in0=xs.t[:], scalar=rs.t[:, 0:1], in1=ngb.t[:],
                                                            op0=ALU.mult, op1=ALU.mult),
             reads=[xs.b, rs.b, ngb.b], writes=[hb.b])
        for k0 in range(0, KC, 4):
            def tr(e, k0=k0):
                last = None
                for i in range(4):
                    last = e.transpose(out=kb.ptr.t[:, i * 128:(i + 1) * 128], in_=hb.t[:, (k0 + i) * 128:(k0 + i + 1) * 128],
                                       identity=kb.ident)
                return last
            P.op("pe", tr, reads=[hb.b, kb.cst.b], writes=[kb.ptr.b])
            src = kb.ptr.t[:, 0:512].rearrange("p (a b) -> p a b", a=4)
            dst = hT.t[:, k0:k0 + 4, t * 128:(t + 1) * 128]
            if (k0 // 4) % 2 == 0:
                P.op("act", lambda e, src=src, dst=dst: e.activation(out=dst, in_=src, func=AF.Copy), reads=[kb.ptr.b], writes=[hT.b])
            else:
                P.op("dve", lambda e, src=src, dst=dst: e.tensor_copy(out=dst, in_=src), reads=[kb.ptr.b], writes=[hT.b])


def proj_fm(kb, wv, wb, col, ncol, hT, tok0, ntok, pp):
    def f(e):
        last = None
        for k in range(KC):
            last = e.matmul(pp.t[0:ncol, 0:ntok], lhsT=wv[:, k, col:col + ncol], rhs=hT.t[:, k, tok0:tok0 + ntok],
                            start=(k == 0), stop=(k == KC - 1))
        return last
    kb.P.op("pe", f, reads=[wb, hT.b], writes=[pp.b])


def proj_tm(kb, wv, wb, col, ncol, hT, tok0, pp):
    def f(e):
        last = None
        for k in range(KC):
            last = e.matmul(pp.t[:, 0:ncol], lhsT=hT.t[:, k, tok0:tok0 + 128], rhs=wv[:, k, col:col + ncol],
                            start=(k == 0), stop=(k == KC - 1))
        return last
    kb.P.op("pe", f, reads=[wb, hT.b], writes=[pp.b])


def qknorm(kb, pp, ntok, gcol, gb, dst, dstb, dd=1):
    P = kb.P
    i = kb.qn_i % 2
    kb.qn_i += 1
    sq, rstd = kb.sq[i], kb.rstd[i]
    P.op("act", lambda e: e.activation(out=sq.t[:, 0:ntok], in_=pp.t[:, 0:ntok], func=AF.Square), reads=[pp.b], writes=[sq.b])
    P.op("pe", lambda e: e.matmul(kb.pss.t[:, 0:ntok], lhsT=kb.ones128, rhs=sq.t[:, 0:ntok], start=True, stop=True),
         reads=[sq.b, kb.cst.b], writes=[kb.pss.b])
    P.op("act", lambda e: e.activation(out=rstd.t[:, 0:ntok], in_=kb.pss.t[:, 0:ntok], func=AF.Ln, bias=kb.epsc.t[:, 0:1]),
         reads=[kb.pss.b, kb.epsc.b], writes=[rstd.b])
    P.op("act", lambda e: e.activation(out=rstd.t[:, 0:ntok], in_=rstd.t[:, 0:ntok], func=AF.Exp, scale=-0.5),
         reads=[rstd.b], writes=[rstd.b])
    a = pp.t[:, 0:ntok]
    b = rstd.t[:, 0:ntok]
    if dd > 1:
        a = a.rearrange("p (i r) -> p i r", r=dd)
        b = b.rearrange("p (i r) -> p i r", r=dd)
    P.op("dve", lambda e: e.scalar_tensor_tensor(out=dst, in0=a, scalar=gcol, in1=b, op0=ALU.mult, op1=ALU.mult),
         reads=[pp.b, rstd.b, gb], writes=[dstb])


A_DD = (1, 4, 16)


def build_kv():
    kb = KB()
    P = kb.P
    x_d = kb.din("x", [T, D])
    ng_d = kb.din("ng", [1, D])
    w_d = kb.din("wkv", [D, 5632])
    gk_d = kb.din("gk", [128, 3])
    c_d = kb.din("consts", [128, 384], BF16)
    kT_d = kb.dout("kT", [22 * 128, T], BF16)
    v_d = kb.dout("v", [T, 2816], BF16)
    common_setup(kb, c_d)
    hT = kb.sb([128, KC, T], BF16)
    ngb = kb.sb([128, D], F32)
    gk = kb.sb([128, 3], F32)
    P.dma("sp", ngb.t[:], ng_d[0, :].partition_broadcast(128), writes=[ngb.b])
    P.dma("sp", gk.t[:], gk_d, writes=[gk.b])
    scr = ([kb.sb([128, D], F32) for _ in range(2)], kb.sb([128, D], F32), kb.sb([128, D], BF16),
           kb.sb([128, 1], F32), kb.sb([128, 1], F32))
    norm_to_hT(kb, x_d, ngb, hT, 8, scr)
    finals = []
    kst = [kb.sb([128, T], BF16) for _ in range(2)]
    for c0 in range(0, 2816, 512):
        ncol = min(512, 2816 - c0)
        wv, wb = load_w(kb, w_d, c0, ncol)
        for hh in range(ncol // 128):
            hd = c0 // 128 + hh
            if hd < 12:
                dd, gi = A_DD[hd // 4], 0
            elif hd < 14:
                dd, gi = 1, 1
            else:
                dd, gi = 1, 2
            ks = kst[hd % 2]
            for th in range(2):
                pp = next_pp(kb)
                proj_fm(kb, wv, wb, hh * 128, 128, hT, th * 512, 512, pp)
                if dd == 1:
                    dst = ks.t[:, th * 512:(th + 1) * 512]
                else:
                    n = 512 // dd
                    dst = ks.t[:].rearrange("p (r i) -> p i r", r=dd)[:, th * n:(th + 1) * n, :]
                qknorm(kb, pp, 512, gk.t[:, gi:gi + 1], gk.b, dst, ks.b, dd)
            finals.append(P.dma("sp", kT_d[hd * 128:(hd + 1) * 128, :], ks.t[:], reads=[ks.b]))
    vst = [kb.sb([128, 8, 512], BF16) for _ in range(2)]
    ci = 0
    for c0 in range(0, 2816, 512):
        ncol = min(512, 2816 - c0)
        wv, wb = load_w(kb, w_d, 2816 + c0, ncol)
        vs = vst[ci % 2]
        ci += 1
        for t in range(8):
            pp = next_pp(kb)
            proj_tm(kb, wv, wb, 0, ncol, hT, t * 128, pp)
            if t % 2 == 0:
                P.op("act", lambda e, pp=pp, t=t, vs=vs, ncol=ncol: e.activation(out=vs.t[:, t, 0:ncol], in_=pp.t[:, 0:ncol], func=AF.Copy),
                     reads=[pp.b], writes=[vs.b])
            else:
                P.op("dve", lambda e, pp=pp, t=t, vs=vs, ncol=ncol: e.tensor_copy(out=vs.t[:, t, 0:ncol], in_=pp.t[:, 0:ncol]),
                     reads=[pp.b], writes=[vs.b])
        finals.append(P.dma("sp", v_d[:, c0:c0 + ncol].rearrange("(t p) c -> p t c", p=128), vs.t[:, :, 0:ncol], reads=[vs.b]))
    P.emit(kb.nc, finals)
    kb.st.close()
    return kb.nc


class Region:
    def __init__(self, kb, nf32):
        self.tb = kb.sb([128, nf32], F32)
        self.n = nf32
        self.off = 0

    def reset(self):
        self.off = 0

    def take(self, nelem, dt):
        w = nelem if dt == F32 else (nelem + 1) // 2
        a = self.tb.t[:, self.off:self.off + w]
        self.off += w
        assert self.off <= self.n, (self.off, self.n)
        o = TB.__new__(TB)
        o.t = a if dt == F32 else a.bitcast(BF16)
        o.b = Buf()
        return o


def build_main(lam_init, debug=False, stage=9):
    kb = KB()
    P = kb.P
    x_d = kb.din("x", [T, D]); ng_d = kb.din("ng", [1, D]); w_d = kb.din("w", [D, 15360])
    bg_d = kb.din("bg", [128, 64]); gq_d = kb.din("gq", [128, 8]); sk_d = kb.din("sk", [128, 8])
    lam_d = kb.din("lam", [128, 512]); subg_d = kb.din("subg", [128, 256])
    mem_d = kb.din("mem", [256, D]); mng_d = kb.din("mng", [1, D]); wmem_d = kb.din("wmem", [D, 1024])
    wbr_d = kb.din("wbr", [3072, D]); wout_d = kb.din("wout", [D, D])
    AW = (1152, 1536, 3072)
    akT_d = [kb.din("akT%d" % g, [512, AW[g]], BF16) for g in range(3)]
    aV_d = [kb.din("aV%d" % g, [AW[g], 512], BF16) for g in range(3)]
    bkT_d = kb.din("bkT", [256, 1152], BF16); bV_d = kb.din("bV", [1152, 256], BF16)
    ckT_d = kb.din("ckT", [1024, 4096], BF16); cV_d = kb.din("cV", [4096, 1024], BF16)
    ckTo_d = kb.din("ckTo", [1024, 1024], BF16); cVo_d = kb.din("cVo", [1024, 1024], BF16)
    abias_d = kb.din("abias", [128, 40 * 128], BF16); cmask_d = kb.din("cmask", [128, 512], BF16)
    cb_d = kb.din("cb", [128, 512]); cbo_d = kb.din("cbo", [128, 128]); hv_d = kb.din("hv", [128, 2])
    c_d = kb.din("consts", [128, 384], BF16)
    xo_d = kb.dout("xo", [T, D])
    yd_d = kb.dout("ydbg", [3072, T]) if debug else None
    md_d = kb.dout("mdbg", [D, T], BF16) if debug else None
    finals = []
    c = kb.sb([128, 384], BF16)
    P.dma("sp", c.t[:], c_d, writes=[c.b])
    kb.cst = c; kb.ident = c.t[:, 0:128]; kb.ones = c.t[:, 128:256]; kb.ones128 = c.t[:, 256:384]
    kb.p0 = kb.ps([128, 512]); kb.p1 = kb.ps([128, 512]); kb.pss = kb.ps([128, 512])
    kb.s0 = kb.ps([128, 512]); kb.s1 = kb.ps([128, 512]); kb.ud0 = kb.ps([128, 512]); kb.ud1 = kb.ps([128, 512])
    kb.ptr = kb.ps([128, 1024], BF16)
    kb.ring = [kb.sb([128, 4096], BF16) for _ in range(3)]
    kb.ring_i = 0
    kb.sq = [kb.sb([128, 512], BF16) for _ in range(2)]
    kb.rstd = [kb.sb([128, 512], F32) for _ in range(2)]
    kb.qn_i = 0; kb.pp_i = 0
    kb.epsc = kb.sb([128, 1], F32)
    P.op("dve", lambda e: e.memset(kb.epsc.t[:], EPS), writes=[kb.epsc.b])
    hT = kb.sb([128, KC, T], BF16)
    uT = kb.sb([128, 24, T], BF16)
    R = Region(kb, 11500)
    gq = kb.sb([128, 8], F32); bg = kb.sb([128, 64], F32); esk = kb.sb([128, 8], F32)
    cb = kb.sb([128, 512], F32); cbo = kb.sb([128, 128], F32); hv = kb.sb([128, 2], F32)
    cmask = kb.sb([128, 512], BF16); subg = kb.sb([128, 256], F32); nlam = kb.sb([128, 1], F32)
    qT = kb.sb([128, 3, T], BF16); zs = kb.sb([128, 2, T], BF16)
    pT = [kb.sb([128, 512], BF16) for _ in range(2)]
    tf = [kb.sb([128, 512], F32) for _ in range(3)]
    mkT = kb.sb([128, 4, 256], BF16); mv = kb.sb([128, 2, 512], BF16)
    for t_, d_ in ((gq, gq_d), (bg, bg_d), (esk, sk_d), (cb, cb_d), (cbo, cbo_d), (hv, hv_d), (cmask, cmask_d), (subg, subg_d)):
        P.dma("sp", t_.t[:], d_, writes=[t_.b])
    for col in (0, 2, 4, 6):
        P.op("dve", lambda e, col=col: e.tensor_scalar(out=gq.t[:, col:col + 1], in0=gq.t[:, col:col + 1], scalar1=128 ** -0.5,
                                                       scalar2=None, op0=ALU.mult), reads=[gq.b], writes=[gq.b])
    P.op("act", lambda e: e.activation(out=esk.t[:], in_=esk.t[:], func=AF.Exp), reads=[esk.b], writes=[esk.b])
    P.op("dve", lambda e: e.tensor_scalar(out=subg.t[:], in0=subg.t[:], scalar1=1.0 - lam_init, scalar2=None, op0=ALU.mult),
         reads=[subg.b], writes=[subg.b])
    ngb = R.take(D, F32); xt = [R.take(D, F32) for _ in range(2)]; sqf = R.take(D, F32); hb = R.take(D, BF16)
    ss = R.take(1, F32); rs = R.take(1, F32); mnT = kb.sb([128, KC, 256], BF16)
    P.dma("sp", ngb.t[:], ng_d[0, :].partition_broadcast(128), writes=[ngb.b])
    norm_to_hT(kb, x_d, ngb, hT, 8, (xt, sqf, hb, ss, rs))
    P.dma("sp", ngb.t[:], mng_d[0, :].partition_broadcast(128), writes=[ngb.b])
    norm_to_hT(kb, mem_d, ngb, mnT, 2, (xt, sqf, hb, ss, rs))
    lm = xt[0]
    P.dma("sp", lm.t[:, 0:512], lam_d, writes=[lm.b])
    P.op("dve", lambda e: e.tensor_tensor(out=lm.t[:, 512:640], in0=lm.t[:, 0:128], in1=lm.t[:, 128:256], op=ALU.mult), reads=[lm.b], writes=[lm.b])
    P.op("dve", lambda e: e.tensor_tensor(out=lm.t[:, 640:768], in0=lm.t[:, 256:384], in1=lm.t[:, 384:512], op=ALU.mult), reads=[lm.b], writes=[lm.b])
    P.op("dve", lambda e: e.reduce_sum(out=lm.t[:, 800:801], in_=lm.t[:, 512:640], axis=AX.X), reads=[lm.b], writes=[lm.b])
    P.op("dve", lambda e: e.reduce_sum(out=lm.t[:, 801:802], in_=lm.t[:, 640:768], axis=AX.X), reads=[lm.b], writes=[lm.b])
    P.op("act", lambda e: e.activation(out=lm.t[:, 800:802], in_=lm.t[:, 800:802], func=AF.Exp), reads=[lm.b], writes=[lm.b])
    P.op("dve", lambda e: e.tensor_tensor(out=nlam.t[:], in0=lm.t[:, 801:802], in1=lm.t[:, 800:801], op=ALU.subtract), reads=[lm.b], writes=[nlam.b])
    P.op("dve", lambda e: e.tensor_scalar(out=nlam.t[:], in0=nlam.t[:], scalar1=-lam_init, scalar2=None, op0=ALU.add), reads=[nlam.b], writes=[nlam.b])
    wv, wb = load_w(kb, wmem_d, 0, 256)
    wv2, wb2 = load_w(kb, wmem_d, 256, 256)
    for hh in range(4):
        pp = next_pp(kb)
        proj_fm(kb, (wv, wv2)[hh // 2], (wb, wb2)[hh // 2], (hh % 2) * 128, 128, mnT, 0, 256, pp)
        qknorm(kb, pp, 256, gq.t[:, 7:8], gq.b, mkT.t[:, hh, :], mkT.b)
    for cc in range(2):
        wv, wb = load_w(kb, wmem_d, 512 + cc * 256, 256)
        for mt in range(2):
            pp = next_pp(kb)
            proj_tm(kb, wv, wb, 0, 256, mnT, mt * 128, pp)
            P.op("act", lambda e, pp=pp, mt=mt, cc=cc: e.activation(out=mv.t[:, mt, cc * 256:(cc + 1) * 256], in_=pp.t[:, 0:256], func=AF.Copy),
                 reads=[pp.b], writes=[mv.b])
    P.barrier()
    wcol = [0]

    def wchunk():
        v = load_w(kb, w_d, wcol[0], 256)
        wcol[0] += 256
        return v

    def do_q(wv, wb, col, gcol, slot, dd=1):
        for th in range(2):
            pp = next_pp(kb)
            proj_fm(kb, wv, wb, col, 128, hT, th * 512, 512, pp)
            if dd == 1:
                dst = qT.t[:, slot, th * 512:(th + 1) * 512]
            else:
                n = 512 // dd
                dst = qT.t[:, slot, :].rearrange("p (r i) -> p i r", r=dd)[:, th * n:(th + 1) * n, :]
            qknorm(kb, pp, 512, gq.t[:, gcol:gcol + 1], gq.b, dst, qT.b, dd)

    def do_z(wv, wb, col, slot):
        for th in range(2):
            pp = next_pp(kb)
            proj_fm(kb, wv, wb, col, 128, hT, th * 512, 512, pp)
            P.op("act", lambda e, pp=pp, th=th: e.activation(out=zs.t[:, slot, th * 512:(th + 1) * 512], in_=pp.t[:, 0:512], func=AF.Silu),
                 reads=[pp.b], writes=[zs.b])

    def dbg_y(y_ap, yb, row0, c0, n):
        if debug:
            finals.append(P.dma("sp", yd_d[row0:row0 + 128, c0:c0 + n], y_ap, reads=[yb]))

    sc = [kb.s0, kb.s1]
    sci = [0]

    def tiles_pipeline(tiles):
        n = len(tiles)
        slots = []

        def qk(i):
            t = tiles[i]
            s = sc[sci[0] % 2]; p = pT[sci[0] % 2]; sci[0] += 1
            slots.append((s, p))
            nk, nq = t["nk"], t["nq"]

            def f(e):
                last = e.matmul(s.t[0:nk, 0:nq], lhsT=t["kT"], rhs=t["q"], start=True, stop=(t["bias"] is None))
                if t["bias"] is not None:
                    last = e.matmul(s.t[0:nk, 0:nq], lhsT=kb.ident[0:nk, 0:nk], rhs=t["bias"], start=False, stop=True)
                return last
            P.op("pe", f, reads=list(t["kdeps"]) + [kb.cst.b], writes=[s.b])

        def ex_pv(i):
            t = tiles[i]
            s, p = slots[i]
            nk, nq = t["nk"], t["nq"]
            P.op("act", lambda e: e.activation(out=p.t[0:nk, 0:nq], in_=s.t[0:nk, 0:nq], func=AF.Exp, bias=t["bcol"]),
                 reads=[s.b] + list(t["bdeps"]), writes=[p.b])
            P.op("pe", lambda e: t["pv"](e, p.t[0:nk, 0:nq]), reads=[p.b, kb.cst.b] + list(t["pvr"]), writes=list(t["pvw"]))
            if t.get("after"):
                t["after"]()
        for i in range(n):
            if i == 0:
                qk(0)
            if i + 1 < n:
                qk(i + 1)
            ex_pv(i)

    def fin_fm(u_ap, d_ap, ub, n, zslot, z0, chunk, c0, esk_col=None):
        a, b2 = tf[0], tf[1]
        if esk_col is not None:
            P.op("dve", lambda e: e.tensor_scalar(out=a.t[:, 0:n], in0=d_ap, scalar1=esk_col, scalar2=None, op0=ALU.add), reads=ub + [esk.b], writes=[a.b])
            P.op("dve", lambda e: e.reciprocal(out=a.t[:, 0:n], in_=a.t[:, 0:n]), reads=[a.b], writes=[a.b])
        else:
            P.op("dve", lambda e: e.reciprocal(out=a.t[:, 0:n], in_=d_ap), reads=ub, writes=[a.b])
        P.op("dve", lambda e: e.tensor_tensor(out=b2.t[:, 0:n], in0=u_ap, in1=a.t[:, 0:n], op=ALU.mult), reads=ub + [a.b], writes=[b2.b])
        dbg_y(b2.t[:, 0:n], b2.b, chunk * 128, c0, n)
        P.op("dve", lambda e: e.tensor_tensor(out=uT.t[:, chunk, c0:c0 + n], in0=b2.t[:, 0:n], in1=zs.t[:, zslot, z0:z0 + n], op=ALU.mult),
             reads=[b2.b, zs.b], writes=[uT.b])

    for hh in range(4 if stage >= 1 else 0):
        wv, wb = wchunk()
        do_q(wv, wb, 0, 6, 0)
        do_z(wv, wb, 128, 0)
        for th in range(2):
            tl = []
            for mt in range(2):
                def pv(e, p_ap, mt=mt, hh=hh):
                    e.matmul(kb.ud0.t[:, 0:512], lhsT=mv.t[:, mt, hh * 128:(hh + 1) * 128], rhs=p_ap, start=(mt == 0), stop=(mt == 1))
                    return e.matmul(kb.ud1.t[:, 0:512], lhsT=kb.ones, rhs=p_ap, start=(mt == 0), stop=(mt == 1))
                tl.append(dict(kT=mkT.t[:, hh, mt * 128:(mt + 1) * 128], q=qT.t[:, 0, th * 512:(th + 1) * 512], nk=128, nq=512, bias=None,
                               bcol=0.0, bdeps=[], pv=pv, pvr=[mv.b], pvw=[kb.ud0.b, kb.ud1.b], kdeps=[mkT.b, qT.b]))
            tiles_pipeline(tl)
            fin_fm(kb.ud0.t[:, 0:512], kb.ud1.t[:, 0:512], [kb.ud0.b, kb.ud1.b], 512, 0, th * 512, 20 + hh, th * 512)
    P.barrier()
    ab = kb.sb([128, 6, 128], BF16)
    udl = [kb.ud0, kb.ud1]
    udi = [0]

    def band_block(kT_t, kTb, k0, nk1, q_ap, nq, bt, vt0, vt1, vb, hcol, after):
        ud = udl[udi[0] % 2]; udi[0] += 1
        tl = []
        for ti in range(2):
            nk = 128 if ti == 0 else nk1
            vv = vt0 if ti == 0 else vt1

            def pv(e, p_ap, ti=ti, nk=nk, vv=vv):
                e.matmul(ud.t[:, 0:nq], lhsT=vv, rhs=p_ap, start=(ti == 0), stop=(ti == 1), skip_group_check=True)
                return e.matmul(ud.t[:, 128:128 + nq], lhsT=kb.ones[0:nk, :], rhs=p_ap, start=False, stop=(ti == 1), skip_group_check=True)
            tl.append(dict(kT=kT_t[:, k0 + ti * 128:k0 + ti * 128 + nk], q=q_ap, nk=nk, nq=nq, bias=bt[ti][0:nk, 0:nq],
                           bcol=(hcol if (ti == 0 and hcol is not None) else 0.0), bdeps=[hv.b, ab.b], pv=pv, pvr=[vb], pvw=[ud.b],
                           kdeps=[kTb, qT.b, ab.b], after=(lambda: after(ud)) if ti == 1 else None))
        tiles_pipeline(tl)

    for kv in range(2 if (stage >= 2 and stage not in (30, 31)) else 0):
        R.reset()
        bk = R.take(1152, BF16); bvv = R.take(9 * 128, BF16)
        P.dma("sp", bk.t[:], bkT_d[kv * 128:(kv + 1) * 128, :], writes=[bk.b])
        bv3 = bvv.t[:].rearrange("p (a c) -> p a c", a=9)
        P.dma("sp", bv3, bV_d[:, kv * 128:(kv + 1) * 128].rearrange("(a p) c -> p a c", p=128), writes=[bvv.b])
        for gi in range(4):
            hd = kv * 4 + gi
            wv, wb = wchunk()
            do_q(wv, wb, 0, 2, 0)
            do_z(wv, wb, 128, 0)
            P.dma("sp", ab.t[:, 0:2, :], abias_d[:, (24 + hd * 2) * 128:(26 + hd * 2) * 128].rearrange("p (a c) -> p a c", a=2), writes=[ab.b])
            for qb in range(8):
                def after(ud, qb=qb, hd=hd):
                    fin_fm(ud.t[:, 0:128], ud.t[:, 128:256], [ud.b], 128, 0, qb * 128, 4 + hd, qb * 128, esk_col=esk.t[:, hd:hd + 1])
                band_block(bk.t, bk.b, qb * 128, 128, qT.t[:, 0, qb * 128:(qb + 1) * 128], 128, (ab.t[:, 0, :], ab.t[:, 1, :]),
                           bv3[:, qb, :], bv3[:, qb + 1, :], bvv.b, hv.t[:, 0:1] if qb == 0 else None, after)
        P.barrier()
    for h in range(4 if stage >= 3 else 0):
        R.reset()
        ak = [R.take(AW[g] + (64 if g == 2 else 0), BF16) for g in range(3)]
        av0 = R.take(9 * 128, BF16); av1 = R.take(12 * 128, BF16); av2h = R.take(16 * 128, BF16); av2o = R.take(16 * 128, BF16)
        Ua = R.take(T, F32); Sa = R.take(T, F32)
        for g in range(3):
            if g >= 1 and 'k' in _ASK:
                continue
            if g == 2:
                P.op("dve", lambda e: e.memset(ak[2].t[:, 3072:3136], 0.0), writes=[ak[2].b])
            P.dma("sp", ak[g].t[:, 0:AW[g]], akT_d[g][h * 128:(h + 1) * 128, :], writes=[ak[g].b])
        v0 = av0.t[:].rearrange("p (a c) -> p a c", a=9)
        v1 = av1.t[:].rearrange("p (a c) -> p a c", a=12)
        v2h = av2h.t[:].rearrange("p (a c) -> p a c", a=16)
        v2o = av2o.t[:].rearrange("p (a c) -> p a c", a=16)
        P.dma("sp", v0, aV_d[0][:, h * 128:(h + 1) * 128].rearrange("(a p) c -> p a c", p=128), writes=[av0.b])
        if 'k' not in _ASK:
            P.dma("sp", v1, aV_d[1][:, h * 128:(h + 1) * 128].rearrange("(a p) c -> p a c", p=128), writes=[av1.b])
        a2 = aV_d[2][:, h * 128:(h + 1) * 128].rearrange("(r w) c -> w r c", w=192)
        if 'v' not in _ASK:
            P.dma("sp", v2h[:, 0:8, :], a2[0:128, 0:8, :], writes=[av2h.b])
            P.dma("sp", v2h[:, 8:16, :], a2[0:128, 8:16, :], writes=[av2h.b])
            P.op("dve", lambda e: e.memset(av2o.t[:], 0.0), writes=[av2o.b])
            P.dma("sp", v2o[0:64, :, :], a2[128:192, :, :], writes=[av2o.b])
        wv, wb = wchunk()
        do_q(wv, wb, 0, 0, 0)
        do_q(wv, wb, 128, 0, 1, 1 if 'q' in _ASK else 4)
        wv, wb = wchunk()
        do_q(wv, wb, 0, 0, 2, 1 if 'q' in _ASK else 16)
        do_z(wv, wb, 128, 0)
        for g in range(3):
            P.dma("sp", ab.t[:, 2 * g:2 * g + 2, :], abias_d[:, ((g * 4 + h) * 2) * 128:((g * 4 + h) * 2 + 2) * 128].rearrange("p (a c) -> p a c", a=2),
                  writes=[ab.b])

        def evac(ud, nq, dst_u, dst_s, first):
            if first:
                P.op("dve", lambda e: e.tensor_copy(out=dst_u, in_=ud.t[:, 0:nq]), reads=[ud.b], writes=[Ua.b])
                P.op("dve", lambda e: e.tensor_copy(out=dst_s, in_=ud.t[:, 128:128 + nq]), reads=[ud.b], writes=[Sa.b])
            else:
                P.op("dve", lambda e: e.tensor_tensor(out=dst_u, in0=dst_u, in1=ud.t[:, 0:nq], op=ALU.add), reads=[ud.b, Ua.b], writes=[Ua.b])
                P.op("dve", lambda e: e.tensor_tensor(out=dst_s, in0=dst_s, in1=ud.t[:, 128:128 + nq], op=ALU.add), reads=[ud.b, Sa.b], writes=[Sa.b])
        for qb in range(0 if 'g' in _ASK else 8):
            band_block(ak[0].t, ak[0].b, qb * 128, 128, qT.t[:, 0, qb * 128:(qb + 1) * 128], 128, (ab.t[:, 0, :], ab.t[:, 1, :]),
                       v0[:, qb, :], v0[:, qb + 1, :], av0.b, hv.t[:, 0:1] if qb == 0 else None,
                       lambda ud, qb=qb: evac(ud, 128, Ua.t[:, qb * 128:(qb + 1) * 128], Sa.t[:, qb * 128:(qb + 1) * 128], True))
        for r in range(4 if stage != 31 else 0):
            for sbk in range(2):
                st0 = 4 * sbk * 128 + r
                band_block(ak[1].t, ak[1].b, r * 384 + sbk * 128, 128, qT.t[:, 1, r * 256 + sbk * 128:r * 256 + sbk * 128 + 128], 128,
                           (ab.t[:, 2, :], ab.t[:, 3, :]), v1[:, r * 3 + sbk, :], v1[:, r * 3 + sbk + 1, :], av1.b,
                           hv.t[:, 0:1] if sbk == 0 else None,
                           lambda ud, st0=st0: evac(ud, 128, Ua.t[:, st0:st0 + 509:4], Sa.t[:, st0:st0 + 509:4], False))
        for r in range(16 if stage != 31 else 0):
            band_block(ak[2].t, ak[2].b, r * 192, 128, qT.t[:, 2, r * 64:(r + 1) * 64], 64, (ab.t[:, 4, :], ab.t[:, 5, :]),
                       v2h[:, r, :], v2o[:, r, :], av2h.b, hv.t[:, 1:2],
                       lambda ud, r=r: evac(ud, 64, Ua.t[:, r:1024:16], Sa.t[:, r:1024:16], False))
        for th in range(0 if 'f' in _ASK else 2):
            fin_fm(Ua.t[:, th * 512:(th + 1) * 512], Sa.t[:, th * 512:(th + 1) * 512], [Ua.b, Sa.b, av2o.b], 512, 0, th * 512, h, th * 512)
        P.barrier()
    for h in range(4 if (stage >= 4 and stage not in (30, 31)) else 0):
        R.reset()
        ck = R.take(2 * 4096, BF16); cva = R.take(32 * 257, BF16); cko = R.take(2 * 1024, BF16); cvo = R.take(8 * 257, BF16)
        o0s = R.take(2 * 257, F32); yf = R.take(256, F32); ysq = R.take(256, F32); yn = R.take(256, BF16); sm = R.take(8, F32)
        ck3 = ck.t[:].rearrange("p (a c) -> p a c", a=2); cko3 = cko.t[:].rearrange("p (a c) -> p a c", a=2)
        cva3 = cva.t[:].rearrange("p (a c) -> p a c", a=32); cvo3 = cvo.t[:].rearrange("p (a c) -> p a c", a=8)
        o03 = o0s.t[:].rearrange("p (a c) -> p a c", a=2)
        for c_ in range(2):
            P.dma("sp", ck3[:, c_, :], ckT_d[(h * 2 + c_) * 128:(h * 2 + c_ + 1) * 128, :], writes=[ck.b])
            P.dma("sp", cko3[:, c_, :], ckTo_d[(h * 2 + c_) * 128:(h * 2 + c_ + 1) * 128, :], writes=[cko.b])
        for q4 in range(4):
            P.dma("sp", cva3[:, q4 * 8:(q4 + 1) * 8, 0:256],
                  cV_d[q4 * 1024:(q4 + 1) * 1024, h * 256:(h + 1) * 256].rearrange("(a p) c -> p a c", p=128), writes=[cva.b])
        P.dma("sp", cvo3[:, :, 0:256], cVo_d[:, h * 256:(h + 1) * 256].rearrange("(a p) c -> p a c", p=128), writes=[cvo.b])
        P.op("dve", lambda e: e.memset(cva3[:, :, 256:257], 1.0), writes=[cva.b])
        P.op("dve", lambda e: e.memset(cvo3[:, :, 256:257], 1.0), writes=[cvo.b])
        wv, wb = wchunk()
        do_q(wv, wb, 0, 4, 0)
        do_q(wv, wb, 128, 4, 1)
        wv, wb = wchunk()
        do_z(wv, wb, 0, 0)
        do_z(wv, wb, 128, 1)
        for g in range(4):
            for c_ in range(2):
                tl = []
                nt = 32 + 2 * g + 2
                for ti in range(nt):
                    own = ti >= 32
                    lk = ti - 32

                    def pv(e, p_ap, ti=ti, own=own, lk=lk, nt=nt):
                        vv = cvo3[:, lk, :] if own else cva3[:, ti, :]
                        e.matmul(kb.ud0.t[:, 0:257], lhsT=p_ap[:, 0:128], rhs=vv, start=(ti == 0), stop=(ti == nt - 1))
                        return e.matmul(kb.ud1.t[:, 0:257], lhsT=p_ap[:, 128:256], rhs=vv, start=(ti == 0), stop=(ti == nt - 1))
                    bias = None
                    if own and lk == 2 * g:
                        bias = cmask.t[:, 0:256]
                    elif own and lk == 2 * g + 1:
                        bias = cmask.t[:, 256:512]
                    kT_ap = cko3[:, c_, lk * 128:(lk + 1) * 128] if own else ck3[:, c_, ti * 128:(ti + 1) * 128]
                    bcol = cbo.t[:, (h * 4 + g) * 8 + lk:(h * 4 + g) * 8 + lk + 1] if own else cb.t[:, (h * 4 + g) * 32 + ti:(h * 4 + g) * 32 + ti + 1]
                    tl.append(dict(kT=kT_ap, q=qT.t[:, c_, g * 256:(g + 1) * 256], nk=128, nq=256, bias=bias, bcol=bcol, bdeps=[cb.b, cbo.b, cmask.b],
                                   pv=pv, pvr=[cva.b, cvo.b], pvw=[kb.ud0.b, kb.ud1.b], kdeps=[ck.b, cko.b, qT.b, cmask.b]))
                tiles_pipeline(tl)
                if c_ == 0:
                    P.op("act", lambda e: e.activation(out=o03[:, 0, :], in_=kb.ud0.t[:, 0:257], func=AF.Copy), reads=[kb.ud0.b], writes=[o0s.b])
                    P.op("act", lambda e: e.activation(out=o03[:, 1, :], in_=kb.ud1.t[:, 0:257], func=AF.Copy), reads=[kb.ud1.b], writes=[o0s.b])
            for j in range(2):
                ud = udl[j]
                qblk = 2 * g + j
                P.op("dve", lambda e, j=j: e.reciprocal(out=sm.t[:, 0:1], in_=o03[:, j, 256:257]), reads=[o0s.b], writes=[sm.b])
                P.op("dve", lambda e, ud=ud: e.reciprocal(out=sm.t[:, 1:2], in_=ud.t[:, 256:257]), reads=[ud.b], writes=[sm.b])
                P.op("dve", lambda e: e.tensor_tensor(out=sm.t[:, 1:2], in0=sm.t[:, 1:2], in1=nlam.t[:, 0:1], op=ALU.mult), reads=[sm.b, nlam.b], writes=[sm.b])
                P.op("dve", lambda e, ud=ud: e.tensor_scalar(out=ysq.t[:], in0=ud.t[:, 0:256], scalar1=sm.t[:, 1:2], scalar2=None, op0=ALU.mult),
                     reads=[ud.b, sm.b], writes=[ysq.b])
                P.op("dve", lambda e, j=j: e.scalar_tensor_tensor(out=yf.t[:], in0=o03[:, j, 0:256], scalar=sm.t[:, 0:1], in1=ysq.t[:],
                                                                  op0=ALU.mult, op1=ALU.add), reads=[o0s.b, sm.b, ysq.b], writes=[yf.b])
                P.op("act", lambda e: e.activation(out=ysq.t[:], in_=yf.t[:], func=AF.Square), reads=[yf.b], writes=[ysq.b])
                P.op("dve", lambda e: e.reduce_sum(out=sm.t[:, 2:3], in_=ysq.t[:], axis=AX.X), reads=[ysq.b], writes=[sm.b])
                P.op("act", lambda e: e.activation(out=sm.t[:, 3:4], in_=sm.t[:, 2:3], func=AF.Ln, bias=kb.epsc.t[:, 0:1], scale=1.0 / 256),
                     reads=[sm.b, kb.epsc.b], writes=[sm.b])
                P.op("act", lambda e: e.activation(out=sm.t[:, 3:4], in_=sm.t[:, 3:4], func=AF.Exp, scale=-0.5), reads=[sm.b], writes=[sm.b])
                P.op("dve", lambda e: e.scalar_tensor_tensor(out=yn.t[:], in0=yf.t[:], scalar=sm.t[:, 3:4], in1=subg.t[:], op0=ALU.mult, op1=ALU.mult),
                     reads=[yf.b, sm.b, subg.b], writes=[yn.b])

                def tr(e):
                    e.transpose(out=kb.ptr.t[:, 0:128], in_=yn.t[:, 0:128], identity=kb.ident)
                    return e.transpose(out=kb.ptr.t[:, 128:256], in_=yn.t[:, 128:256], identity=kb.ident)
                P.op("pe", tr, reads=[yn.b, kb.cst.b], writes=[kb.ptr.b])
                for e2 in range(2):
                    if debug:
                        P.op("dve", lambda e, e2=e2: e.tensor_copy(out=tf[2].t[:, 0:128], in_=kb.ptr.t[:, e2 * 128:(e2 + 1) * 128]), reads=[kb.ptr.b], writes=[tf[2].b])
                        dbg_y(tf[2].t[:, 0:128], tf[2].b, (12 + 2 * h + e2) * 128, qblk * 128, 128)
                    P.op("dve", lambda e, e2=e2, qblk=qblk, h=h: e.tensor_tensor(out=uT.t[:, 12 + 2 * h + e2, qblk * 128:(qblk + 1) * 128],
                                                                            in0=kb.ptr.t[:, e2 * 128:(e2 + 1) * 128],
                                                                            in1=zs.t[:, e2, qblk * 128:(qblk + 1) * 128], op=ALU.mult),
                         reads=[kb.ptr.b, zs.b], writes=[uT.b])
        P.barrier()
    R.reset()
    mT = R.take(KC * T, BF16)
    mT3 = TB.__new__(TB); mT3.t = mT.t[:].rearrange("p (k t) -> p k t", k=KC); mT3.b = mT.b
    gsb = [kb.sb([128, 512], BF16) for _ in range(4)]
    bch = [(0, 4), (4, 12), (12, 20), (20, 24)]
    pb = [kb.s0, kb.s1, kb.ud0, kb.ud1]
    wcol[0] = 15360 - 8192
    for oc in range(16 if (stage >= 5 and stage not in (30, 31)) else 0):
        wa = wchunk(); wb_ = wchunk()
        r = next_ring(kb)
        wbr = r.t[:, 0:24 * 128].rearrange("p (k c) -> p k c", k=24)
        for q2 in range(2):
            P.dma("pool", wbr[:, q2 * 12:(q2 + 1) * 12, :],
                  wbr_d[q2 * 1536:(q2 + 1) * 1536, oc * 128:(oc + 1) * 128].rearrange("(k p) c -> p k c", p=128), writes=[r.b])
        for th in range(2):
            for b_ in range(4):
                pp = next_pp(kb)
                wv, wb = (wa, wb_)[b_ // 2]
                proj_fm(kb, wv, wb, (b_ % 2) * 128, 128, hT, th * 512, 512, pp)
                P.op("act", lambda e, pp=pp, b_=b_, oc=oc: e.activation(out=gsb[b_].t[:], in_=pp.t[:, 0:512], func=AF.Sigmoid,
                                                                       bias=bg.t[:, oc * 4 + b_:oc * 4 + b_ + 1]),
                     reads=[pp.b, bg.b], writes=[gsb[b_].b])
            for b_ in range(4):
                def f(e, b_=b_, th=th, wbr=wbr):
                    last = None
                    lo, hi = bch[b_]
                    for k in range(lo, hi):
                        last = e.matmul(pb[b_].t[:, 0:512], lhsT=wbr[:, k, :], rhs=uT.t[:, k, th * 512:(th + 1) * 512], start=(k == lo), stop=(k == hi - 1))
                    return last
                P.op("pe", f, reads=[r.b, uT.b], writes=[pb[b_].b])
            P.op("dve", lambda e: e.tensor_tensor(out=tf[0].t[:], in0=pb[0].t[:, 0:512], in1=gsb[0].t[:], op=ALU.mult), reads=[pb[0].b, gsb[0].b], writes=[tf[0].b])
            for b_ in range(1, 4):
                P.op("dve", lambda e, b_=b_: e.tensor_tensor(out=tf[1].t[:], in0=pb[b_].t[:, 0:512], in1=gsb[b_].t[:], op=ALU.mult),
                     reads=[pb[b_].b, gsb[b_].b], writes=[tf[1].b])
                if b_ < 3:
                    P.op("dve", lambda e: e.tensor_tensor(out=tf[0].t[:], in0=tf[0].t[:], in1=tf[1].t[:], op=ALU.add), reads=[tf[0].b, tf[1].b], writes=[tf[0].b])
                else:
                    P.op("dve", lambda e, oc=oc, th=th: e.tensor_tensor(out=mT3.t[:, oc, th * 512:(th + 1) * 512], in0=tf[0].t[:], in1=tf[1].t[:], op=ALU.add),
                         reads=[tf[0].b, tf[1].b], writes=[mT.b])
    if debug:
        finals.append(P.dma("sp", md_d.rearrange("(k p) t -> p k t", p=128), mT3.t[:], reads=[mT.b]))
    xs = [kb.sb([128, 256], F32) for _ in range(2)]
    ot = [kb.sb([128, 256], F32) for _ in range(2)]
    i_ = 0
    for cc in range(8):
        wv, wb = load_w(kb, wout_d, cc * 256, 256)
        for t in range(8):
            pp = next_pp(kb)
            proj_tm(kb, wv, wb, 0, 256, mT3, t * 128, pp)
            x1, o1 = xs[i_ % 2], ot[i_ % 2]
            i_ += 1
            P.dma("sp", x1.t[:], x_d[t * 128:(t + 1) * 128, cc * 256:(cc + 1) * 256], writes=[x1.b])
            P.op("dve", lambda e, pp=pp, x1=x1, o1=o1: e.tensor_tensor(out=o1.t[:], in0=pp.t[:, 0:256], in1=x1.t[:], op=ALU.add),
                 reads=[pp.b, x1.b], writes=[o1.b])
            finals.append(P.dma("sp", xo_d[t * 128:(t + 1) * 128, cc * 256:(cc + 1) * 256], o1.t[:], reads=[o1.b]))
    P.emit(kb.nc, finals)
    kb.st.close()
    return kb.nc


def build_fused(lam_inits, debug=False, stage=9):
    kb = KB()
    P = kb.P
    P.ring["cc"] = 12
    P.ring["act"] = 12
    nl = len(lam_inits)
    x_d = kb.din("x", [T, D]); mem_d = kb.din("mem", [256, D])
    ng_all = kb.din("ng", [nl, D]); mng_all = kb.din("mng", [nl, D])
    wkv_all = kb.din("wkv", [nl * D, 5632]); w_all = kb.din("w", [nl * D, 15360])
    wmem_all = kb.din("wmem", [nl * D, 1024]); wbr_all = kb.din("wbr", [nl * 3072, D]); wout_all = kb.din("wout", [nl * D, D])
    bg_all = kb.din("bg", [nl * 128, 64]); gq_all = kb.din("gq", [nl * 128, 8]); sk_all = kb.din("sk", [nl * 128, 8])
    gk_all = kb.din("gk", [nl * 128, 3])
    lam_all = kb.din("lam", [nl * 128, 512]); subg_all = kb.din("subg", [nl * 128, 256])
    abias_d = kb.din("abias", [128, 40 * 128], BF16); cmask_d = kb.din("cmask", [128, 512], BF16)
    cb_d = kb.din("cb", [128, 512]); cbo_d = kb.din("cbo", [128, 128]); hv_d = kb.din("hv", [128, 2]); sel_d = kb.din("sel", [128, 12])
    c_d = kb.din("consts", [128, 384], BF16)
    xo_d = kb.dout("xo", [T, D])
    yd_d = kb.dout("ydbg", [3072, T]) if debug else None
    md_d = kb.dout("mdbg", [D, T], BF16) if debug else None
    nc = kb.nc
    xb = [nc.dram_tensor("xb%d" % i, [T, D], F32).ap() for i in range(2)]
    Bxb = [Buf(), Buf()]
    KP = [512] * 5 + [256]
    kTl_p = [nc.dram_tensor("kTl%d" % p, [KP[p], T], BF16) for p in range(6)]
    kTa_p = [nc.dram_tensor("kTa%d" % p, [4 * KP[p], T], BF16) for p in range(6)]
    vl_p = [nc.dram_tensor("vl%d" % p, [T, KP[p]], BF16) for p in range(6)]
    va_p = [nc.dram_tensor("va%d" % p, [4 * T, KP[p]], BF16) for p in range(6)]
    BkTl = [Buf() for _ in range(6)]; Bvl = [Buf() for _ in range(6)]; BkTa = [Buf() for _ in range(6)]; Bva = [Buf() for _ in range(6)]

    def kTl_rows(q0, n):
        p = q0 // 512
        return kTl_p[p].ap()[q0 - p * 512:q0 - p * 512 + n, :]

    def kTa_rows(r, q0, n):
        p = q0 // 512
        o = r * KP[p] + q0 - p * 512
        return kTa_p[p].ap()[o:o + n, :]

    def vl_cols(c0, n):
        p = c0 // 512
        return vl_p[p].ap()[:, c0 - p * 512:c0 - p * 512 + n]

    def va_cols(c0, n):
        p = c0 // 512
        return va_p[p].ap()[:, c0 - p * 512:c0 - p * 512 + n]
    RG = [[0, 1, 2, 3], [4, 5, 6, 7]]
    AW = (1152, 1536, 3072)
    gates_d = nc.dram_tensor("gates_scr", [8192, T], BF16).ap()
    Bgates = Buf()
    akT_d = [nc.dram_tensor("akT_rel%d" % g, [512, AW[g]], BF16).ap() for g in range(3)]
    aV_d = [nc.dram_tensor("aV_rel%d" % g, [AW[g], 512], BF16).ap() for g in range(3)]
    bkT_d = nc.dram_tensor("bkT_rel", [256, 1152], BF16).ap(); bV_d = nc.dram_tensor("bV_rel", [1152, 256], BF16).ap()
    finals = []
    c = kb.sb([128, 384], BF16)
    P.dma("sp", c.t[:], c_d, writes=[c.b])
    kb.cst = c; kb.ident = c.t[:, 0:128]; kb.ones = c.t[:, 128:256]; kb.ones128 = c.t[:, 256:384]
    kb.p0 = kb.ps([128, 512]); kb.p1 = kb.ps([128, 512]); kb.pss = kb.ps([128, 512])
    kb.s0 = kb.ps([128, 512]); kb.s1 = kb.ps([128, 512]); kb.ud0 = kb.ps([128, 512]); kb.ud1 = kb.ps([128, 512])
    kb.ptr = kb.ps([128, 1024], BF16)
    kb.ring = [kb.sb([128, 4096], BF16) for _ in range(3)]
    kb.ring_i = 0
    kb.sq = [kb.sb([128, 512], BF16) for _ in range(2)]
    kb.rstd = [kb.sb([128, 512], F32) for _ in range(2)]
    kb.qn_i = 0; kb.pp_i = 0
    kb.pp_list = [kb.p0, kb.p1, kb.s0, kb.s1]
    pend = [None]

    def push(projf, epif):
        pp = next_pp(kb)
        projf(pp)
        epif(pp)

    def drain():
        if pend[0] is not None:
            pend[0][0](pend[0][1])
            pend[0] = None
    kb.epsc = kb.sb([128, 1], F32)
    P.op("dve", lambda e: e.memset(kb.epsc.t[:], EPS), writes=[kb.epsc.b])
    hT = kb.sb([128, KC, T], BF16)
    uT = kb.sb([128, 24, T], BF16)
    R = Region(kb, 11500)
    gq = kb.sb([128, 8], F32); bg = kb.sb([128, 64], F32); esk = kb.sb([128, 8], F32)
    cb = kb.sb([128, 512], F32); cbo = kb.sb([128, 128], F32); hv = kb.sb([128, 2], F32)
    cmask = kb.sb([128, 512], BF16); subg = kb.sb([128, 256], F32); nlam = kb.sb([128, 1], F32)
    qT = kb.sb([128, 3, T], BF16); zs = kb.sb([128, 2, T], BF16)
    pT = [kb.sb([128, 512], BF16) for _ in range(4)]
    tf = [kb.sb([128, 512], F32) for _ in range(3)]
    mkT = kb.sb([128, 4, 256], BF16); mv = kb.sb([128, 2, 512], BF16)
    gk = kb.sb([128, 3], F32); sel = kb.sb([128, 12], F32)
    mnT = kb.sb([128, KC, 256], BF16)
    ab = kb.sb([128, 6, 128], BF16)
    gsb = [kb.sb([128, 512], BF16) for _ in range(4)]
    xs = [kb.sb([128, 256], F32) for _ in range(2)]
    ot = [kb.sb([128, 256], F32) for _ in range(2)]
    stg = kb.sb([128, 2048], BF16)
    for t_, d_ in ((cb, cb_d), (cbo, cbo_d), (hv, hv_d), (cmask, cmask_d), (sel, sel_d)):
        P.dma("sp", t_.t[:], d_, writes=[t_.b])
    for l in range(nl):
        if l > 0:
            P.barrier()
            P.epoch = l
        lam_init = lam_inits[l]
        x_src = x_d if l == 0 else xb[(l - 1) % 2]
        xdep = [] if l == 0 else [Bxb[(l - 1) % 2]]
        x_dst = xo_d if l == nl - 1 else xb[l % 2]
        xdst_b = [] if l == nl - 1 else [Bxb[l % 2]]
        w_d = w_all[l * D:(l + 1) * D, :]; wkv_d = wkv_all[l * D:(l + 1) * D, :]
        wmem_d = wmem_all[l * D:(l + 1) * D, :]; wbr_d = wbr_all[l * 3072:(l + 1) * 3072, :]; wout_d = wout_all[l * D:(l + 1) * D, :]
        R.reset()
        for t_, d_ in ((gq, gq_all), (bg, bg_all), (esk, sk_all), (subg, subg_all), (gk, gk_all)):
            P.dma("sp", t_.t[:], d_[l * 128:(l + 1) * 128, :], writes=[t_.b])
        for col in (0, 2, 4, 6):
            P.op("dve", lambda e, col=col: e.tensor_scalar(out=gq.t[:, col:col + 1], in0=gq.t[:, col:col + 1], scalar1=128 ** -0.5,
                                                           scalar2=None, op0=ALU.mult), reads=[gq.b], writes=[gq.b])
        P.op("act", lambda e: e.activation(out=esk.t[:], in_=esk.t[:], func=AF.Exp), reads=[esk.b], writes=[esk.b])
        P.op("dve", lambda e, li=lam_init: e.tensor_scalar(out=subg.t[:], in0=subg.t[:], scalar1=1.0 - li, scalar2=None, op0=ALU.mult),
             reads=[subg.b], writes=[subg.b])
        ngb = R.take(D, F32); xt = [R.take(D, F32) for _ in range(2)]; sqf = R.take(D, F32); hb = R.take(D, BF16)
        ss = R.take(1, F32); rs = R.take(1, F32)
        P.dma("sp", ngb.t[:], ng_all[l, :].partition_broadcast(128), writes=[ngb.b])
        norm_to_hT(kb, x_src, ngb, hT, 8, (xt, sqf, hb, ss, rs), xdep)
        P.dma("sp", ngb.t[:], mng_all[l, :].partition_broadcast(128), writes=[ngb.b])
        norm_to_hT(kb, mem_d, ngb, mnT, 2, (xt, sqf, hb, ss, rs))
        lm = xt[0]
        P.dma("sp", lm.t[:, 0:512], lam_all[l * 128:(l + 1) * 128, :], writes=[lm.b])
        P.op("dve", lambda e: e.tensor_tensor(out=lm.t[:, 512:640], in0=lm.t[:, 0:128], in1=lm.t[:, 128:256], op=ALU.mult), reads=[lm.b], writes=[lm.b])
        P.op("dve", lambda e: e.tensor_tensor(out=lm.t[:, 640:768], in0=lm.t[:, 256:384], in1=lm.t[:, 384:512], op=ALU.mult), reads=[lm.b], writes=[lm.b])
        P.op("dve", lambda e: e.reduce_sum(out=lm.t[:, 800:801], in_=lm.t[:, 512:640], axis=AX.X), reads=[lm.b], writes=[lm.b])
        P.op("dve", lambda e: e.reduce_sum(out=lm.t[:, 801:802], in_=lm.t[:, 640:768], axis=AX.X), reads=[lm.b], writes=[lm.b])
        P.op("act", lambda e: e.activation(out=lm.t[:, 800:802], in_=lm.t[:, 800:802], func=AF.Exp), reads=[lm.b], writes=[lm.b])
        P.op("dve", lambda e: e.tensor_tensor(out=nlam.t[:], in0=lm.t[:, 801:802], in1=lm.t[:, 800:801], op=ALU.subtract), reads=[lm.b], writes=[nlam.b])
        P.op("dve", lambda e, li=lam_init: e.tensor_scalar(out=nlam.t[:], in0=nlam.t[:], scalar1=-li, scalar2=None, op0=ALU.add), reads=[nlam.b], writes=[nlam.b])
        P.barrier()
        R.reset()
        kst = [R.take(T, BF16) for _ in range(2)]
        vst = [R.take(8 * 256, BF16) for _ in range(2)]
        for c0 in range(0, 2816, 256):
            wv, wb = load_w(kb, wkv_d, c0, 256)
            for hh in range(2):
                hd = c0 // 128 + hh
                if hd < 12:
                    dd, gi = A_DD[hd // 4], 0
                elif hd < 14:
                    dd, gi = 1, 1
                else:
                    dd, gi = 1, 2
                ks = kst[hd % 2]
                for th in range(2):
                    if dd == 1:
                        dst = ks.t[:, th * 512:(th + 1) * 512]
                    else:
                        n = 512 // dd
                        dst = ks.t[:].rearrange("p (r i) -> p i r", r=dd)[:, th * n:(th + 1) * n, :]

                    def epi(pp, gi=gi, dst=dst, ks=ks, dd=dd, th=th, hd=hd):
                        qknorm(kb, pp, 512, gk.t[:, gi:gi + 1], gk.b, dst, ks.b, dd)
                        if th == 1:
                            P.dma("sp", kTl_rows(hd * 128, 128), ks.t[:], reads=[ks.b], writes=[BkTl[hd // 4]])
                    push(lambda pp, wv=wv, wb=wb, hh=hh, th=th: proj_fm(kb, wv, wb, hh * 128, 128, hT, th * 512, 512, pp), epi)
        drain()
        for p_ in range(6):
            P.op("pool", lambda e, p_=p_: e.collective_compute("AllGather", ALU.bypass, replica_groups=RG, ins=[kTl_p[p_].ap().opt()],
                                                               outs=[kTa_p[p_].ap().opt()]),
                 reads=[BkTl[p_]], writes=[BkTa[p_]], dma=True, inc=1, ring="cc")
        ci = 0
        for c0 in range(0, 2816, 256):
            wv, wb = load_w(kb, wkv_d, 2816 + c0, 256)
            vs = vst[ci % 2]
            vs3 = vs.t[:].rearrange("p (a c) -> p a c", a=8)
            ci += 1
            for t in range(8):
                pp = next_pp(kb)
                proj_tm(kb, wv, wb, 0, 256, hT, t * 128, pp)
                if t % 2 == 0:
                    P.op("act", lambda e, pp=pp, t=t, vs3=vs3: e.activation(out=vs3[:, t, :], in_=pp.t[:, 0:256], func=AF.Copy),
                         reads=[pp.b], writes=[vs.b])
                else:
                    P.op("dve", lambda e, pp=pp, t=t, vs3=vs3: e.tensor_copy(out=vs3[:, t, :], in_=pp.t[:, 0:256]),
                         reads=[pp.b], writes=[vs.b])
            P.dma("sp", vl_cols(c0, 256).rearrange("(t p) c -> p t c", p=128), vs3, reads=[vs.b], writes=[Bvl[c0 // 512]])
        for p_ in range(6):
            P.op("pool", lambda e, p_=p_: e.collective_compute("AllGather", ALU.bypass, replica_groups=RG, ins=[vl_p[p_].ap().opt()],
                                                               outs=[va_p[p_].ap().opt()]),
                 reads=[Bvl[p_]], writes=[Bva[p_]], dma=True, inc=1, ring="cc")
        stq = [R.take(2048, BF16) for _ in range(4)]
        sti = [0]
        accs = [R.take(4096, BF16) for _ in range(2)]
        acci = [0]

        def next_acc():
            acci[0] += 1
            return accs[acci[0] % 2]

        def select(dst_ap, cands, scols, shape, acc, dep):
            for r in range(4):
                st = stq[sti[0] % 4]; sti[0] += 1
                for fn, src in cands[r]:
                    P.dma("sp", fn(st.t), src, reads=list(dep), writes=[st.b])
                for (accv, stv, scol) in shape(st.t):
                    sc_ = sel.t[:, scol + r:scol + r + 1]
                    if r == 0:
                        P.op("dve", lambda e, accv=accv, stv=stv, sc_=sc_: e.tensor_scalar(out=accv, in0=stv, scalar1=sc_, scalar2=None, op0=ALU.mult),
                             reads=[st.b, sel.b], writes=[acc.b])
                    else:
                        P.op("dve", lambda e, accv=accv, stv=stv, sc_=sc_: e.scalar_tensor_tensor(out=accv, in0=stv, scalar=sc_, in1=accv, op0=ALU.mult, op1=ALU.add),
                             reads=[st.b, sel.b, acc.b], writes=[acc.b])
            for dd_, ss_ in (dst_ap if isinstance(dst_ap, list) else [dst_ap]):
                P.dma("sp", dd_, ss_, reads=[acc.b])

        acc = next_acc()
        a3 = acc.t[:, 0:512].rearrange("p (h c) -> p h c", h=4)
        select((akT_d[0][:, 0:128].rearrange("(h d) c -> d h c", d=128), a3),
               [[(lambda st: st[:, 0:512].rearrange("p (h c) -> p h c", h=4), kTa_rows(r, 0, 512)[:, 896:1024].rearrange("(h d) c -> d h c", d=128))] for r in range(4)],
               None, lambda st: [(a3, st[:, 0:512].rearrange("p (h c) -> p h c", h=4), 0)], acc, BkTa)
        acc = next_acc()
        a3 = acc.t[:, 0:256].rearrange("p (h c) -> p h c", h=2)
        select((bkT_d[:, 0:128].rearrange("(h d) c -> d h c", d=128), a3),
               [[(lambda st: st[:, 0:256].rearrange("p (h c) -> p h c", h=2), kTa_rows(r, 1536, 256)[:, 896:1024].rearrange("(h d) c -> d h c", d=128))] for r in range(4)],
               None, lambda st: [(a3, st[:, 0:256].rearrange("p (h c) -> p h c", h=2), 0)], acc, BkTa)
        acc = next_acc()
        a2_ = acc.t[:, 0:512]
        select((aV_d[0][0:128, :], a2_),
               [[(lambda st: st[:, 0:512], va_cols(0, 512)[r * 1024 + 896:r * 1024 + 1024, :])] for r in range(4)],
               None, lambda st: [(a2_, st[:, 0:512], 0)], acc, Bva)
        acc = next_acc()
        a2_ = acc.t[:, 0:256]
        select((bV_d[0:128, :], a2_),
               [[(lambda st: st[:, 0:256], va_cols(1536, 256)[r * 1024 + 896:r * 1024 + 1024, :])] for r in range(4)],
               None, lambda st: [(a2_, st[:, 0:256], 0)], acc, Bva)
        acc = next_acc()
        a4 = acc.t[:, 0:2048].rearrange("p (h rr c) -> p h rr c", h=4, rr=4)
        select([(akT_d[1][h_ * 128:(h_ + 1) * 128, :].rearrange("d (rr w) -> d rr w", rr=4)[:, :, 0:128], a4[:, h_]) for h_ in range(4)],
               [[(lambda st, h_=h_: st[:, h_ * 512:(h_ + 1) * 512].rearrange("p (rr c) -> p rr c", rr=4),
                  kTa_rows(r, 512 + h_ * 128, 128).rearrange("d (rr i) -> d rr i", rr=4)[:, :, 128:256])
                 for h_ in range(4)] for r in range(4)],
               None, lambda st: [(a4, st[:].rearrange("p (h rr c) -> p h rr c", h=4, rr=4), 0)], acc, BkTa)
        acc = next_acc()
        a3 = acc.t[:, 0:2048].rearrange("p (rr c) -> p rr c", rr=4)
        select((aV_d[1].rearrange("(rr w) c -> w rr c", rr=4)[0:128, :, :], a3),
               [[(lambda st: st[:].rearrange("p (rr c) -> p rr c", rr=4),
                  va_cols(512, 512)[r * 1024 + 512:r * 1024 + 1024, :].rearrange("(i rr) c -> i rr c", rr=4))] for r in range(4)],
               None, lambda st: [(a3, st[:].rearrange("p (rr c) -> p rr c", rr=4), 0)], acc, Bva)
        for hp in range(2):
            acc = next_acc()
            a4 = acc.t[:].rearrange("p (h rr c) -> p h rr c", h=2, rr=16)
            select([(akT_d[2][(hp * 2 + hh) * 128:(hp * 2 + hh + 1) * 128, :].rearrange("d (rr w) -> d rr w", rr=16)[:, :, 0:128], a4[:, hh]) for hh in range(2)],
                   [[(lambda st, hh=hh: st[:, hh * 1024:(hh + 1) * 1024],
                      kTa_rows(r, 1024 + (hp * 2 + hh) * 128, 128)) for hh in range(2)] for r in range(4)],
                   None, lambda st: [(a4[:, :, :, 0:64], st[:].rearrange("p (h rr c) -> p h rr c", h=2, rr=16), 8),
                                     (a4[:, :, :, 64:128], st[:].rearrange("p (h rr c) -> p h rr c", h=2, rr=16), 0)], acc, BkTa)
        for q4 in range(4):
            acc = next_acc()
            a3 = acc.t[:, 0:2048].rearrange("p (rr c) -> p rr c", rr=4)
            cands = []
            for r in range(4):
                src = va_cols(1024, 512)[r * 1024:(r + 1) * 1024, :].rearrange("(i rr) c -> i rr c", rr=16)[:, q4 * 4:(q4 + 1) * 4, :]
                cands.append([(lambda st: st[0:64, :].rearrange("p (rr c) -> p rr c", rr=4), src),
                              (lambda st: st[64:128, :].rearrange("p (rr c) -> p rr c", rr=4), src)])
            select((aV_d[2].rearrange("(rr w) c -> w rr c", rr=16)[0:128, q4 * 4:(q4 + 1) * 4, :], a3), cands,
                   None, lambda st: [(a3, st[:].rearrange("p (rr c) -> p rr c", rr=4), 4)], acc, Bva)
        for g in range(3):
            dd = A_DD[g]; L = 1024 // dd
            P.dma("sp", akT_d[g].rearrange("q (rr w) -> q rr w", rr=dd)[:, :, 128:128 + L],
                  kTl_rows(g * 512, 512).rearrange("q (rr i) -> q rr i", rr=dd), reads=BkTl)
            P.dma("sp", aV_d[g].rearrange("(rr w) c -> rr w c", rr=dd)[:, 128:128 + L, :],
                  vl_cols(g * 512, 512).rearrange("(i rr) c -> rr i c", rr=dd), reads=Bvl)
        P.dma("sp", bkT_d[:, 128:1152], kTl_rows(1536, 256), reads=BkTl)
        P.dma("sp", bV_d[128:1152, :], vl_cols(1536, 256), reads=Bvl)
        gsi = 0
        for oc in range(16):
            wa = load_w(kb, w_d, 7168 + oc * 512, 256); wb_ = load_w(kb, w_d, 7168 + oc * 512 + 256, 256)
            for th in range(2):
                for b_ in range(4):
                    pp = next_pp(kb)
                    wv, wb = (wa, wb_)[b_ // 2]
                    proj_fm(kb, wv, wb, (b_ % 2) * 128, 128, hT, th * 512, 512, pp)
                    g_ = gsb[gsi % 4]; gsi += 1
                    P.op("act", lambda e, pp=pp, b_=b_, oc=oc, g_=g_: e.activation(out=g_.t[:], in_=pp.t[:, 0:512], func=AF.Sigmoid,
                                                                                  bias=bg.t[:, oc * 4 + b_:oc * 4 + b_ + 1]),
                         reads=[pp.b, bg.b], writes=[g_.b])
                    P.dma("act", gates_d[(oc * 4 + b_) * 128:(oc * 4 + b_ + 1) * 128, th * 512:(th + 1) * 512], g_.t[:], reads=[g_.b], writes=[Bgates])
        wv, wb = load_w(kb, wmem_d, 0, 256)
        wv2, wb2 = load_w(kb, wmem_d, 256, 256)
        for hh in range(4):
            pp = next_pp(kb)
            proj_fm(kb, (wv, wv2)[hh // 2], (wb, wb2)[hh // 2], (hh % 2) * 128, 128, mnT, 0, 256, pp)
            qknorm(kb, pp, 256, gq.t[:, 7:8], gq.b, mkT.t[:, hh, :], mkT.b)
        for cc in range(2):
            wv, wb = load_w(kb, wmem_d, 512 + cc * 256, 256)
            for mt in range(2):
                pp = next_pp(kb)
                proj_tm(kb, wv, wb, 0, 256, mnT, mt * 128, pp)
                P.op("act", lambda e, pp=pp, mt=mt, cc=cc: e.activation(out=mv.t[:, mt, cc * 256:(cc + 1) * 256], in_=pp.t[:, 0:256], func=AF.Copy),
                     reads=[pp.b], writes=[mv.b])
        P.barrier()
        wcol = [0]

        def wchunk():
            v = load_w(kb, w_d, wcol[0], 256)
            wcol[0] += 256
            return v

        def do_q(wv, wb, col, gcol, slot, dd=1):
            for th in range(2):
                if dd == 1:
                    dst = qT.t[:, slot, th * 512:(th + 1) * 512]
                else:
                    n = 512 // dd
                    dst = qT.t[:, slot, :].rearrange("p (r i) -> p i r", r=dd)[:, th * n:(th + 1) * n, :]
                push(lambda pp, wv=wv, wb=wb, col=col, th=th: proj_fm(kb, wv, wb, col, 128, hT, th * 512, 512, pp),
                     lambda pp, gcol=gcol, dst=dst, dd=dd: qknorm(kb, pp, 512, gq.t[:, gcol:gcol + 1], gq.b, dst, qT.b, dd))

        def do_z(wv, wb, col, slot):
            for th in range(2):
                pp = next_pp(kb)
                proj_fm(kb, wv, wb, col, 128, hT, th * 512, 512, pp)
                P.op("act", lambda e, pp=pp, th=th: e.activation(out=zs.t[:, slot, th * 512:(th + 1) * 512], in_=pp.t[:, 0:512], func=AF.Silu),
                     reads=[pp.b], writes=[zs.b])

        def dbg_y(y_ap, yb, row0, c0, n):
            if debug and l == nl - 1:
                finals.append(P.dma("sp", yd_d[row0:row0 + 128, c0:c0 + n], y_ap, reads=[yb]))

        sc = [kb.s0, kb.s1, kb.p0, kb.p1]
        sci = [0]

        def tiles_pipeline(tiles):
            drain()
            n = len(tiles)
            slots = []

            def qk(i):
                t = tiles[i]
                s = sc[sci[0] % 4]; p = pT[sci[0] % 4]; sci[0] += 1
                slots.append((s, p))
                nk, nq = t["nk"], t["nq"]

                def f(e):
                    last = e.matmul(s.t[0:nk, 0:nq], lhsT=t["kT"], rhs=t["q"], start=True, stop=(t["bias"] is None))
                    if t["bias"] is not None:
                        last = e.matmul(s.t[0:nk, 0:nq], lhsT=kb.ident[0:nk, 0:nk], rhs=t["bias"], start=False, stop=True)
                    return last
                P.op("pe", f, reads=list(t["kdeps"]) + [kb.cst.b], writes=[s.b])

            def ex_pv(i):
                t = tiles[i]
                s, p = slots[i]
                nk, nq = t["nk"], t["nq"]
                P.op("act", lambda e: e.activation(out=p.t[0:nk, 0:nq], in_=s.t[0:nk, 0:nq], func=AF.Exp, bias=t["bcol"]),
                     reads=[s.b] + list(t["bdeps"]), writes=[p.b])
                P.op("pe", lambda e: t["pv"](e, p.t[0:nk, 0:nq]), reads=[p.b, kb.cst.b] + list(t["pvr"]), writes=list(t["pvw"]))
                if t.get("after"):
                    t["after"]()
            DEPTH = 3
            for i in range(min(DEPTH, n)):
                qk(i)
            for i in range(n):
                ex_pv(i)
                if i + DEPTH < n:
                    qk(i + DEPTH)

        def fin_fm(u_ap, d_ap, ub, n, zslot, z0, chunk, c0, esk_col=None):
            a, b2 = tf[0], tf[1]
            if esk_col is not None:
                P.op("dve", lambda e: e.tensor_scalar(out=a.t[:, 0:n], in0=d_ap, scalar1=esk_col, scalar2=None, op0=ALU.add), reads=ub + [esk.b], writes=[a.b])
                P.op("dve", lambda e: e.reciprocal(out=a.t[:, 0:n], in_=a.t[:, 0:n]), reads=[a.b], writes=[a.b])
            else:
                P.op("dve", lambda e: e.reciprocal(out=a.t[:, 0:n], in_=d_ap), reads=ub, writes=[a.b])
            P.op("dve", lambda e: e.tensor_tensor(out=b2.t[:, 0:n], in0=u_ap, in1=a.t[:, 0:n], op=ALU.mult), reads=ub + [a.b], writes=[b2.b])
            dbg_y(b2.t[:, 0:n], b2.b, chunk * 128, c0, n)
            P.op("dve", lambda e: e.tensor_tensor(out=uT.t[:, chunk, c0:c0 + n], in0=b2.t[:, 0:n], in1=zs.t[:, zslot, z0:z0 + n], op=ALU.mult),
                 reads=[b2.b, zs.b], writes=[uT.b])

        for hh in range(4 if stage >= 1 else 0):
            wv, wb = wchunk()
            do_q(wv, wb, 0, 6, 0)
            do_z(wv, wb, 128, 0)
            for th in range(2):
                tl = []
                for mt in range(2):
                    def pv(e, p_ap, mt=mt, hh=hh):
                        e.matmul(kb.ud0.t[:, 0:512], lhsT=mv.t[:, mt, hh * 128:(hh + 1) * 128], rhs=p_ap, start=(mt == 0), stop=(mt == 1))
                        return e.matmul(kb.ud1.t[:, 0:512], lhsT=kb.ones, rhs=p_ap, start=(mt == 0), stop=(mt == 1))
                    tl.append(dict(kT=mkT.t[:, hh, mt * 128:(mt + 1) * 128], q=qT.t[:, 0, th * 512:(th + 1) * 512], nk=128, nq=512, bias=None,
                                   bcol=0.0, bdeps=[], pv=pv, pvr=[mv.b], pvw=[kb.ud0.b, kb.ud1.b], kdeps=[mkT.b, qT.b]))
                tiles_pipeline(tl)
                fin_fm(kb.ud0.t[:, 0:512], kb.ud1.t[:, 0:512], [kb.ud0.b, kb.ud1.b], 512, 0, th * 512, 20 + hh, th * 512)
        P.barrier()
        udl = [kb.ud0, kb.ud1]
        udi = [0]

        def band_block(kT_t, kTb, k0, nk1, q_ap, nq, bt, vt0, vt1, vb, hcol, after):
            ud = udl[udi[0] % 2]; udi[0] += 1
            tl = []
            for ti in range(2):
                nk = 128 if ti == 0 else nk1
                vv = vt0 if ti == 0 else vt1

                def pv(e, p_ap, ti=ti, nk=nk, vv=vv):
                    e.matmul(ud.t[:, 0:nq], lhsT=vv, rhs=p_ap, start=(ti == 0), stop=(ti == 1), skip_group_check=True)
                    return e.matmul(ud.t[:, 128:128 + nq], lhsT=kb.ones[0:nk, :], rhs=p_ap, start=False, stop=(ti == 1), skip_group_check=True)
                tl.append(dict(kT=kT_t[:, k0 + ti * 128:k0 + ti * 128 + nk], q=q_ap, nk=nk, nq=nq, bias=bt[ti][0:nk, 0:nq],
                               bcol=(hcol if (ti == 0 and hcol is not None) else 0.0), bdeps=[hv.b, ab.b], pv=pv, pvr=[vb], pvw=[ud.b],
                               kdeps=[kTb, qT.b, ab.b], after=(lambda: after(ud)) if ti == 1 else None))
            return tl

        for kv in range(2 if (stage >= 2 and stage not in (30, 31)) else 0):
            R.reset()
            bk = R.take(1152, BF16); bvv = R.take(9 * 128, BF16)
            P.dma("sp", bk.t[:], bkT_d[kv * 128:(kv + 1) * 128, :], writes=[bk.b])
            bv3 = bvv.t[:].rearrange("p (a c) -> p a c", a=9)
            P.dma("sp", bv3, bV_d[:, kv * 128:(kv + 1) * 128].rearrange("(a p) c -> p a c", p=128), writes=[bvv.b])
            for gi in range(4):
                hd = kv * 4 + gi
                wv, wb = wchunk()
                do_q(wv, wb, 0, 2, 0)
                do_z(wv, wb, 128, 0)
                P.dma("sp", ab.t[:, 0:2, :], abias_d[:, (24 + hd * 2) * 128:(26 + hd * 2) * 128].rearrange("p (a c) -> p a c", a=2), writes=[ab.b])
                tls = []
                for qb in range(8):
                    def after(ud, qb=qb, hd=hd):
                        fin_fm(ud.t[:, 0:128], ud.t[:, 128:256], [ud.b], 128, 0, qb * 128, 4 + hd, qb * 128, esk_col=esk.t[:, hd:hd + 1])
                    tls += band_block(bk.t, bk.b, qb * 128, 128, qT.t[:, 0, qb * 128:(qb + 1) * 128], 128, (ab.t[:, 0, :], ab.t[:, 1, :]),
                                      bv3[:, qb, :], bv3[:, qb + 1, :], bvv.b, hv.t[:, 0:1] if qb == 0 else None, after)
                tiles_pipeline(tls)
            P.barrier()
        for h in range(4 if stage >= 3 else 0):
            R.reset()
            ak = [R.take(AW[g] + (64 if g == 2 else 0), BF16) for g in range(3)]
            av0 = R.take(9 * 128, BF16); av1 = R.take(12 * 128, BF16); av2h = R.take(16 * 128, BF16); av2o = R.take(16 * 128, BF16)
            Ua = R.take(T, F32); Sa = R.take(T, F32)
            for g in range(3):
                if g >= 1 and 'k' in _ASK:
                    continue
                if g == 2:
                    P.op("dve", lambda e: e.memset(ak[2].t[:, 3072:3136], 0.0), writes=[ak[2].b])
                P.dma("sp", ak[g].t[:, 0:AW[g]], akT_d[g][h * 128:(h + 1) * 128, :], writes=[ak[g].b])
            v0 = av0.t[:].rearrange("p (a c) -> p a c", a=9)
            v1 = av1.t[:].rearrange("p (a c) -> p a c", a=12)
            v2h = av2h.t[:].rearrange("p (a c) -> p a c", a=16)
            v2o = av2o.t[:].rearrange("p (a c) -> p a c", a=16)
            P.dma("sp", v0, aV_d[0][:, h * 128:(h + 1) * 128].rearrange("(a p) c -> p a c", p=128), writes=[av0.b])
            if 'k' not in _ASK:
                P.dma("sp", v1, aV_d[1][:, h * 128:(h + 1) * 128].rearrange("(a p) c -> p a c", p=128), writes=[av1.b])
            a2 = aV_d[2][:, h * 128:(h + 1) * 128].rearrange("(r w) c -> w r c", w=192)
            if 'v' not in _ASK:
                P.dma("sp", v2h[:, 0:8, :], a2[0:128, 0:8, :], writes=[av2h.b])
                P.dma("sp", v2h[:, 8:16, :], a2[0:128, 8:16, :], writes=[av2h.b])
                P.op("dve", lambda e: e.memset(av2o.t[:], 0.0), writes=[av2o.b])
                P.dma("sp", v2o[0:64, :, :], a2[128:192, :, :], writes=[av2o.b])
            wv, wb = wchunk()
            do_q(wv, wb, 0, 0, 0)
            do_q(wv, wb, 128, 0, 1, 1 if 'q' in _ASK else 4)
            wv, wb = wchunk()
            do_q(wv, wb, 0, 0, 2, 1 if 'q' in _ASK else 16)
            do_z(wv, wb, 128, 0)
            for g in range(3):
                P.dma("sp", ab.t[:, 2 * g:2 * g + 2, :], abias_d[:, ((g * 4 + h) * 2) * 128:((g * 4 + h) * 2 + 2) * 128].rearrange("p (a c) -> p a c", a=2),
                      writes=[ab.b])

            def evac(ud, nq, dst_u, dst_s, first):
                if first:
                    P.op("dve", lambda e: e.tensor_copy(out=dst_u, in_=ud.t[:, 0:nq]), reads=[ud.b], writes=[Ua.b])
                    P.op("dve", lambda e: e.tensor_copy(out=dst_s, in_=ud.t[:, 128:128 + nq]), reads=[ud.b], writes=[Sa.b])
                else:
                    P.op("dve", lambda e: e.tensor_tensor(out=dst_u, in0=dst_u, in1=ud.t[:, 0:nq], op=ALU.add), reads=[ud.b, Ua.b], writes=[Ua.b])
                    P.op("dve", lambda e: e.tensor_tensor(out=dst_s, in0=dst_s, in1=ud.t[:, 128:128 + nq], op=ALU.add), reads=[ud.b, Sa.b], writes=[Sa.b])
            tls = []
            for qb in range(0 if 'g' in _ASK else 8):
                tls += band_block(ak[0].t, ak[0].b, qb * 128, 128, qT.t[:, 0, qb * 128:(qb + 1) * 128], 128, (ab.t[:, 0, :], ab.t[:, 1, :]),
                           v0[:, qb, :], v0[:, qb + 1, :], av0.b, hv.t[:, 0:1] if qb == 0 else None,
                           lambda ud, qb=qb: evac(ud, 128, Ua.t[:, qb * 128:(qb + 1) * 128], Sa.t[:, qb * 128:(qb + 1) * 128], True))
            for r in range(4 if stage != 31 else 0):
                for sbk in range(2):
                    st0 = 4 * sbk * 128 + r
                    tls += band_block(ak[1].t, ak[1].b, r * 384 + sbk * 128, 128, qT.t[:, 1, r * 256 + sbk * 128:r * 256 + sbk * 128 + 128], 128,
                               (ab.t[:, 2, :], ab.t[:, 3, :]), v1[:, r * 3 + sbk, :], v1[:, r * 3 + sbk + 1, :], av1.b,
                               hv.t[:, 0:1] if sbk == 0 else None,
                               lambda ud, st0=st0: evac(ud, 128, Ua.t[:, st0:st0 + 509:4], Sa.t[:, st0:st0 + 509:4], False))
            for r in range(16 if stage != 31 else 0):
                tls += band_block(ak[2].t, ak[2].b, r * 192, 128, qT.t[:, 2, r * 64:(r + 1) * 64], 64, (ab.t[:, 4, :], ab.t[:, 5, :]),
                           v2h[:, r, :], v2o[:, r, :], av2h.b, hv.t[:, 1:2],
                           lambda ud, r=r: evac(ud, 64, Ua.t[:, r:1024:16], Sa.t[:, r:1024:16], False))
            tiles_pipeline(tls)
            for th in range(0 if 'f' in _ASK else 2):
                fin_fm(Ua.t[:, th * 512:(th + 1) * 512], Sa.t[:, th * 512:(th + 1) * 512], [Ua.b, Sa.b, av2o.b], 512, 0, th * 512, h, th * 512)
            P.barrier()
        for h in range(4 if (stage >= 4 and stage not in (30, 31)) else 0):
            R.reset()
            ck = R.take(2 * 4096, BF16); cva = R.take(32 * 257, BF16); cko = R.take(2 * 1024, BF16); cvo = R.take(8 * 257, BF16)
            o0s = R.take(2 * 257, F32); yf = R.take(256, F32); ysq = R.take(256, F32); yn = R.take(256, BF16); sm = R.take(8, F32)
            ck3 = ck.t[:].rearrange("p (a c) -> p a c", a=2); cko3 = cko.t[:].rearrange("p (a c) -> p a c", a=2)
            cva3 = cva.t[:].rearrange("p (a c) -> p a c", a=32); cvo3 = cvo.t[:].rearrange("p (a c) -> p a c", a=8)
            o03 = o0s.t[:].rearrange("p (a c) -> p a c", a=2)
            for c_ in range(2):
                for r_ in range(4):
                    P.dma("sp", ck3[:, c_, r_ * 1024:(r_ + 1) * 1024],
                          kTa_rows(r_, (14 + h * 2 + c_) * 128, 128), writes=[ck.b])
                P.dma("sp", cko3[:, c_, :], kTl_rows((14 + h * 2 + c_) * 128, 128), writes=[cko.b])
            for q4 in range(4):
                P.dma("sp", cva3[:, q4 * 8:(q4 + 1) * 8, 0:256],
                      va_cols(1792 + h * 256, 256)[q4 * 1024:(q4 + 1) * 1024, :].rearrange("(a p) c -> p a c", p=128), writes=[cva.b])
            P.dma("sp", cvo3[:, :, 0:256], vl_cols(1792 + h * 256, 256).rearrange("(a p) c -> p a c", p=128), writes=[cvo.b])
            P.op("dve", lambda e: e.memset(cva3[:, :, 256:257], 1.0), writes=[cva.b])
            P.op("dve", lambda e: e.memset(cvo3[:, :, 256:257], 1.0), writes=[cvo.b])
            wv, wb = wchunk()
            do_q(wv, wb, 0, 4, 0)
            do_q(wv, wb, 128, 4, 1)
            wv, wb = wchunk()
            do_z(wv, wb, 0, 0)
            do_z(wv, wb, 128, 1)
            for g in range(4):
                for c_ in range(2):
                    tl = []
                    nt = 32 + 2 * g + 2
                    for ti in range(nt):
                        own = ti >= 32
                        lk = ti - 32

                        def pv(e, p_ap, ti=ti, own=own, lk=lk, nt=nt):
                            vv = cvo3[:, lk, :] if own else cva3[:, ti, :]
                            e.matmul(kb.ud0.t[:, 0:257], lhsT=p_ap[:, 0:128], rhs=vv, start=(ti == 0), stop=(ti == nt - 1))
                            return e.matmul(kb.ud1.t[:, 0:257], lhsT=p_ap[:, 128:256], rhs=vv, start=(ti == 0), stop=(ti == nt - 1))
                        bias = None
                        if own and lk == 2 * g:
                            bias = cmask.t[:, 0:256]
                        elif own and lk == 2 * g + 1:
                            bias = cmask.t[:, 256:512]
                        kT_ap = cko3[:, c_, lk * 128:(lk + 1) * 128] if own else ck3[:, c_, ti * 128:(ti + 1) * 128]
                        bcol = cbo.t[:, (h * 4 + g) * 8 + lk:(h * 4 + g) * 8 + lk + 1] if own else cb.t[:, (h * 4 + g) * 32 + ti:(h * 4 + g) * 32 + ti + 1]
                        tl.append(dict(kT=kT_ap, q=qT.t[:, c_, g * 256:(g + 1) * 256], nk=128, nq=256, bias=bias, bcol=bcol, bdeps=[cb.b, cbo.b, cmask.b],
                                       pv=pv, pvr=[cva.b, cvo.b], pvw=[kb.ud0.b, kb.ud1.b], kdeps=[ck.b, cko.b, qT.b, cmask.b]))
                    tiles_pipeline(tl)
                    if c_ == 0:
                        P.op("act", lambda e: e.activation(out=o03[:, 0, :], in_=kb.ud0.t[:, 0:257], func=AF.Copy), reads=[kb.ud0.b], writes=[o0s.b])
                        P.op("act", lambda e: e.activation(out=o03[:, 1, :], in_=kb.ud1.t[:, 0:257], func=AF.Copy), reads=[kb.ud1.b], writes=[o0s.b])
                for j in range(2):
                    ud = udl[j]
                    qblk = 2 * g + j
                    P.op("dve", lambda e, j=j: e.reciprocal(out=sm.t[:, 0:1], in_=o03[:, j, 256:257]), reads=[o0s.b], writes=[sm.b])
                    P.op("dve", lambda e, ud=ud: e.reciprocal(out=sm.t[:, 1:2], in_=ud.t[:, 256:257]), reads=[ud.b], writes=[sm.b])
                    P.op("dve", lambda e: e.tensor_tensor(out=sm.t[:, 1:2], in0=sm.t[:, 1:2], in1=nlam.t[:, 0:1], op=ALU.mult), reads=[sm.b, nlam.b], writes=[sm.b])
                    P.op("dve", lambda e, ud=ud: e.tensor_scalar(out=ysq.t[:], in0=ud.t[:, 0:256], scalar1=sm.t[:, 1:2], scalar2=None, op0=ALU.mult),
                         reads=[ud.b, sm.b], writes=[ysq.b])
                    P.op("dve", lambda e, j=j: e.scalar_tensor_tensor(out=yf.t[:], in0=o03[:, j, 0:256], scalar=sm.t[:, 0:1], in1=ysq.t[:],
                                                                      op0=ALU.mult, op1=ALU.add), reads=[o0s.b, sm.b, ysq.b], writes=[yf.b])
                    P.op("act", lambda e: e.activation(out=ysq.t[:], in_=yf.t[:], func=AF.Square), reads=[yf.b], writes=[ysq.b])
                    P.op("dve", lambda e: e.reduce_sum(out=sm.t[:, 2:3], in_=ysq.t[:], axis=AX.X), reads=[ysq.b], writes=[sm.b])
                    P.op("act", lambda e: e.activation(out=sm.t[:, 3:4], in_=sm.t[:, 2:3], func=AF.Ln, bias=kb.epsc.t[:, 0:1], scale=1.0 / 256),
                         reads=[sm.b, kb.epsc.b], writes=[sm.b])
                    P.op("act", lambda e: e.activation(out=sm.t[:, 3:4], in_=sm.t[:, 3:4], func=AF.Exp, scale=-0.5), reads=[sm.b], writes=[sm.b])
                    P.op("dve", lambda e: e.scalar_tensor_tensor(out=yn.t[:], in0=yf.t[:], scalar=sm.t[:, 3:4], in1=subg.t[:], op0=ALU.mult, op1=ALU.mult),
                         reads=[yf.b, sm.b, subg.b], writes=[yn.b])

                    def tr(e):
                        e.transpose(out=kb.ptr.t[:, 0:128], in_=yn.t[:, 0:128], identity=kb.ident)
                        return e.transpose(out=kb.ptr.t[:, 128:256], in_=yn.t[:, 128:256], identity=kb.ident)
                    P.op("pe", tr, reads=[yn.b, kb.cst.b], writes=[kb.ptr.b])
                    for e2 in range(2):
                        if debug and l == nl - 1:
                            P.op("dve", lambda e, e2=e2: e.tensor_copy(out=tf[2].t[:, 0:128], in_=kb.ptr.t[:, e2 * 128:(e2 + 1) * 128]), reads=[kb.ptr.b], writes=[tf[2].b])
                            dbg_y(tf[2].t[:, 0:128], tf[2].b, (12 + 2 * h + e2) * 128, qblk * 128, 128)
                        P.op("dve", lambda e, e2=e2, qblk=qblk, h=h: e.tensor_tensor(out=uT.t[:, 12 + 2 * h + e2, qblk * 128:(qblk + 1) * 128],
                                                                                in0=kb.ptr.t[:, e2 * 128:(e2 + 1) * 128],
                                                                                in1=zs.t[:, e2, qblk * 128:(qblk + 1) * 128], op=ALU.mult),
                             reads=[kb.ptr.b, zs.b], writes=[uT.b])
            P.barrier()
        R.reset()
        mT = R.take(KC * T, BF16)
        mT3 = TB.__new__(TB); mT3.t = mT.t[:].rearrange("p (k t) -> p k t", k=KC); mT3.b = mT.b
        bch = [(0, 4), (4, 12), (12, 20), (20, 24)]
        pb = [kb.s0, kb.s1, kb.ud0, kb.ud1]
        wcol[0] = 15360 - 8192
        for oc in range(16 if (stage >= 5 and stage not in (30, 31)) else 0):
            r = next_ring(kb)
            wbr = r.t[:, 0:24 * 128].rearrange("p (k c) -> p k c", k=24)
            for q2 in range(2):
                P.dma("pool", wbr[:, q2 * 12:(q2 + 1) * 12, :],
                      wbr_d[q2 * 1536:(q2 + 1) * 1536, oc * 128:(oc + 1) * 128].rearrange("(k p) c -> p k c", p=128), writes=[r.b])
            for th in range(2):
                for b_ in range(4):
                    P.dma("sp", gsb[b_].t[:], gates_d[(oc * 4 + b_) * 128:(oc * 4 + b_ + 1) * 128, th * 512:(th + 1) * 512],
                          reads=[Bgates], writes=[gsb[b_].b])
                for b_ in range(4):
                    def f(e, b_=b_, th=th, wbr=wbr):
                        last = None
                        lo, hi = bch[b_]
                        for k in range(lo, hi):
                            last = e.matmul(pb[b_].t[:, 0:512], lhsT=wbr[:, k, :], rhs=uT.t[:, k, th * 512:(th + 1) * 512], start=(k == lo), stop=(k == hi - 1))
                        return last
                    P.op("pe", f, reads=[r.b, uT.b], writes=[pb[b_].b])
                P.op("dve", lambda e: e.tensor_tensor(out=tf[0].t[:], in0=pb[0].t[:, 0:512], in1=gsb[0].t[:], op=ALU.mult), reads=[pb[0].b, gsb[0].b], writes=[tf[0].b])
                for b_ in range(1, 4):
                    P.op("dve", lambda e, b_=b_: e.tensor_tensor(out=tf[1].t[:], in0=pb[b_].t[:, 0:512], in1=gsb[b_].t[:], op=ALU.mult),
                         reads=[pb[b_].b, gsb[b_].b], writes=[tf[1].b])
                    if b_ < 3:
                        P.op("dve", lambda e: e.tensor_tensor(out=tf[0].t[:], in0=tf[0].t[:], in1=tf[1].t[:], op=ALU.add), reads=[tf[0].b, tf[1].b], writes=[tf[0].b])
                    else:
                        P.op("dve", lambda e, oc=oc, th=th: e.tensor_tensor(out=mT3.t[:, oc, th * 512:(th + 1) * 512], in0=tf[0].t[:], in1=tf[1].t[:], op=ALU.add),
                             reads=[tf[0].b, tf[1].b], writes=[mT.b])
        if debug and l == nl - 1:
            finals.append(P.dma("sp", md_d.rearrange("(k p) t -> p k t", p=128), mT3.t[:], reads=[mT.b]))
        i_ = 0
        for cc in range(8):
            wv, wb = load_w(kb, wout_d, cc * 256, 256)
            for t in range(8):
                pp = next_pp(kb)
                proj_tm(kb, wv, wb, 0, 256, mT3, t * 128, pp)
                x1, o1 = xs[i_ % 2], ot[i_ % 2]
                i_ += 1
                P.dma("sp", x1.t[:], x_src[t * 128:(t + 1) * 128, cc * 256:(cc + 1) * 256], reads=list(xdep), writes=[x1.b])
                P.op("dve", lambda e, pp=pp, x1=x1, o1=o1: e.tensor_tensor(out=o1.t[:], in0=pp.t[:, 0:256], in1=x1.t[:], op=ALU.add),
                     reads=[pp.b, x1.b], writes=[o1.b])
                fo = P.dma("sp", x_dst[t * 128:(t + 1) * 128, cc * 256:(cc + 1) * 256], o1.t[:], reads=[o1.b], writes=list(xdst_b))
                if l == nl - 1:
                    finals.append(fo)
    P.emit(kb.nc, finals)
    kb.st.close()
    return kb.nc


_BF = ml_dtypes.bfloat16
_KCOLS = list(range(1536, 3072)) + list(range(5632, 5888)) + list(range(7168, 8192))
_VCOLS = list(range(3072, 4608)) + list(range(5888, 6144)) + list(range(8192, 9216))


def _main_cols():
    cols = []
    blk = lambda s: list(range(s, s + 128))
    for hh in range(4):
        cols += blk(9216 + hh * 128) + blk(12288 + hh * 128)
    for hd in range(8):
        cols += blk(4608 + hd * 128) + blk(10240 + hd * 128)
    for h in range(4):
        cols += blk(h * 128) + blk(512 + h * 128) + blk(1024 + h * 128) + blk(9728 + h * 128)
    for h in range(4):
        cols += blk(6144 + h * 256) + blk(6144 + h * 256 + 128) + blk(11264 + h * 256) + blk(11264 + h * 256 + 128)
    for oc in range(16):
        for b in range(4):
            cols += blk(12800 + b * 2048 + oc * 128)
    return cols


def _tables():
    consts = np.concatenate([np.eye(128), np.ones((128, 128)), np.full((128, 128), 1 / 128)], 1).astype(_BF)
    w = np.arange(128)[:, None].astype(np.float64)
    qi = np.arange(128)[None, :].astype(np.float64)
    ab = np.zeros((128, 40, 128), np.float32)
    for g in range(3):
        for h in range(4):
            sl = 2.0 ** (-2.0 * (h + 1)) * A_DD[g]
            ab[:, (g * 4 + h) * 2 + 0, :] = np.where(w >= qi, -sl * (qi + 128 - w), NEG)
            ab[:, (g * 4 + h) * 2 + 1, :] = np.where(w <= qi, -sl * (qi - w), NEG)
    for hd in range(8):
        sl = 2.0 ** (-(hd + 1.0))
        ab[:, 24 + hd * 2 + 0, :] = np.where(w >= qi + 1, -sl * (qi + 128 - w), NEG)
        ab[:, 24 + hd * 2 + 1, :] = np.where(w <= qi, -sl * (qi - w), NEG)
    abias = ab.reshape(128, 40 * 128).astype(_BF)
    tri = np.where(w <= qi, 0.0, NEG)
    cmask = np.concatenate([tri, np.zeros((128, 128)), np.full((128, 128), NEG), tri], 1).astype(_BF)
    jp = np.arange(128).astype(np.float64)
    cbo = np.zeros((128, 128), np.float32)
    cbs = []
    for j in range(4):
        cb = np.zeros((128, 512), np.float32)
        for h in range(4):
            sl = 2.0 ** (-2.0 * (h + 1))
            for g in range(4):
                for kb_ in range(32):
                    if kb_ < 8 * j:
                        cb[:, (h * 4 + g) * 32 + kb_] = sl * ((kb_ * 128 + jp) - (j * 1024 + g * 256 + 255))
                    else:
                        cb[:, (h * 4 + g) * 32 + kb_] = NEG
                for lk in range(8):
                    cbo[:, (h * 4 + g) * 8 + lk] = sl * ((lk * 128 + jp) - (g * 256 + 255))
        cbs.append(cb)
    hvs = []
    for j in range(4):
        hv = np.zeros((128, 2), np.float32)
        hv[:, 0] = 0.0 if j >= 1 else NEG
        hv[:64, 1] = 0.0 if j >= 2 else NEG
        hv[64:, 1] = 0.0 if j >= 1 else NEG
        hvs.append(hv)
    return consts, abias, cmask, cbs, cbo, hvs


def _exchange(kTs, vs):
    outs = []
    for b in range(2):
        kc = [np.asarray(kTs[4 * b + j]) for j in range(4)]
        v_all = np.concatenate([np.asarray(vs[4 * b + j]) for j in range(4)], 0)
        ckT = np.concatenate([k[14 * 128:22 * 128] for k in kc], 1)
        cV = np.ascontiguousarray(v_all[:, 1792:2816])
        for j in range(4):
            m = {}
            for g in range(3):
                dd = A_DD[g]
                L = 1024 // dd
                rows = slice(g * 512, (g + 1) * 512)
                KG = np.concatenate([k[rows].reshape(512, dd, L) for k in kc], 2)
                lo = j * L - 128
                if lo < 0:
                    seg = np.concatenate([np.zeros((512, dd, -lo), KG.dtype), KG[:, :, 0:(j + 1) * L]], 2)
                else:
                    seg = KG[:, :, lo:(j + 1) * L]
                m["akT%d" % g] = np.ascontiguousarray(seg.reshape(512, dd * (128 + L)))
                VG = v_all[:, g * 512:(g + 1) * 512].reshape(4096 // dd, dd, 512).transpose(1, 0, 2)
                if lo < 0:
                    segv = np.concatenate([np.zeros((dd, -lo, 512), VG.dtype), VG[:, 0:(j + 1) * L]], 1)
                else:
                    segv = VG[:, lo:(j + 1) * L]
                m["aV%d" % g] = np.ascontiguousarray(segv.reshape(dd * (128 + L), 512))
            KB_ = np.concatenate([k[12 * 128:14 * 128] for k in kc], 1)
            VB_ = v_all[:, 1536:1792]
            lo = j * 1024 - 128
            if lo < 0:
                m["bkT"] = np.ascontiguousarray(np.concatenate([np.zeros((256, 128), KB_.dtype), KB_[:, 0:1024]], 1))
                m["bV"] = np.ascontiguousarray(np.concatenate([np.zeros((128, 256), VB_.dtype), VB_[0:1024]], 0))
            else:
                m["bkT"] = np.ascontiguousarray(KB_[:, lo:lo + 1152])
                m["bV"] = np.ascontiguousarray(VB_[lo:lo + 1152])
            m["ckT"] = ckT
            m["cV"] = cV
            m["ckTo"] = np.ascontiguousarray(kc[j][14 * 128:22 * 128])
            m["cVo"] = np.ascontiguousarray(v_all[j * 1024:(j + 1) * 1024, 1792:2816])
            outs.append(m)
    return outs


_CACHE = {}


def run_layer(l, xs, P, debug=False):
    consts, abias, cmask, cbs, cbo, hvs = _tables()
    w_in = P["w_in"][l]
    wkv = np.ascontiguousarray(w_in[:, _KCOLS + _VCOLS])
    gk = np.ascontiguousarray(P["qk_gain"][l][[1, 3, 5]].T)
    ng = np.ascontiguousarray(P["norm_g"][l][None, :])
    if "kv" not in _CACHE:
        _CACHE["kv"] = build_kv()
    res = run_bass_kernel_spmd(_CACHE["kv"], [{"x": xs[c], "ng": ng, "wkv": wkv, "gk": gk, "consts": consts} for c in range(8)],
                               core_ids=list(range(8)))
    ex = _exchange([r["kT"] for r in res.results], [r["v"] for r in res.results])
    lam_init = 0.8 - 0.6 * math.exp(-0.3 * l)
    key = ("main", l, debug)
    if key not in _CACHE:
        _CACHE[key] = build_main(lam_init, debug)
    wm = np.ascontiguousarray(w_in[:, _main_cols()])
    bgm = np.ascontiguousarray(P["b_gate"][l].reshape(4, 16, 128).transpose(2, 1, 0).reshape(128, 64))
    common = {"ng": ng, "w": wm, "bg": bgm, "gq": np.ascontiguousarray(P["qk_gain"][l].T),
              "sk": np.ascontiguousarray(np.broadcast_to(P["sinks"][l][None, :], (128, 8))),
              "lam": np.ascontiguousarray(np.broadcast_to(P["lam"][l].reshape(1, 512), (128, 512))),
              "subg": np.ascontiguousarray(np.broadcast_to(P["subln_g"][l][None, :], (128, 256))),
              "mng": np.ascontiguousarray(P["mem_norm_g"][l][None, :]), "wmem": P["w_mem_kv"][l], "wbr": P["w_branch"][l],
              "wout": P["w_out"][l], "abias": abias, "cmask": cmask, "cbo": cbo, "consts": consts}
    in_maps = []
    for c in range(8):
        m = dict(common)
        m.update(ex[c])
        m["x"] = xs[c]
        m["mem"] = np.ascontiguousarray(P["mem"][c // 4])
        m["cb"] = cbs[c % 4]
        m["hv"] = hvs[c % 4]
        in_maps.append(m)
    res = run_bass_kernel_spmd(_CACHE[key], in_maps, core_ids=list(range(8)))
    if debug:
        return [r["xo"] for r in res.results], [r["ydbg"] for r in res.results], [r["mdbg"] for r in res.results]
    return [r["xo"] for r in res.results]


def _sel_tables():
    sels = []
    for j in range(4):
        sl = np.zeros((128, 12), np.float32)
        for r in range(4):
            sl[:, r] = 1.0 if r == j - 1 else 0.0
            sl[:64, 4 + r] = 1.0 if r == j - 2 else 0.0
            sl[64:, 4 + r] = 1.0 if r == j - 1 else 0.0
            sl[:, 8 + r] = 1.0 if r == j - 2 else 0.0
        sels.append(sl)
    return sels


def run_fused(nl, xs, P, debug=False, l0=0):
    consts, abias, cmask, cbs, cbo, hvs = _tables()
    sels = _sel_tables()
    Ls = list(range(l0, l0 + nl))
    lam_inits = [0.8 - 0.6 * math.exp(-0.3 * l) for l in Ls]
    key = ("fused", tuple(Ls), debug)
    if key not in _CACHE:
        _CACHE[key] = build_fused(lam_inits, debug)
    mc = _main_cols()
    cat = lambda f: np.ascontiguousarray(np.concatenate([f(l) for l in Ls], 0))
    common = {
        "ng": cat(lambda l: P["norm_g"][l][None, :]), "mng": cat(lambda l: P["mem_norm_g"][l][None, :]),
        "wkv": cat(lambda l: P["w_in"][l][:, _KCOLS + _VCOLS]), "w": cat(lambda l: P["w_in"][l][:, mc]),
        "wmem": cat(lambda l: P["w_mem_kv"][l]), "wbr": cat(lambda l: P["w_branch"][l]), "wout": cat(lambda l: P["w_out"][l]),
        "bg": cat(lambda l: P["b_gate"][l].reshape(4, 16, 128).transpose(2, 1, 0).reshape(128, 64)),
        "gq": cat(lambda l: P["qk_gain"][l].T), "gk": cat(lambda l: P["qk_gain"][l][[1, 3, 5]].T),
        "sk": cat(lambda l: np.broadcast_to(P["sinks"][l][None, :], (128, 8))),
        "lam": cat(lambda l: np.broadcast_to(P["lam"][l].reshape(1, 512), (128, 512))),
        "subg": cat(lambda l: np.broadcast_to(P["subln_g"][l][None, :], (128, 256))),
        "abias": abias, "cmask": cmask, "cbo": cbo, "consts": consts}
    in_maps = []
    for c in range(8):
        m = dict(common)
        m["x"] = xs[c]
        m["mem"] = np.ascontiguousarray(P["mem"][c // 4])
        m["cb"] = cbs[c % 4]
        m["hv"] = hvs[c % 4]
        m["sel"] = sels[c % 4]
        in_maps.append(m)
    res = run_bass_kernel_spmd(_CACHE[key], in_maps, core_ids=list(range(8)))
    if debug:
        return [r["xo"] for r in res.results], [r["ydbg"] for r in res.results], [r["mdbg"] for r in res.results]
    return [r["xo"] for r in res.results]


def kernel(x, mem, norm_g, w_in, b_gate, qk_gain, sinks, lam, subln_g, mem_norm_g, w_mem_kv, w_branch, w_out):
    P = dict(mem=np.asarray(mem), norm_g=np.asarray(norm_g), w_in=np.asarray(w_in), b_gate=np.asarray(b_gate),
             qk_gain=np.asarray(qk_gain), sinks=np.asarray(sinks), lam=np.asarray(lam), subln_g=np.asarray(subln_g),
             mem_norm_g=np.asarray(mem_norm_g), w_mem_kv=np.asarray(w_mem_kv), w_branch=np.asarray(w_branch), w_out=np.asarray(w_out))
    x = np.asarray(x)
    xs = [np.ascontiguousarray(x[c // 4, (c % 4) * 1024:(c % 4 + 1) * 1024]) for c in range(8)]
    xs = run_fused(4, xs, P)
    out = np.zeros_like(x)
    for c in range(8):
        out[c // 4, (c % 4) * 1024:(c % 4 + 1) * 1024] = xs[c]
    return out
```

```python
import contextlib
import math
import os
_ASK = os.environ.get('ASKIP', '')
import numpy as np
import ml_dtypes
import concourse.bass as bass
import concourse.mybir as mybir
from concourse.bass_utils import run_bass_kernel_spmd

F32 = mybir.dt.float32
BF16 = mybir.dt.bfloat16
AF = mybir.ActivationFunctionType
ALU = mybir.AluOpType
AX = mybir.AxisListType
EPS = 1e-6
NEG = -30000.0
D = 2048
T = 1024
KC = 16
NCORE = 8


class Buf:
    __slots__ = ("last_w", "readers", "dw")

    def __init__(self):
        self.last_w = None
        self.readers = []
        self.dw = []


class Op:
    __slots__ = ("eng", "fn", "deps", "signal", "seq", "idx", "dma", "sem", "val", "inc", "ring", "epoch")


class Prog:
    def __init__(self):
        self.ops = []
        self.ring = {"sp": 16, "pool": 8, "act": 2}
        self.fence = []
        self.since = []
        self.last = {}
        self.epoch = 0

    def op(self, eng, fn, reads=(), writes=(), dma=False, inc=16, ring=None):
        o = Op()
        o.ring = ring or eng
        o.epoch = self.epoch
        o.eng, o.fn, o.idx, o.signal, o.dma, o.seq, o.sem, o.val, o.inc = eng, fn, len(self.ops), False, dma, None, None, None, inc
        deps = {}
        for b in reads:
            if b.last_w is not None:
                deps[b.last_w.idx] = b.last_w
            for w in b.dw:
                deps[w.idx] = w
        for b in writes:
            if b.last_w is not None:
                deps[b.last_w.idx] = b.last_w
            if not dma:
                for w in b.dw:
                    deps[w.idx] = w
            for r in b.readers:
                deps[r.idx] = r
        for f in self.fence:
            deps[f.idx] = f
        o.deps = list(deps.values())
        for b in reads:
            b.readers.append(o)
        for b in writes:
            if dma:
                b.dw.append(o)
            else:
                b.last_w = o
                b.dw = []
            b.readers = []
        self.ops.append(o)
        if dma:
            self.since.append(o)
        else:
            self.last[eng] = o
        return o

    def barrier(self):
        self.fence = list(self.last.values()) + list(self.since)
        self.since = []

    def dma(self, eng, out, in_, reads=(), writes=()):
        return self.op(eng, lambda e: e.dma_start(out=out, in_=in_), reads, writes, dma=True)

    def emit(self, nc, final_ops=()):
        ops = self.ops
        engs = ["pe", "act", "dve", "pool", "sp"]
        for o in ops:
            for d in o.deps:
                if d.dma:
                    continue
                if d.eng == "pe" and o.eng == "pe" and not o.dma:
                    continue
                d.signal = True
        seqc = {}
        for o in ops:
            if not o.dma and o.signal:
                k = (o.eng, o.epoch)
                seqc[k] = seqc.get(k, 0) + 1
                o.seq = seqc[k]
        stack = contextlib.ExitStack()
        prog_sem = {k: stack.enter_context(nc.semaphore("prg_%s%d" % k)) for k in seqc}
        ring_sems = {q: [stack.enter_context(nc.semaphore("rg_%s%d" % (q, i))) for i in range(n)]
                     for q, n in self.ring.items()}
        ring_cnt = {q: [0] * n for q, n in self.ring.items()}
        ring_last = {q: [None] * n for q, n in self.ring.items()}
        ring_pos = {q: 0 for q in self.ring}
        for o in ops:
            if o.dma:
                q = o.ring
                i = ring_pos[q]
                ring_pos[q] = (i + 1) % self.ring[q]
                prev = ring_last[q][i]
                if prev is not None:
                    o.deps.append(prev)
                ring_cnt[q][i] += o.inc
                o.sem = ring_sems[q][i]
                o.val = ring_cnt[q][i]
                ring_last[q][i] = o
            elif o.signal:
                o.sem = prog_sem[(o.eng, o.epoch)]
                o.val = o.seq
        per_eng = {e: [o for o in ops if o.eng == e] for e in engs}
        block = stack.enter_context(nc.Block())

        def run(eng_name, handle):
            waited = {}
            for o in per_eng[eng_name]:
                need = {}
                for d in o.deps:
                    if (not d.dma) and d.eng == "pe" and eng_name == "pe" and not o.dma:
                        continue
                    k = id(d.sem)
                    if k not in need or need[k][1] < d.val:
                        need[k] = (d.sem, d.val)
                for k, (s, v) in need.items():
                    if waited.get(k, 0) >= v:
                        continue
                    handle.wait_ge(s, v)
                    waited[k] = v
                ins = o.fn(handle)
                if o.dma:
                    ins.then_inc(o.sem, o.inc)
                elif o.signal:
                    ins.then_inc(o.sem, 1)
            if eng_name == "sp":
                for o in final_ops:
                    handle.wait_ge(o.sem, o.val)

        block.tensor(lambda e: run("pe", e))
        block.scalar(lambda e: run("act", e))
        block.vector(lambda e: run("dve", e))
        block.gpsimd(lambda e: run("pool", e))
        block.sync(lambda e: run("sp", e))
        stack.close()


class TB:
    def __init__(self, t):
        self.t = t
        self.b = Buf()


class KB:
    def __init__(self):
        self.nc = bass.Bass("TRN2", target_bir_lowering=False)
        self.P = Prog()
        self.st = contextlib.ExitStack()
        self.n = 0

    def sb(self, shape, dt):
        self.n += 1
        return TB(self.st.enter_context(self.nc.sbuf_tensor("sb%d" % self.n, list(shape), dt)))

    def ps(self, shape, dt=F32):
        self.n += 1
        return TB(self.st.enter_context(self.nc.psum_tensor("ps%d" % self.n, list(shape), dt)))

    def din(self, name, shape, dt=F32):
        return self.nc.dram_tensor(name, list(shape), dt, kind="ExternalInput").ap()

    def dout(self, name, shape, dt=F32):
        return self.nc.dram_tensor(name, list(shape), dt, kind="ExternalOutput").ap()


def common_setup(kb, consts_d):
    P = kb.P
    c = kb.sb([128, 384], BF16)
    P.dma("sp", c.t[:], consts_d, writes=[c.b])
    kb.cst = c
    kb.ident = c.t[:, 0:128]
    kb.ones = c.t[:, 128:256]
    kb.ones128 = c.t[:, 256:384]
    kb.p0 = kb.ps([128, 512])
    kb.p1 = kb.ps([128, 512])
    kb.pss = kb.ps([128, 512])
    kb.s0 = kb.ps([128, 512])
    kb.s1 = kb.ps([128, 512])
    kb.ud0 = kb.ps([128, 512])
    kb.ud1 = kb.ps([128, 512])
    kb.ptr = kb.ps([128, 1024], BF16)
    kb.ring = [kb.sb([128, 8192], BF16) for _ in range(2)]
    kb.ring_i = 0
    kb.sq = [kb.sb([128, 512], BF16) for _ in range(2)]
    kb.rstd = [kb.sb([128, 512], F32) for _ in range(2)]
    kb.qn_i = 0
    kb.pp_i = 0
    kb.epsc = kb.sb([128, 1], F32)
    P.op("dve", lambda e: e.memset(kb.epsc.t[:], EPS), writes=[kb.epsc.b])


def next_ring(kb):
    r = kb.ring[kb.ring_i % len(kb.ring)]
    kb.ring_i += 1
    return r


def next_pp(kb):
    lst = getattr(kb, "pp_list", None) or (kb.p0, kb.p1)
    p = lst[kb.pp_i % len(lst)]
    kb.pp_i += 1
    return p


def load_w(kb, w_d, c0, nc_, nk=KC):
    r = next_ring(kb)
    view = r.t[:, 0:nk * nc_].rearrange("p (k c) -> p k c", k=nk)
    src = w_d[:, c0:c0 + nc_].rearrange("(k p) c -> p k c", p=128)
    kb.P.dma("pool", view, src, writes=[r.b])
    return view, r.b


def norm_to_hT(kb, x_d, ngb, hT, ntile, scr, xdep=()):
    P = kb.P
    xt, sqf, hb, ss, rs = scr
    for t in range(ntile):
        xs = xt[t % 2]
        P.dma("sp", xs.t[:], x_d[t * 128:(t + 1) * 128, :], reads=list(xdep), writes=[xs.b])
        P.op("act", lambda e, xs=xs: e.activation(out=sqf.t[:], in_=xs.t[:], func=AF.Square), reads=[xs.b], writes=[sqf.b])
        P.op("dve", lambda e: e.reduce_sum(out=ss.t[:], in_=sqf.t[:], axis=AX.X), reads=[sqf.b], writes=[ss.b])
        P.op("act", lambda e: e.activation(out=rs.t[:], in_=ss.t[:], func=AF.Ln, bias=kb.epsc.t[:, 0:1], scale=1.0 / D),
             reads=[ss.b, kb.epsc.b], writes=[rs.b])
        P.op("act", lambda e: e.activation(out=rs.t[:], in_=rs.t[:], func=AF.Exp, scale=-0.5), reads=[rs.b], writes=[rs.b])
        P.op("dve", lambda e, xs=xs: e.scalar_tensor_tensor(out=hb.t[:], in0=xs.t[:], scalar=rs.t[:, 0:1], in1=ngb.t[:],
                                                            op0=ALU.mult, op1=ALU.mult),
             reads=[xs.b, rs.b, ngb.b], writes=[hb.b])
        for k0 in range(0, KC, 4):
            def tr(e, k0=k0):
                last = None
                for i in range(4):
                    last = e.transpose(out=kb.ptr.t[:, i * 128:(i + 1) * 128], in_=hb.t[:, (k0 + i) * 128:(k0 + i + 1) * 128],
                                       identity=kb.ident)
                return last
            P.op("pe", tr, reads=[hb.b, kb.cst.b], writes=[kb.ptr.b])
            src = kb.ptr.t[:, 0:512].rearrange("p (a b) -> p a b", a=4)
            dst = hT.t[:, k0:k0 + 4, t * 128:(t + 1) * 128]
            if (k0 // 4) % 2 == 0:
                P.op("act", lambda e, src=src, dst=dst: e.activation(out=dst, in_=src, func=AF.Copy), reads=[kb.ptr.b], writes=[hT.b])
            else:
                P.op("dve", lambda e, src=src, dst=dst: e.tensor_copy(out=dst, in_=src), reads=[kb.ptr.b], writes=[hT.b])


def proj_fm(kb, wv, wb, col, ncol, hT, tok0, ntok, pp):
    def f(e):
        last = None
        for k in range(KC):
            last = e.matmul(pp.t[0:ncol, 0:ntok], lhsT=wv[:, k, col:col + ncol], rhs=hT.t[:, k, tok0:tok0 + ntok],
                            start=(k == 0), stop=(k == KC - 1))
        return last
    kb.P.op("pe", f, reads=[wb, hT.b], writes=[pp.b])


def proj_tm(kb, wv, wb, col, ncol, hT, tok0, pp):
    def f(e):
        last = None
        for k in range(KC):
            last = e.matmul(pp.t[:, 0:ncol], lhsT=hT.t[:, k, tok0:tok0 + 128], rhs=wv[:, k, col:col + ncol],
                            start=(k == 0), stop=(k == KC - 1))
        return last
    kb.P.op("pe", f, reads=[wb, hT.b], writes=[pp.b])


def qknorm(kb, pp, ntok, gcol, gb, dst, dstb, dd=1):
    P = kb.P
    i = kb.qn_i % 2
    kb.qn_i += 1
    sq, rstd = kb.sq[i], kb.rstd[i]
    P.op("act", lambda e: e.activation(out=sq.t[:, 0:ntok], in_=pp.t[:, 0:ntok], func=AF.Square), reads=[pp.b], writes=[sq.b])
    P.op("pe", lambda e: e.matmul(kb.pss.t[:, 0:ntok], lhsT=kb.ones128, rhs=sq.t[:, 0:ntok], start=True, stop=True),
         reads=[sq.b, kb.cst.b], writes=[kb.pss.b])
    P.op("act", lambda e: e.activation(out=rstd.t[:, 0:ntok], in_=kb.pss.t[:, 0:ntok], func=AF.Ln, bias=kb.epsc.t[:, 0:1]),
         reads=[kb.pss.b, kb.epsc.b], writes=[rstd.b])
    P.op("act", lambda e: e.activation(out=rstd.t[:, 0:ntok], in_=rstd.t[:, 0:ntok], func=AF.Exp, scale=-0.5),
         reads=[rstd.b], writes=[rstd.b])
    a = pp.t[:, 0:ntok]
    b = rstd.t[:, 0:ntok]
    if dd > 1:
        a = a.rearrange("p (i r) -> p i r", r=dd)
        b = b.rearrange("p (i r) -> p i r", r=dd)
    P.op("dve", lambda e: e.scalar_tensor_tensor(out=dst, in0=a, scalar=gcol, in1=b, op0=ALU.mult, op1=ALU.mult),
         reads=[pp.b, rstd.b, gb], writes=[dstb])


A_DD = (1, 4, 16)


def build_kv():
    kb = KB()
    P = kb.P
    x_d = kb.din("x", [T, D])
    ng_d = kb.din("ng", [1, D])
    w_d = kb.din("wkv", [D, 5632])
    gk_d = kb.din("gk", [128, 3])
    c_d = kb.din("consts", [128, 384], BF16)
    kT_d = kb.dout("kT", [22 * 128, T], BF16)
    v_d = kb.dout("v", [T, 2816], BF16)
    common_setup(kb, c_d)
    hT = kb.sb([128, KC, T], BF16)
    ngb = kb.sb([128, D], F32)
    gk = kb.sb([128, 3], F32)
    P.dma("sp", ngb.t[:], ng_d[0, :].partition_broadcast(128), writes=[ngb.b])
    P.dma("sp", gk.t[:], gk_d, writes=[gk.b])
    scr = ([kb.sb([128, D], F32) for _ in range(2)], kb.sb([128, D], F32), kb.sb([128, D], BF16),
           kb.sb([128, 1], F32), kb.sb([128, 1], F32))
    norm_to_hT(kb, x_d, ngb, hT, 8, scr)
    finals = []
    kst = [kb.sb([128, T], BF16) for _ in range(2)]
    for c0 in range(0, 2816, 512):
        ncol = min(512, 2816 - c0)
        wv, wb = load_w(kb, w_d, c0, ncol)
        for hh in range(ncol // 128):
            hd = c0 // 128 + hh
            if hd < 12:
                dd, gi = A_DD[hd // 4], 0
            elif hd < 14:
                dd, gi = 1, 1
            else:
                dd, gi = 1, 2
            ks = kst[hd % 2]
            for th in range(2):
                pp = next_pp(kb)
                proj_fm(kb, wv, wb, hh * 128, 128, hT, th * 512, 512, pp)
                if dd == 1:
                    dst = ks.t[:, th * 512:(th + 1) * 512]
                else:
                    n = 512 // dd
                    dst = ks.t[:].rearrange("p (r i) -> p i r", r=dd)[:, th * n:(th + 1) * n, :]
                qknorm(kb, pp, 512, gk.t[:, gi:gi + 1], gk.b, dst, ks.b, dd)
            finals.append(P.dma("sp", kT_d[hd * 128:(hd + 1) * 128, :], ks.t[:], reads=[ks.b]))
    vst = [kb.sb([128, 8, 512], BF16) for _ in range(2)]
    ci = 0
    for c0 in range(0, 2816, 512):
        ncol = min(512, 2816 - c0)
        wv, wb = load_w(kb, w_d, 2816 + c0, ncol)
        vs = vst[ci % 2]
        ci += 1
        for t in range(8):
            pp = next_pp(kb)
            proj_tm(kb, wv, wb, 0, ncol, hT, t * 128, pp)
            if t % 2 == 0:
                P.op("act", lambda e, pp=pp, t=t, vs=vs, ncol=ncol: e.activation(out=vs.t[:, t, 0:ncol], in_=pp.t[:, 0:ncol], func=AF.Copy),
                     reads=[pp.b], writes=[vs.b])
            else:
                P.op("dve", lambda e, pp=pp, t=t, vs=vs, ncol=ncol: e.tensor_copy(out=vs.t[:, t, 0:ncol], in_=pp.t[:, 0:ncol]),
                     reads=[pp.b], writes=[vs.b])
        finals.append(P.dma("sp", v_d[:, c0:c0 + ncol].rearrange("(t p) c -> p t c", p=128), vs.t[:, :, 0:ncol], reads=[vs.b]))
    P.emit(kb.nc, finals)
    kb.st.close()
    return kb.nc


class Region:
    def __init__(self, kb, nf32):
        self.tb = kb.sb([128, nf32], F32)
        self.n = nf32
        self.off = 0

    def reset(self):
        self.off = 0

    def take(self, nelem, dt):
        w = nelem if dt == F32 else (nelem + 1) // 2
        a = self.tb.t[:, self.off:self.off + w]
        self.off += w
        assert self.off <= self.n, (self.off, self.n)
        o = TB.__new__(TB)
        o.t = a if dt == F32 else a.bitcast(BF16)
        o.b = Buf()
        return o


def build_main(lam_init, debug=False, stage=9):
    kb = KB()
    P = kb.P
    x_d = kb.din("x", [T, D]); ng_d = kb.din("ng", [1, D]); w_d = kb.din("w", [D, 15360])
    bg_d = kb.din("bg", [128, 64]); gq_d = kb.din("gq", [128, 8]); sk_d = kb.din("sk", [128, 8])
    lam_d = kb.din("lam", [128, 512]); subg_d = kb.din("subg", [128, 256])
    mem_d = kb.din("mem", [256, D]); mng_d = kb.din("mng", [1, D]); wmem_d = kb.din("wmem", [D, 1024])
    wbr_d = kb.din("wbr", [3072, D]); wout_d = kb.din("wout", [D, D])
    AW = (1152, 1536, 3072)
    akT_d = [kb.din("akT%d" % g, [512, AW[g]], BF16) for g in range(3)]
    aV_d = [kb.din("aV%d" % g, [AW[g], 512], BF16) for g in range(3)]
    bkT_d = kb.din("bkT", [256, 1152], BF16); bV_d = kb.din("bV", [1152, 256], BF16)
    ckT_d = kb.din("ckT", [1024, 4096], BF16); cV_d = kb.din("cV", [4096, 1024], BF16)
    ckTo_d = kb.din("ckTo", [1024, 1024], BF16); cVo_d = kb.din("cVo", [1024, 1024], BF16)
    abias_d = kb.din("abias", [128, 40 * 128], BF16); cmask_d = kb.din("cmask", [128, 512], BF16)
    cb_d = kb.din("cb", [128, 512]); cbo_d = kb.din("cbo", [128, 128]); hv_d = kb.din("hv", [128, 2])
    c_d = kb.din("consts", [128, 384], BF16)
    xo_d = kb.dout("xo", [T, D])
    yd_d = kb.dout("ydbg", [3072, T]) if debug else None
    md_d = kb.dout("mdbg", [D, T], BF16) if debug else None
    finals = []
    c = kb.sb([128, 384], BF16)
    P.dma("sp", c.t[:], c_d, writes=[c.b])
    kb.cst = c; kb.ident = c.t[:, 0:128]; kb.ones = c.t[:, 128:256]; kb.ones128 = c.t[:, 256:384]
    kb.p0 = kb.ps([128, 512]); kb.p1 = kb.ps([128, 512]); kb.pss = kb.ps([128, 512])
    kb.s0 = kb.ps([128, 512]); kb.s1 = kb.ps([128, 512]); kb.ud0 = kb.ps([128, 512]); kb.ud1 = kb.ps([128, 512])
    kb.ptr = kb.ps([128, 1024], BF16)
    kb.ring = [kb.sb([128, 4096], BF16) for _ in range(3)]
    kb.ring_i = 0
    kb.sq = [kb.sb([128, 512], BF16) for _ in range(2)]
    kb.rstd = [kb.sb([128, 512], F32) for _ in range(2)]
    kb.qn_i = 0; kb.pp_i = 0
    kb.epsc = kb.sb([128, 1], F32)
    P.op("dve", lambda e: e.memset(kb.epsc.t[:], EPS), writes=[kb.epsc.b])
    hT = kb.sb([128, KC, T], BF16)
    uT = kb.sb([128, 24, T], BF16)
    R = Region(kb, 11500)
    gq = kb.sb([128, 8], F32); bg = kb.sb([128, 64], F32); esk = kb.sb([128, 8], F32)
    cb = kb.sb([128, 512], F32); cbo = kb.sb([128, 128], F32); hv = kb.sb([128, 2], F32)
    cmask = kb.sb([128, 512], BF16); subg = kb.sb([128, 256], F32); nlam = kb.sb([128, 1], F32)
    qT = kb.sb([128, 3, T], BF16); zs = kb.sb([128, 2, T], BF16)
    pT = [kb.sb([128, 512], BF16) for _ in range(2)]
    tf = [kb.sb([128, 512], F32) for _ in range(3)]
    mkT = kb.sb([128, 4, 256], BF16); mv = kb.sb([128, 2, 512], BF16)
    for t_, d_ in ((gq, gq_d), (bg, bg_d), (esk, sk_d), (cb, cb_d), (cbo, cbo_d), (hv, hv_d), (cmask, cmask_d), (subg, subg_d)):
        P.dma("sp", t_.t[:], d_, writes=[t_.b])
    for col in (0, 2, 4, 6):
        P.op("dve", lambda e, col=col: e.tensor_scalar(out=gq.t[:, col:col + 1], in0=gq.t[:, col:col + 1], scalar1=128 ** -0.5,
                                                       scalar2=None, op0=ALU.mult), reads=[gq.b], writes=[gq.b])
    P.op("act", lambda e: e.activation(out=esk.t[:], in_=esk.t[:], func=AF.Exp), reads=[esk.b], writes=[esk.b])
    P.op("dve", lambda e: e.tensor_scalar(out=subg.t[:], in0=subg.t[:], scalar1=1.0 - lam_init, scalar2=None, op0=ALU.mult),
         reads=[subg.b], writes=[subg.b])
    ngb = R.take(D, F32); xt = [R.take(D, F32) for _ in range(2)]; sqf = R.take(D, F32); hb = R.take(D, BF16)
    ss = R.take(1, F32); rs = R.take(1, F32); mnT = kb.sb([128, KC, 256], BF16)
    P.dma("sp", ngb.t[:], ng_d[0, :].partition_broadcast(128), writes=[ngb.b])
    norm_to_hT(kb, x_d, ngb, hT, 8, (xt, sqf, hb, ss, rs))
    P.dma("sp", ngb.t[:], mng_d[0, :].partition_broadcast(128), writes=[ngb.b])
    norm_to_hT(kb, mem_d, ngb, mnT, 2, (xt, sqf, hb, ss, rs))
    lm = xt[0]
    P.dma("sp", lm.t[:, 0:512], lam_d, writes=[lm.b])
    P.op("dve", lambda e: e.tensor_tensor(out=lm.t[:, 512:640], in0=lm.t[:, 0:128], in1=lm.t[:, 128:256], op=ALU.mult), reads=[lm.b], writes=[lm.b])
    P.op("dve", lambda e: e.tensor_tensor(out=lm.t[:, 640:768], in0=lm.t[:, 256:384], in1=lm.t[:, 384:512], op=ALU.mult), reads=[lm.b], writes=[lm.b])
    P.op("dve", lambda e: e.reduce_sum(out=lm.t[:, 800:801], in_=lm.t[:, 512:640], axis=AX.X), reads=[lm.b], writes=[lm.b])
    P.op("dve", lambda e: e.reduce_sum(out=lm.t[:, 801:802], in_=lm.t[:, 640:768], axis=AX.X), reads=[lm.b], writes=[lm.b])
    P.op("act", lambda e: e.activation(out=lm.t[:, 800:802], in_=lm.t[:, 800:802], func=AF.Exp), reads=[lm.b], writes=[lm.b])
    P.op("dve", lambda e: e.tensor_tensor(out=nlam.t[:], in0=lm.t[:, 801:802], in1=lm.t[:, 800:801], op=ALU.subtract), reads=[lm.b], writes=[nlam.b])
    P.op("dve", lambda e: e.tensor_scalar(out=nlam.t[:], in0=nlam.t[:], scalar1=-lam_init, scalar2=None, op0=ALU.add), reads=[nlam.b], writes=[nlam.b])
    wv, wb = load_w(kb, wmem_d, 0, 256)
    wv2, wb2 = load_w(kb, wmem_d, 256, 256)
    for hh in range(4):
        pp = next_pp(kb)
        proj_fm(kb, (wv, wv2)[hh // 2], (wb, wb2)[hh // 2], (hh % 2) * 128, 128, mnT, 0, 256, pp)
        qknorm(kb, pp, 256, gq.t[:, 7:8], gq.b, mkT.t[:, hh, :], mkT.b)
    for cc in range(2):
        wv, wb = load_w(kb, wmem_d, 512 + cc * 256, 256)
        for mt in range(2):
            pp = next_pp(kb)
            proj_tm(kb, wv, wb, 0, 256, mnT, mt * 128, pp)
            P.op("act", lambda e, pp=pp, mt=mt, cc=cc: e.activation(out=mv.t[:, mt, cc * 256:(cc + 1) * 256], in_=pp.t[:, 0:256], func=AF.Copy),
                 reads=[pp.b], writes=[mv.b])
    P.barrier()
    wcol = [0]

    def wchunk():
        v = load_w(kb, w_d, wcol[0], 256)
        wcol[0] += 256
        return v

    def do_q(wv, wb, col, gcol, slot, dd=1):
        for th in range(2):
            pp = next_pp(kb)
            proj_fm(kb, wv, wb, col, 128, hT, th * 512, 512, pp)
            if dd == 1:
                dst = qT.t[:, slot, th * 512:(th + 1) * 512]
            else:
                n = 512 // dd
                dst = qT.t[:, slot, :].rearrange("p (r i) -> p i r", r=dd)[:, th * n:(th + 1) * n, :]
            qknorm(kb, pp, 512, gq.t[:, gcol:gcol + 1], gq.b, dst, qT.b, dd)

    def do_z(wv, wb, col, slot):
        for th in range(2):
            pp = next_pp(kb)
            proj_fm(kb, wv, wb, col, 128, hT, th * 512, 512, pp)
            P.op("act", lambda e, pp=pp, th=th: e.activation(out=zs.t[:, slot, th * 512:(th + 1) * 512], in_=pp.t[:, 0:512], func=AF.Silu),
                 reads=[pp.b], writes=[zs.b])

    def dbg_y(y_ap, yb, row0, c0, n):
        if debug:
            finals.append(P.dma("sp", yd_d[row0:row0 + 128, c0:c0 + n], y_ap, reads=[yb]))

    sc = [kb.s0, kb.s1]
    sci = [0]

    def tiles_pipeline(tiles):
        n = len(tiles)
        slots = []

        def qk(i):
            t = tiles[i]
            s = sc[sci[0] % 2]; p = pT[sci[0] % 2]; sci[0] += 1
            slots.append((s, p))
            nk, nq = t["nk"], t["nq"]

            def f(e):
                last = e.matmul(s.t[0:nk, 0:nq], lhsT=t["kT"], rhs=t["q"], start=True, stop=(t["bias"] is None))
                if t["bias"] is not None:
                    last = e.matmul(s.t[0:nk, 0:nq], lhsT=kb.ident[0:nk, 0:nk], rhs=t["bias"], start=False, stop=True)
                return last
            P.op("pe", f, reads=list(t["kdeps"]) + [kb.cst.b], writes=[s.b])

        def ex_pv(i):
            t = tiles[i]
            s, p = slots[i]
            nk, nq = t["nk"], t["nq"]
            P.op("act", lambda e: e.activation(out=p.t[0:nk, 0:nq], in_=s.t[0:nk, 0:nq], func=AF.Exp, bias=t["bcol"]),
                 reads=[s.b] + list(t["bdeps"]), writes=[p.b])
            P.op("pe", lambda e: t["pv"](e, p.t[0:nk, 0:nq]), reads=[p.b, kb.cst.b] + list(t["pvr"]), writes=list(t["pvw"]))
            if t.get("after"):
                t["after"]()
        for i in range(n):
            if i == 0:
                qk(0)
            if i + 1 < n:
                qk(i + 1)
            ex_pv(i)

    def fin_fm(u_ap, d_ap, ub, n, zslot, z0, chunk, c0, esk_col=None):
        a, b2 = tf[0], tf[1]
        if esk_col is not None:
            P.op("dve", lambda e: e.tensor_scalar(out=a.t[:, 0:n], in0=d_ap, scalar1=esk_col, scalar2=None, op0=ALU.add), reads=ub + [esk.b], writes=[a.b])
            P.op("dve", lambda e: e.reciprocal(out=a.t[:, 0:n], in_=a.t[:, 0:n]), reads=[a.b], writes=[a.b])
        else:
            P.op("dve", lambda e: e.reciprocal(out=a.t[:, 0:n], in_=d_ap), reads=ub, writes=[a.b])
        P.op("dve", lambda e: e.tensor_tensor(out=b2.t[:, 0:n], in0=u_ap, in1=a.t[:, 0:n], op=ALU.mult), reads=ub + [a.b], writes=[b2.b])
        dbg_y(b2.t[:, 0:n], b2.b, chunk * 128, c0, n)
        P.op("dve", lambda e: e.tensor_tensor(out=uT.t[:, chunk, c0:c0 + n], in0=b2.t[:, 0:n], in1=zs.t[:, zslot, z0:z0 + n], op=ALU.mult),
             reads=[b2.b, zs.b], writes=[uT.b])

    for hh in range(4 if stage >= 1 else 0):
        wv, wb = wchunk()
        do_q(wv, wb, 0, 6, 0)
        do_z(wv, wb, 128, 0)
        for th in range(2):
            tl = []
            for mt in range(2):
                def pv(e, p_ap, mt=mt, hh=hh):
                    e.matmul(kb.ud0.t[:, 0:512], lhsT=mv.t[:, mt, hh * 128:(hh + 1) * 128], rhs=p_ap, start=(mt == 0), stop=(mt == 1))
                    return e.matmul(kb.ud1.t[:, 0:512], lhsT=kb.ones, rhs=p_ap, start=(mt == 0), stop=(mt == 1))
                tl.append(dict(kT=mkT.t[:, hh, mt * 128:(mt + 1) * 128], q=qT.t[:, 0, th * 512:(th + 1) * 512], nk=128, nq=512, bias=None,
                               bcol=0.0, bdeps=[], pv=pv, pvr=[mv.b], pvw=[kb.ud0.b, kb.ud1.b], kdeps=[mkT.b, qT.b]))
            tiles_pipeline(tl)
            fin_fm(kb.ud0.t[:, 0:512], kb.ud1.t[:, 0:512], [kb.ud0.b, kb.ud1.b], 512, 0, th * 512, 20 + hh, th * 512)
    P.barrier()
    ab = kb.sb([128, 6, 128], BF16)
    udl = [kb.ud0, kb.ud1]
    udi = [0]

    def band_block(kT_t, kTb, k0, nk1, q_ap, nq, bt, vt0, vt1, vb, hcol, after):
        ud = udl[udi[0] % 2]; udi[0] += 1
        tl = []
        for ti in range(2):
            nk = 128 if ti == 0 else nk1
            vv = vt0 if ti == 0 else vt1

            def pv(e, p_ap, ti=ti, nk=nk, vv=vv):
                e.matmul(ud.t[:, 0:nq], lhsT=vv, rhs=p_ap, start=(ti == 0), stop=(ti == 1), skip_group_check=True)
                return e.matmul(ud.t[:, 128:128 + nq], lhsT=kb.ones[0:nk, :], rhs=p_ap, start=False, stop=(ti == 1), skip_group_check=True)
            tl.append(dict(kT=kT_t[:, k0 + ti * 128:k0 + ti * 128 + nk], q=q_ap, nk=nk, nq=nq, bias=bt[ti][0:nk, 0:nq],
                           bcol=(hcol if (ti == 0 and hcol is not None) else 0.0), bdeps=[hv.b, ab.b], pv=pv, pvr=[vb], pvw=[ud.b],
                           kdeps=[kTb, qT.b, ab.b], after=(lambda: after(ud)) if ti == 1 else None))
        tiles_pipeline(tl)

    for kv in range(2 if (stage >= 2 and stage not in (30, 31)) else 0):
        R.reset()
        bk = R.take(1152, BF16); bvv = R.take(9 * 128, BF16)
        P.dma("sp", bk.t[:], bkT_d[kv * 128:(kv + 1) * 128, :], writes=[bk.b])
        bv3 = bvv.t[:].rearrange("p (a c) -> p a c", a=9)
        P.dma("sp", bv3, bV_d[:, kv * 128:(kv + 1) * 128].rearrange("(a p) c -> p a c", p=128), writes=[bvv.b])
        for gi in range(4):
            hd = kv * 4 + gi
            wv, wb = wchunk()
            do_q(wv, wb, 0, 2, 0)
            do_z(wv, wb, 128, 0)
            P.dma("sp", ab.t[:, 0:2, :], abias_d[:, (24 + hd * 2) * 128:(26 + hd * 2) * 128].rearrange("p (a c) -> p a c", a=2), writes=[ab.b])
            for qb in range(8):
                def after(ud, qb=qb, hd=hd):
                    fin_fm(ud.t[:, 0:128], ud.t[:, 128:256], [ud.b], 128, 0, qb * 128, 4 + hd, qb * 128, esk_col=esk.t[:, hd:hd + 1])
                band_block(bk.t, bk.b, qb * 128, 128, qT.t[:, 0, qb * 128:(qb + 1) * 128], 128, (ab.t[:, 0, :], ab.t[:, 1, :]),
                           bv3[:, qb, :], bv3[:, qb + 1, :], bvv.b, hv.t[:, 0:1] if qb == 0 else None, after)
        P.barrier()
    for h in range(4 if stage >= 3 else 0):
        R.reset()
        ak = [R.take(AW[g] + (64 if g == 2 else 0), BF16) for g in range(3)]
        av0 = R.take(9 * 128, BF16); av1 = R.take(12 * 128, BF16); av2h = R.take(16 * 128, BF16); av2o = R.take(16 * 128, BF16)
        Ua = R.take(T, F32); Sa = R.take(T, F32)
        for g in range(3):
            if g >= 1 and 'k' in _ASK:
                continue
            if g == 2:
                P.op("dve", lambda e: e.memset(ak[2].t[:, 3072:3136], 0.0), writes=[ak[2].b])
            P.dma("sp", ak[g].t[:, 0:AW[g]], akT_d[g][h * 128:(h + 1) * 128, :], writes=[ak[g].b])
        v0 = av0.t[:].rearrange("p (a c) -> p a c", a=9)
        v1 = av1.t[:].rearrange("p (a c) -> p a c", a=12)
        v2h = av2h.t[:].rearrange("p (a c) -> p a c", a=16)
        v2o = av2o.t[:].rearrange("p (a c) -> p a c", a=16)
        P.dma("sp", v0, aV_d[0][:, h * 128:(h + 1) * 128].rearrange("(a p) c -> p a c", p=128), writes=[av0.b])
        if 'k' not in _ASK:
            P.dma("sp", v1, aV_d[1][:, h * 128:(h + 1) * 128].rearrange("(a p) c -> p a c", p=128), writes=[av1.b])
        a2 = aV_d[2][:, h * 128:(h + 1) * 128].rearrange("(r w) c -> w r c", w=192)
        if 'v' not in _ASK:
            P.dma("sp", v2h[:, 0:8, :], a2[0:128, 0:8, :], writes=[av2h.b])
            P.dma("sp", v2h[:, 8:16, :], a2[0:128, 8:16, :], writes=[av2h.b])
            P.op("dve", lambda e: e.memset(av2o.t[:], 0.0), writes=[av2o.b])
            P.dma("sp", v2o[0:64, :, :], a2[128:192, :, :], writes=[av2o.b])
        wv, wb = wchunk()
        do_q(wv, wb, 0, 0, 0)
        do_q(wv, wb, 128, 0, 1, 1 if 'q' in _ASK else 4)
        wv, wb = wchunk()
        do_q(wv, wb, 0, 0, 2, 1 if 'q' in _ASK else 16)
        do_z(wv, wb, 128, 0)
        for g in range(3):
            P.dma("sp", ab.t[:, 2 * g:2 * g + 2, :], abias_d[:, ((g * 4 + h) * 2) * 128:((g * 4 + h) * 2 + 2) * 128].rearrange("p (a c) -> p a c", a=2),
                  writes=[ab.b])

        def evac(ud, nq, dst_u, dst_s, first):
            if first:
                P.op("dve", lambda e: e.tensor_copy(out=dst_u, in_=ud.t[:, 0:nq]), reads=[ud.b], writes=[Ua.b])
                P.op("dve", lambda e: e.tensor_copy(out=dst_s, in_=ud.t[:, 128:128 + nq]), reads=[ud.b], writes=[Sa.b])
            else:
                P.op("dve", lambda e: e.tensor_tensor(out=dst_u, in0=dst_u, in1=ud.t[:, 0:nq], op=ALU.add), reads=[ud.b, Ua.b], writes=[Ua.b])
                P.op("dve", lambda e: e.tensor_tensor(out=dst_s, in0=dst_s, in1=ud.t[:, 128:128 + nq], op=ALU.add), reads=[ud.b, Sa.b], writes=[Sa.b])
        for qb in range(0 if 'g' in _ASK else 8):
            band_block(ak[0].t, ak[0].b, qb * 128, 128, qT.t[:, 0, qb * 128:(qb + 1) * 128], 128, (ab.t[:, 0, :], ab.t[:, 1, :]),
                       v0[:, qb, :], v0[:, qb + 1, :], av0.b, hv.t[:, 0:1] if qb == 0 else None,
                       lambda ud, qb=qb: evac(ud, 128, Ua.t[:, qb * 128:(qb + 1) * 128], Sa.t[:, qb * 128:(qb + 1) * 128], True))
        for r in range(4 if stage != 31 else 0):
            for sbk in range(2):
                st0 = 4 * sbk * 128 + r
                band_block(ak[1].t, ak[1].b, r * 384 + sbk * 128, 128, qT.t[:, 1, r * 256 + sbk * 128:r * 256 + sbk * 128 + 128], 128,
                           (ab.t[:, 2, :], ab.t[:, 3, :]), v1[:, r * 3 + sbk, :], v1[:, r * 3 + sbk + 1, :], av1.b,
                           hv.t[:, 0:1] if sbk == 0 else None,
                           lambda ud, st0=st0: evac(ud, 128, Ua.t[:, st0:st0 + 509:4], Sa.t[:, st0:st0 + 509:4], False))
        for r in range(16 if stage != 31 else 0):
            band_block(ak[2].t, ak[2].b, r * 192, 128, qT.t[:, 2, r * 64:(r + 1) * 64], 64, (ab.t[:, 4, :], ab.t[:, 5, :]),
                       v2h[:, r, :], v2o[:, r, :], av2h.b, hv.t[:, 1:2],
                       lambda ud, r=r: evac(ud, 64, Ua.t[:, r:1024:16], Sa.t[:, r:1024:16], False))
        for th in range(0 if 'f' in _ASK else 2):
            fin_fm(Ua.t[:, th * 512:(th + 1) * 512], Sa.t[:, th * 512:(th + 1) * 512], [Ua.b, Sa.b, av2o.b], 512, 0, th * 512, h, th * 512)
        P.barrier()
    for h in range(4 if (stage >= 4 and stage not in (30, 31)) else 0):
        R.reset()
        ck = R.take(2 * 4096, BF16); cva = R.take(32 * 257, BF16); cko = R.take(2 * 1024, BF16); cvo = R.take(8 * 257, BF16)
        o0s = R.take(2 * 257, F32); yf = R.take(256, F32); ysq = R.take(256, F32); yn = R.take(256, BF16); sm = R.take(8, F32)
        ck3 = ck.t[:].rearrange("p (a c) -> p a c", a=2); cko3 = cko.t[:].rearrange("p (a c) -> p a c", a=2)
        cva3 = cva.t[:].rearrange("p (a c) -> p a c", a=32); cvo3 = cvo.t[:].rearrange("p (a c) -> p a c", a=8)
        o03 = o0s.t[:].rearrange("p (a c) -> p a c", a=2)
        for c_ in range(2):
            P.dma("sp", ck3[:, c_, :], ckT_d[(h * 2 + c_) * 128:(h * 2 + c_ + 1) * 128, :], writes=[ck.b])
            P.dma("sp", cko3[:, c_, :], ckTo_d[(h * 2 + c_) * 128:(h * 2 + c_ + 1) * 128, :], writes=[cko.b])
        for q4 in range(4):
            P.dma("sp", cva3[:, q4 * 8:(q4 + 1) * 8, 0:256],
                  cV_d[q4 * 1024:(q4 + 1) * 1024, h * 256:(h + 1) * 256].rearrange("(a p) c -> p a c", p=128), writes=[cva.b])
        P.dma("sp", cvo3[:, :, 0:256], cVo_d[:, h * 256:(h + 1) * 256].rearrange("(a p) c -> p a c", p=128), writes=[cvo.b])
        P.op("dve", lambda e: e.memset(cva3[:, :, 256:257], 1.0), writes=[cva.b])
        P.op("dve", lambda e: e.memset(cvo3[:, :, 256:257], 1.0), writes=[cvo.b])
        wv, wb = wchunk()
        do_q(wv, wb, 0, 4, 0)
        do_q(wv, wb, 128, 4, 1)
        wv, wb = wchunk()
        do_z(wv, wb, 0, 0)
        do_z(wv, wb, 128, 1)
        for g in range(4):
            for c_ in range(2):
                tl = []
                nt = 32 + 2 * g + 2
                for ti in range(nt):
                    own = ti >= 32
                    lk = ti - 32

                    def pv(e, p_ap, ti=ti, own=own, lk=lk, nt=nt):
                        vv = cvo3[:, lk, :] if own else cva3[:, ti, :]
                        e.matmul(kb.ud0.t[:, 0:257], lhsT=p_ap[:, 0:128], rhs=vv, start=(ti == 0), stop=(ti == nt - 1))
                        return e.matmul(kb.ud1.t[:, 0:257], lhsT=p_ap[:, 128:256], rhs=vv, start=(ti == 0), stop=(ti == nt - 1))
                    bias = None
                    if own and lk == 2 * g:
                        bias = cmask.t[:, 0:256]
                    elif own and lk == 2 * g + 1:
                        bias = cmask.t[:, 256:512]
                    kT_ap = cko3[:, c_, lk * 128:(lk + 1) * 128] if own else ck3[:, c_, ti * 128:(ti + 1) * 128]
                    bcol = cbo.t[:, (h * 4 + g) * 8 + lk:(h * 4 + g) * 8 + lk + 1] if own else cb.t[:, (h * 4 + g) * 32 + ti:(h * 4 + g) * 32 + ti + 1]
                    tl.append(dict(kT=kT_ap, q=qT.t[:, c_, g * 256:(g + 1) * 256], nk=128, nq=256, bias=bias, bcol=bcol, bdeps=[cb.b, cbo.b, cmask.b],
                                   pv=pv, pvr=[cva.b, cvo.b], pvw=[kb.ud0.b, kb.ud1.b], kdeps=[ck.b, cko.b, qT.b, cmask.b]))
                tiles_pipeline(tl)
                if c_ == 0:
                    P.op("act", lambda e: e.activation(out=o03[:, 0, :], in_=kb.ud0.t[:, 0:257], func=AF.Copy), reads=[kb.ud0.b], writes=[o0s.b])
                    P.op("act", lambda e: e.activation(out=o03[:, 1, :], in_=kb.ud1.t[:, 0:257], func=AF.Copy), reads=[kb.ud1.b], writes=[o0s.b])
            for j in range(2):
                ud = udl[j]
                qblk = 2 * g + j
                P.op("dve", lambda e, j=j: e.reciprocal(out=sm.t[:, 0:1], in_=o03[:, j, 256:257]), reads=[o0s.b], writes=[sm.b])
                P.op("dve", lambda e, ud=ud: e.reciprocal(out=sm.t[:, 1:2], in_=ud.t[:, 256:257]), reads=[ud.b], writes=[sm.b])
                P.op("dve", lambda e: e.tensor_tensor(out=sm.t[:, 1:2], in0=sm.t[:, 1:2], in1=nlam.t[:, 0:1], op=ALU.mult), reads=[sm.b, nlam.b], writes=[sm.b])
                P.op("dve", lambda e, ud=ud: e.tensor_scalar(out=ysq.t[:], in0=ud.t[:, 0:256], scalar1=sm.t[:, 1:2], scalar2=None, op0=ALU.mult),
                     reads=[ud.b, sm.b], writes=[ysq.b])
                P.op("dve", lambda e, j=j: e.scalar_tensor_tensor(out=yf.t[:], in0=o03[:, j, 0:256], scalar=sm.t[:, 0:1], in1=ysq.t[:],
                                                                  op0=ALU.mult, op1=ALU.add), reads=[o0s.b, sm.b, ysq.b], writes=[yf.b])
                P.op("act", lambda e: e.activation(out=ysq.t[:], in_=yf.t[:], func=AF.Square), reads=[yf.b], writes=[ysq.b])
                P.op("dve", lambda e: e.reduce_sum(out=sm.t[:, 2:3], in_=ysq.t[:], axis=AX.X), reads=[ysq.b], writes=[sm.b])
                P.op("act", lambda e: e.activation(out=sm.t[:, 3:4], in_=sm.t[:, 2:3], func=AF.Ln, bias=kb.epsc.t[:, 0:1], scale=1.0 / 256),
                     reads=[sm.b, kb.epsc.b], writes=[sm.b])
                P.op("act", lambda e: e.activation(out=sm.t[:, 3:4], in_=sm.t[:, 3:4], func=AF.Exp, scale=-0.5), reads=[sm.b], writes=[sm.b])
                P.op("dve", lambda e: e.scalar_tensor_tensor(out=yn.t[:], in0=yf.t[:], scalar=sm.t[:, 3:4], in1=subg.t[:], op0=ALU.mult, op1=ALU.mult),
                     reads=[yf.b, sm.b, subg.b], writes=[yn.b])

                def tr(e):
                    e.transpose(out=kb.ptr.t[:, 0:128], in_=yn.t[:, 0:128], identity=kb.ident)
                    return e.transpose(out=kb.ptr.t[:, 128:256], in_=yn.t[:, 128:256], identity=kb.ident)
                P.op("pe", tr, reads=[yn.b, kb.cst.b], writes=[kb.ptr.b])
                for e2 in range(2):
                    if debug:
                        P.op("dve", lambda e, e2=e2: e.tensor_copy(out=tf[2].t[:, 0:128], in_=kb.ptr.t[:, e2 * 128:(e2 + 1) * 128]), reads=[kb.ptr.b], writes=[tf[2].b])
                        dbg_y(tf[2].t[:, 0:128], tf[2].b, (12 + 2 * h + e2) * 128, qblk * 128, 128)
                    P.op("dve", lambda e, e2=e2, qblk=qblk, h=h: e.tensor_tensor(out=uT.t[:, 12 + 2 * h + e2, qblk * 128:(qblk + 1) * 128],
                                                                            in0=kb.ptr.t[:, e2 * 128:(e2 + 1) * 128],
                                                                            in1=zs.t[:, e2, qblk * 128:(qblk + 1) * 128], op=ALU.mult),
                         reads=[kb.ptr.b, zs.b], writes=[uT.b])
        P.barrier()
    R.reset()
    mT = R.take(KC * T, BF16)
    mT3 = TB.__new__(TB); mT3.t = mT.t[:].rearrange("p (k t) -> p k t", k=KC); mT3.b = mT.b
    gsb = [kb.sb([128, 512], BF16) for _ in range(4)]
    bch = [(0, 4), (4, 12), (12, 20), (20, 24)]
    pb = [kb.s0, kb.s1, kb.ud0, kb.ud1]
    wcol[0] = 15360 - 8192
    for oc in range(16 if (stage >= 5 and stage not in (30, 31)) else 0):
        wa = wchunk(); wb_ = wchunk()
        r = next_ring(kb)
        wbr = r.t[:, 0:24 * 128].rearrange("p (k c) -> p k c", k=24)
        for q2 in range(2):
            P.dma("pool", wbr[:, q2 * 12:(q2 + 1) * 12, :],
                  wbr_d[q2 * 1536:(q2 + 1) * 1536, oc * 128:(oc + 1) * 128].rearrange("(k p) c -> p k c", p=128), writes=[r.b])
        for th in range(2):
            for b_ in range(4):
                pp = next_pp(kb)
                wv, wb = (wa, wb_)[b_ // 2]
                proj_fm(kb, wv, wb, (b_ % 2) * 128, 128, hT, th * 512, 512, pp)
                P.op("act", lambda e, pp=pp, b_=b_, oc=oc: e.activation(out=gsb[b_].t[:], in_=pp.t[:, 0:512], func=AF.Sigmoid,
                                                                       bias=bg.t[:, oc * 4 + b_:oc * 4 + b_ + 1]),
                     reads=[pp.b, bg.b], writes=[gsb[b_].b])
            for b_ in range(4):
                def f(e, b_=b_, th=th, wbr=wbr):
                    last = None
                    lo, hi = bch[b_]
                    for k in range(lo, hi):
                        last = e.matmul(pb[b_].t[:, 0:512], lhsT=wbr[:, k, :], rhs=uT.t[:, k, th * 512:(th + 1) * 512], start=(k == lo), stop=(k == hi - 1))
                    return last
                P.op("pe", f, reads=[r.b, uT.b], writes=[pb[b_].b])
            P.op("dve", lambda e: e.tensor_tensor(out=tf[0].t[:], in0=pb[0].t[:, 0:512], in1=gsb[0].t[:], op=ALU.mult), reads=[pb[0].b, gsb[0].b], writes=[tf[0].b])
            for b_ in range(1, 4):
                P.op("dve", lambda e, b_=b_: e.tensor_tensor(out=tf[1].t[:], in0=pb[b_].t[:, 0:512], in1=gsb[b_].t[:], op=ALU.mult),
                     reads=[pb[b_].b, gsb[b_].b], writes=[tf[1].b])
                if b_ < 3:
                    P.op("dve", lambda e: e.tensor_tensor(out=tf[0].t[:], in0=tf[0].t[:], in1=tf[1].t[:], op=ALU.add), reads=[tf[0].b, tf[1].b], writes=[tf[0].b])
                else:
                    P.op("dve", lambda e, oc=oc, th=th: e.tensor_tensor(out=mT3.t[:, oc, th * 512:(th + 1) * 512], in0=tf[0].t[:], in1=tf[1].t[:], op=ALU.add),
                         reads=[tf[0].b, tf[1].b], writes=[mT.b])
    if debug:
        finals.append(P.dma("sp", md_d.rearrange("(k p) t -> p k t", p=128), mT3.t[:], reads=[mT.b]))
    xs = [kb.sb([128, 256], F32) for _ in range(2)]
    ot = [kb.sb([128, 256], F32) for _ in range(2)]
    i_ = 0
    for cc in range(8):
        wv, wb = load_w(kb, wout_d, cc * 256, 256)
        for t in range(8):
            pp = next_pp(kb)
            proj_tm(kb, wv, wb, 0, 256, mT3, t * 128, pp)
            x1, o1 = xs[i_ % 2], ot[i_ % 2]
            i_ += 1
            P.dma("sp", x1.t[:], x_d[t * 128:(t + 1) * 128, cc * 256:(cc + 1) * 256], writes=[x1.b])
            P.op("dve", lambda e, pp=pp, x1=x1, o1=o1: e.tensor_tensor(out=o1.t[:], in0=pp.t[:, 0:256], in1=x1.t[:], op=ALU.add),
                 reads=[pp.b, x1.b], writes=[o1.b])
            finals.append(P.dma("sp", xo_d[t * 128:(t + 1) * 128, cc * 256:(cc + 1) * 256], o1.t[:], reads=[o1.b]))
    P.emit(kb.nc, finals)
    kb.st.close()
    return kb.nc


def build_fused(lam_inits, debug=False, stage=9):
    kb = KB()
    P = kb.P
    P.ring["cc"] = 12
    P.ring["sp"] = 32
    P.ring["act"] = 12
    nl = len(lam_inits)
    x_d = kb.din("x", [T, D]); mem_d = kb.din("mem", [256, D])
    ng_all = kb.din("ng", [nl, D]); mng_all = kb.din("mng", [nl, D])
    wkv_all = kb.din("wkv", [nl * D, 5632]); w_all = kb.din("w", [nl * D, 15360])
    wmem_all = kb.din("wmem", [nl * D, 1024]); wbr_all = kb.din("wbr", [nl * 3072, D]); wout_all = kb.din("wout", [nl * D, D])
    bg_all = kb.din("bg", [nl * 128, 64]); gq_all = kb.din("gq", [nl * 128, 8]); sk_all = kb.din("sk", [nl * 128, 8])
    gk_all = kb.din("gk", [nl * 128, 3])
    lam_all = kb.din("lam", [nl * 128, 512]); subg_all = kb.din("subg", [nl * 128, 256])
    abias_d = kb.din("abias", [128, 40 * 128], BF16); cmask_d = kb.din("cmask", [128, 512], BF16)
    cb_d = kb.din("cb", [128, 512]); cbo_d = kb.din("cbo", [128, 128]); hv_d = kb.din("hv", [128, 2]); sel_d = kb.din("sel", [128, 12])
    c_d = kb.din("consts", [128, 384], BF16)
    xo_d = kb.dout("xo", [T, D])
    yd_d = kb.dout("ydbg", [3072, T]) if debug else None
    md_d = kb.dout("mdbg", [D, T], BF16) if debug else None
    nc = kb.nc
    xb = [nc.dram_tensor("xb%d" % i, [T, D], F32).ap() for i in range(2)]
    Bxb = [Buf(), Buf()]
    KP = [512] * 5 + [256]
    kTl_p = [nc.dram_tensor("kTl%d" % p, [KP[p], T], BF16) for p in range(6)]
    kTa_p = [nc.dram_tensor("kTa%d" % p, [4 * KP[p], T], BF16) for p in range(6)]
    vl_p = [nc.dram_tensor("vl%d" % p, [T, KP[p]], BF16) for p in range(6)]
    va_p = [nc.dram_tensor("va%d" % p, [4 * T, KP[p]], BF16) for p in range(6)]
    BkTl = [Buf() for _ in range(6)]; Bvl = [Buf() for _ in range(6)]; BkTa = [Buf() for _ in range(6)]; Bva = [Buf() for _ in range(6)]

    def kTl_rows(q0, n):
        p = q0 // 512
        return kTl_p[p].ap()[q0 - p * 512:q0 - p * 512 + n, :]

    def kTa_rows(r, q0, n):
        p = q0 // 512
        o = r * KP[p] + q0 - p * 512
        return kTa_p[p].ap()[o:o + n, :]

    def vl_cols(c0, n):
        p = c0 // 512
        return vl_p[p].ap()[:, c0 - p * 512:c0 - p * 512 + n]

    def va_cols(c0, n):
        p = c0 // 512
        return va_p[p].ap()[:, c0 - p * 512:c0 - p * 512 + n]
    RG = [[0, 1, 2, 3], [4, 5, 6, 7]]
    AW = (1152, 1536, 3072)
    gates_d = nc.dram_tensor("gates_scr", [8192, T], BF16).ap()
    Bgates = Buf()
    akT_d = [nc.dram_tensor("akT_rel%d" % g, [512, AW[g]], BF16).ap() for g in range(3)]
    aV_d = [nc.dram_tensor("aV_rel%d" % g, [AW[g], 512], BF16).ap() for g in range(3)]
    bkT_d = nc.dram_tensor("bkT_rel", [256, 1152], BF16).ap(); bV_d = nc.dram_tensor("bV_rel", [1152, 256], BF16).ap()
    finals = []
    c = kb.sb([128, 384], BF16)
    P.dma("sp", c.t[:], c_d, writes=[c.b])
    kb.cst = c; kb.ident = c.t[:, 0:128]; kb.ones = c.t[:, 128:256]; kb.ones128 = c.t[:, 256:384]
    kb.p0 = kb.ps([128, 512]); kb.p1 = kb.ps([128, 512]); kb.pss = kb.ps([128, 512])
    kb.s0 = kb.ps([128, 512]); kb.s1 = kb.ps([128, 512]); kb.ud0 = kb.ps([128, 512]); kb.ud1 = kb.ps([128, 512])
    kb.ptr = kb.ps([128, 1024], BF16)
    kb.ring = [kb.sb([128, 4096], BF16) for _ in range(3)]
    kb.ring_i = 0
    kb.sq = [kb.sb([128, 512], BF16) for _ in range(2)]
    kb.rstd = [kb.sb([128, 512], F32) for _ in range(2)]
    kb.qn_i = 0; kb.pp_i = 0
    kb.pp_list = [kb.p0, kb.p1, kb.s0, kb.s1]
    pend = [None]

    def push(projf, epif):
        pp = next_pp(kb)
        projf(pp)
        epif(pp)

    def drain():
        if pend[0] is not None:
            pend[0][0](pend[0][1])
            pend[0] = None
    kb.epsc = kb.sb([128, 1], F32)
    P.op("dve", lambda e: e.memset(kb.epsc.t[:], EPS), writes=[kb.epsc.b])
    hT = kb.sb([128, KC, T], BF16)
    uT = kb.sb([128, 24, T], BF16)
    R = Region(kb, 11500)
    gq = kb.sb([128, 8], F32); bg = kb.sb([128, 64], F32); esk = kb.sb([128, 8], F32)
    cb = kb.sb([128, 512], F32); cbo = kb.sb([128, 128], F32); hv = kb.sb([128, 2], F32)
    cmask = kb.sb([128, 512], BF16); subg = kb.sb([128, 256], F32); nlam = kb.sb([128, 1], F32)
    qT = kb.sb([128, 3, T], BF16); zs = kb.sb([128, 2, T], BF16)
    pT = [kb.sb([128, 512], BF16) for _ in range(4)]
    tf = [kb.sb([128, 512], F32) for _ in range(3)]
    mkT = kb.sb([128, 4, 256], BF16); mv = kb.sb([128, 2, 512], BF16)
    gk = kb.sb([128, 3], F32); sel = kb.sb([128, 12], F32)
    mnT = kb.sb([128, KC, 256], BF16)
    ab = kb.sb([128, 6, 128], BF16)
    gsb = [kb.sb([128, 512], BF16) for _ in range(4)]
    xs = [kb.sb([128, 256], F32) for _ in range(2)]
    ot = [kb.sb([128, 256], F32) for _ in range(2)]
    stg = kb.sb([128, 2048], BF16)
    for t_, d_ in ((cb, cb_d), (cbo, cbo_d), (hv, hv_d), (cmask, cmask_d), (sel, sel_d)):
        P.dma("sp", t_.t[:], d_, writes=[t_.b])
    for l in range(nl):
        if l > 0:
            P.barrier()
            P.epoch = l
        lam_init = lam_inits[l]
        x_src = x_d if l == 0 else xb[(l - 1) % 2]
        xdep = [] if l == 0 else [Bxb[(l - 1) % 2]]
        x_dst = xo_d if l == nl - 1 else xb[l % 2]
        xdst_b = [] if l == nl - 1 else [Bxb[l % 2]]
        w_d = w_all[l * D:(l + 1) * D, :]; wkv_d = wkv_all[l * D:(l + 1) * D, :]
        wmem_d = wmem_all[l * D:(l + 1) * D, :]; wbr_d = wbr_all[l * 3072:(l + 1) * 3072, :]; wout_d = wout_all[l * D:(l + 1) * D, :]
        R.reset()
        for t_, d_ in ((gq, gq_all), (bg, bg_all), (esk, sk_all), (subg, subg_all), (gk, gk_all)):
            P.dma("sp", t_.t[:], d_[l * 128:(l + 1) * 128, :], writes=[t_.b])
        for col in (0, 2, 4, 6):
            P.op("dve", lambda e, col=col: e.tensor_scalar(out=gq.t[:, col:col + 1], in0=gq.t[:, col:col + 1], scalar1=128 ** -0.5,
                                                           scalar2=None, op0=ALU.mult), reads=[gq.b], writes=[gq.b])
        P.op("act", lambda e: e.activation(out=esk.t[:], in_=esk.t[:], func=AF.Exp), reads=[esk.b], writes=[esk.b])
        P.op("dve", lambda e, li=lam_init: e.tensor_scalar(out=subg.t[:], in0=subg.t[:], scalar1=1.0 - li, scalar2=None, op0=ALU.mult),
             reads=[subg.b], writes=[subg.b])
        ngb = R.take(D, F32); xt = [R.take(D, F32) for _ in range(2)]; sqf = R.take(D, F32); hb = R.take(D, BF16)
        ss = R.take(1, F32); rs = R.take(1, F32)
        P.dma("sp", ngb.t[:], ng_all[l, :].partition_broadcast(128), writes=[ngb.b])
        norm_to_hT(kb, x_src, ngb, hT, 8, (xt, sqf, hb, ss, rs), xdep)
        P.dma("sp", ngb.t[:], mng_all[l, :].partition_broadcast(128), writes=[ngb.b])
        norm_to_hT(kb, mem_d, ngb, mnT, 2, (xt, sqf, hb, ss, rs))
        lm = xt[0]
        P.dma("sp", lm.t[:, 0:512], lam_all[l * 128:(l + 1) * 128, :], writes=[lm.b])
        P.op("dve", lambda e: e.tensor_tensor(out=lm.t[:, 512:640], in0=lm.t[:, 0:128], in1=lm.t[:, 128:256], op=ALU.mult), reads=[lm.b], writes=[lm.b])
        P.op("dve", lambda e: e.tensor_tensor(out=lm.t[:, 640:768], in0=lm.t[:, 256:384], in1=lm.t[:, 384:512], op=ALU.mult), reads=[lm.b], writes=[lm.b])
        P.op("dve", lambda e: e.reduce_sum(out=lm.t[:, 800:801], in_=lm.t[:, 512:640], axis=AX.X), reads=[lm.b], writes=[lm.b])
        P.op("dve", lambda e: e.reduce_sum(out=lm.t[:, 801:802], in_=lm.t[:, 640:768], axis=AX.X), reads=[lm.b], writes=[lm.b])
        P.op("act", lambda e: e.activation(out=lm.t[:, 800:802], in_=lm.t[:, 800:802], func=AF.Exp), reads=[lm.b], writes=[lm.b])
        P.op("dve", lambda e: e.tensor_tensor(out=nlam.t[:], in0=lm.t[:, 801:802], in1=lm.t[:, 800:801], op=ALU.subtract), reads=[lm.b], writes=[nlam.b])
        P.op("dve", lambda e, li=lam_init: e.tensor_scalar(out=nlam.t[:], in0=nlam.t[:], scalar1=-li, scalar2=None, op0=ALU.add), reads=[nlam.b], writes=[nlam.b])
        P.barrier()
        R.reset()
        kst = [R.take(T, BF16) for _ in range(2)]
        vst = [R.take(8 * 256, BF16) for _ in range(2)]
        for c0 in range(0, 2816, 256):
            wv, wb = load_w(kb, wkv_d, c0, 256)
            for hh in range(2):
                hd = c0 // 128 + hh
                if hd < 12:
                    dd, gi = A_DD[hd // 4], 0
                elif hd < 14:
                    dd, gi = 1, 1
                else:
                    dd, gi = 1, 2
                ks = kst[hd % 2]
                for th in range(2):
                    if dd == 1:
                        dst = ks.t[:, th * 512:(th + 1) * 512]
                    else:
                        n = 512 // dd
                        dst = ks.t[:].rearrange("p (r i) -> p i r", r=dd)[:, th * n:(th + 1) * n, :]

                    def epi(pp, gi=gi, dst=dst, ks=ks, dd=dd, th=th, hd=hd):
                        qknorm(kb, pp, 512, gk.t[:, gi:gi + 1], gk.b, dst, ks.b, dd)
                        if th == 1:
                            P.dma("sp", kTl_rows(hd * 128, 128), ks.t[:], reads=[ks.b], writes=[BkTl[hd // 4]])
                    push(lambda pp, wv=wv, wb=wb, hh=hh, th=th: proj_fm(kb, wv, wb, hh * 128, 128, hT, th * 512, 512, pp), epi)
        drain()
        for p_ in range(6):
            P.op("pool", lambda e, p_=p_: e.collective_compute("AllGather", ALU.bypass, replica_groups=RG, ins=[kTl_p[p_].ap().opt()],
                                                               outs=[kTa_p[p_].ap().opt()]),
                 reads=[BkTl[p_]], writes=[BkTa[p_]], dma=True, inc=1, ring="cc")
        ci = 0
        for c0 in range(0, 2816, 256):
            wv, wb = load_w(kb, wkv_d, 2816 + c0, 256)
            vs = vst[ci % 2]
            vs3 = vs.t[:].rearrange("p (a c) -> p a c", a=8)
            ci += 1
            for t in range(8):
                pp = next_pp(kb)
                proj_tm(kb, wv, wb, 0, 256, hT, t * 128, pp)
                if t % 2 == 0:
                    P.op("act", lambda e, pp=pp, t=t, vs3=vs3: e.activation(out=vs3[:, t, :], in_=pp.t[:, 0:256], func=AF.Copy),
                         reads=[pp.b], writes=[vs.b])
                else:
                    P.op("dve", lambda e, pp=pp, t=t, vs3=vs3: e.tensor_copy(out=vs3[:, t, :], in_=pp.t[:, 0:256]),
                         reads=[pp.b], writes=[vs.b])
            P.dma("sp", vl_cols(c0, 256).rearrange("(t p) c -> p t c", p=128), vs3, reads=[vs.b], writes=[Bvl[c0 // 512]])
        for p_ in range(6):
            P.op("pool", lambda e, p_=p_: e.collective_compute("AllGather", ALU.bypass, replica_groups=RG, ins=[vl_p[p_].ap().opt()],
                                                               outs=[va_p[p_].ap().opt()]),
                 reads=[Bvl[p_]], writes=[Bva[p_]], dma=True, inc=1, ring="cc")
        stq = [R.take(2048, BF16) for _ in range(4)]
        sti = [0]
        accs = [R.take(4096, BF16) for _ in range(2)]
        acci = [0]

        def next_acc():
            acci[0] += 1
            return accs[acci[0] % 2]

        def select(dst_ap, cands, scols, shape, acc, dep):
            for r in range(4):
                st = stq[sti[0] % 4]; sti[0] += 1
                for fn, src in cands[r]:
                    P.dma("sp", fn(st.t), src, reads=list(dep), writes=[st.b])
                for (accv, stv, scol) in shape(st.t):
                    sc_ = sel.t[:, scol + r:scol + r + 1]
                    if r == 0:
                        P.op("dve", lambda e, accv=accv, stv=stv, sc_=sc_: e.tensor_scalar(out=accv, in0=stv, scalar1=sc_, scalar2=None, op0=ALU.mult),
                             reads=[st.b, sel.b], writes=[acc.b])
                    else:
                        P.op("dve", lambda e, accv=accv, stv=stv, sc_=sc_: e.scalar_tensor_tensor(out=accv, in0=stv, scalar=sc_, in1=accv, op0=ALU.mult, op1=ALU.add),
                             reads=[st.b, sel.b, acc.b], writes=[acc.b])
            for dd_, ss_ in (dst_ap if isinstance(dst_ap, list) else [dst_ap]):
                P.dma("sp", dd_, ss_, reads=[acc.b])

        acc = next_acc()
        a3 = acc.t[:, 0:512].rearrange("p (h c) -> p h c", h=4)
        select((akT_d[0][:, 0:128].rearrange("(h d) c -> d h c", d=128), a3),
               [[(lambda st: st[:, 0:512].rearrange("p (h c) -> p h c", h=4), kTa_rows(r, 0, 512)[:, 896:1024].rearrange("(h d) c -> d h c", d=128))] for r in range(4)],
               None, lambda st: [(a3, st[:, 0:512].rearrange("p (h c) -> p h c", h=4), 0)], acc, BkTa)
        acc = next_acc()
        a3 = acc.t[:, 0:256].rearrange("p (h c) -> p h c", h=2)
        select((bkT_d[:, 0:128].rearrange("(h d) c -> d h c", d=128), a3),
               [[(lambda st: st[:, 0:256].rearrange("p (h c) -> p h c", h=2), kTa_rows(r, 1536, 256)[:, 896:1024].rearrange("(h d) c -> d h c", d=128))] for r in range(4)],
               None, lambda st: [(a3, st[:, 0:256].rearrange("p (h c) -> p h c", h=2), 0)], acc, BkTa)
        acc = next_acc()
        a2_ = acc.t[:, 0:512]
        select((aV_d[0][0:128, :], a2_),
               [[(lambda st: st[:, 0:512], va_cols(0, 512)[r * 1024 + 896:r * 1024 + 1024, :])] for r in range(4)],
               None, lambda st: [(a2_, st[:, 0:512], 0)], acc, Bva)
        acc = next_acc()
        a2_ = acc.t[:, 0:256]
        select((bV_d[0:128, :], a2_),
               [[(lambda st: st[:, 0:256], va_cols(1536, 256)[r * 1024 + 896:r * 1024 + 1024, :])] for r in range(4)],
               None, lambda st: [(a2_, st[:, 0:256], 0)], acc, Bva)
        acc = next_acc()
        a4 = acc.t[:, 0:2048].rearrange("p (h rr c) -> p h rr c", h=4, rr=4)
        select([(akT_d[1][h_ * 128:(h_ + 1) * 128, :].rearrange("d (rr w) -> d rr w", rr=4)[:, :, 0:128], a4[:, h_]) for h_ in range(4)],
               [[(lambda st, h_=h_: st[:, h_ * 512:(h_ + 1) * 512].rearrange("p (rr c) -> p rr c", rr=4),
                  kTa_rows(r, 512 + h_ * 128, 128).rearrange("d (rr i) -> d rr i", rr=4)[:, :, 128:256])
                 for h_ in range(4)] for r in range(4)],
               None, lambda st: [(a4, st[:].rearrange("p (h rr c) -> p h rr c", h=4, rr=4), 0)], acc, BkTa)
        acc = next_acc()
        a3 = acc.t[:, 0:2048].rearrange("p (rr c) -> p rr c", rr=4)
        select((aV_d[1].rearrange("(rr w) c -> w rr c", rr=4)[0:128, :, :], a3),
               [[(lambda st: st[:].rearrange("p (rr c) -> p rr c", rr=4),
                  va_cols(512, 512)[r * 1024 + 512:r * 1024 + 1024, :].rearrange("(i rr) c -> i rr c", rr=4))] for r in range(4)],
               None, lambda st: [(a3, st[:].rearrange("p (rr c) -> p rr c", rr=4), 0)], acc, Bva)
        for hp in range(2):
            acc = next_acc()
            a4 = acc.t[:].rearrange("p (h rr c) -> p h rr c", h=2, rr=16)
            select([(akT_d[2][(hp * 2 + hh) * 128:(hp * 2 + hh + 1) * 128, :].rearrange("d (rr w) -> d rr w", rr=16)[:, :, 0:128], a4[:, hh]) for hh in range(2)],
                   [[(lambda st, hh=hh: st[:, hh * 1024:(hh + 1) * 1024],
                      kTa_rows(r, 1024 + (hp * 2 + hh) * 128, 128)) for hh in range(2)] for r in range(4)],
                   None, lambda st: [(a4[:, :, :, 0:64], st[:].rearrange("p (h rr c) -> p h rr c", h=2, rr=16), 8),
                                     (a4[:, :, :, 64:128], st[:].rearrange("p (h rr c) -> p h rr c", h=2, rr=16), 0)], acc, BkTa)
        for q4 in range(4):
            acc = next_acc()
            a3 = acc.t[:, 0:2048].rearrange("p (rr c) -> p rr c", rr=4)
            cands = []
            for r in range(4):
                src = va_cols(1024, 512)[r * 1024:(r + 1) * 1024, :].rearrange("(i rr) c -> i rr c", rr=16)[:, q4 * 4:(q4 + 1) * 4, :]
                cands.append([(lambda st: st[0:64, :].rearrange("p (rr c) -> p rr c", rr=4), src),
                              (lambda st: st[64:128, :].rearrange("p (rr c) -> p rr c", rr=4), src)])
            select((aV_d[2].rearrange("(rr w) c -> w rr c", rr=16)[0:128, q4 * 4:(q4 + 1) * 4, :], a3), cands,
                   None, lambda st: [(a3, st[:].rearrange("p (rr c) -> p rr c", rr=4), 4)], acc, Bva)
        for g in range(3):
            dd = A_DD[g]; L = 1024 // dd
            P.dma("sp", akT_d[g].rearrange("q (rr w) -> q rr w", rr=dd)[:, :, 128:128 + L],
                  kTl_rows(g * 512, 512).rearrange("q (rr i) -> q rr i", rr=dd), reads=BkTl)
            P.dma("sp", aV_d[g].rearrange("(rr w) c -> rr w c", rr=dd)[:, 128:128 + L, :],
                  vl_cols(g * 512, 512).rearrange("(i rr) c -> rr i c", rr=dd), reads=Bvl)
        P.dma("sp", bkT_d[:, 128:1152], kTl_rows(1536, 256), reads=BkTl)
        P.dma("sp", bV_d[128:1152, :], vl_cols(1536, 256), reads=Bvl)
        gsi = 0
        for oc in range(16):
            wa = load_w(kb, w_d, 7168 + oc * 512, 256); wb_ = load_w(kb, w_d, 7168 + oc * 512 + 256, 256)
            for th in range(2):
                for b_ in range(4):
                    pp = next_pp(kb)
                    wv, wb = (wa, wb_)[b_ // 2]
                    proj_fm(kb, wv, wb, (b_ % 2) * 128, 128, hT, th * 512, 512, pp)
                    g_ = gsb[gsi % 4]; gsi += 1
                    P.op("act", lambda e, pp=pp, b_=b_, oc=oc, g_=g_: e.activation(out=g_.t[:], in_=pp.t[:, 0:512], func=AF.Sigmoid,
                                                                                  bias=bg.t[:, oc * 4 + b_:oc * 4 + b_ + 1]),
                         reads=[pp.b, bg.b], writes=[g_.b])
                    P.dma("act", gates_d[(oc * 4 + b_) * 128:(oc * 4 + b_ + 1) * 128, th * 512:(th + 1) * 512], g_.t[:], reads=[g_.b], writes=[Bgates])
        wv, wb = load_w(kb, wmem_d, 0, 256)
        wv2, wb2 = load_w(kb, wmem_d, 256, 256)
        for hh in range(4):
            pp = next_pp(kb)
            proj_fm(kb, (wv, wv2)[hh // 2], (wb, wb2)[hh // 2], (hh % 2) * 128, 128, mnT, 0, 256, pp)
            qknorm(kb, pp, 256, gq.t[:, 7:8], gq.b, mkT.t[:, hh, :], mkT.b)
        for cc in range(2):
            wv, wb = load_w(kb, wmem_d, 512 + cc * 256, 256)
            for mt in range(2):
                pp = next_pp(kb)
                proj_tm(kb, wv, wb, 0, 256, mnT, mt * 128, pp)
                P.op("act", lambda e, pp=pp, mt=mt, cc=cc: e.activation(out=mv.t[:, mt, cc * 256:(cc + 1) * 256], in_=pp.t[:, 0:256], func=AF.Copy),
                     reads=[pp.b], writes=[mv.b])
        P.barrier()
        wcol = [0]

        def wchunk():
            v = load_w(kb, w_d, wcol[0], 256)
            wcol[0] += 256
            return v

        def do_q(wv, wb, col, gcol, slot, dd=1):
            for th in range(2):
                if dd == 1:
                    dst = qT.t[:, slot, th * 512:(th + 1) * 512]
                else:
                    n = 512 // dd
                    dst = qT.t[:, slot, :].rearrange("p (r i) -> p i r", r=dd)[:, th * n:(th + 1) * n, :]
                push(lambda pp, wv=wv, wb=wb, col=col, th=th: proj_fm(kb, wv, wb, col, 128, hT, th * 512, 512, pp),
                     lambda pp, gcol=gcol, dst=dst, dd=dd: qknorm(kb, pp, 512, gq.t[:, gcol:gcol + 1], gq.b, dst, qT.b, dd))

        def do_z(wv, wb, col, slot):
            for th in range(2):
                pp = next_pp(kb)
                proj_fm(kb, wv, wb, col, 128, hT, th * 512, 512, pp)
                P.op("act", lambda e, pp=pp, th=th: e.activation(out=zs.t[:, slot, th * 512:(th + 1) * 512], in_=pp.t[:, 0:512], func=AF.Silu),
                     reads=[pp.b], writes=[zs.b])

        def dbg_y(y_ap, yb, row0, c0, n):
            if debug and l == nl - 1:
                finals.append(P.dma("sp", yd_d[row0:row0 + 128, c0:c0 + n], y_ap, reads=[yb]))

        sc = [kb.s0, kb.s1, kb.p0, kb.p1]
        sci = [0]

        def tiles_pipeline(tiles):
            drain()
            n = len(tiles)
            slots = []

            def qk(i):
                t = tiles[i]
                s = sc[sci[0] % 4]; p = pT[sci[0] % 4]; sci[0] += 1
                slots.append((s, p))
                nk, nq = t["nk"], t["nq"]

                def f(e):
                    last = e.matmul(s.t[0:nk, 0:nq], lhsT=t["kT"], rhs=t["q"], start=True, stop=(t["bias"] is None))
                    if t["bias"] is not None:
                        last = e.matmul(s.t[0:nk, 0:nq], lhsT=kb.ident[0:nk, 0:nk], rhs=t["bias"], start=False, stop=True)
                    return last
                P.op("pe", f, reads=list(t["kdeps"]) + [kb.cst.b], writes=[s.b])

            def ex_pv(i):
                t = tiles[i]
                s, p = slots[i]
                nk, nq = t["nk"], t["nq"]
                P.op("act", lambda e: e.activation(out=p.t[0:nk, 0:nq], in_=s.t[0:nk, 0:nq], func=AF.Exp, bias=t["bcol"]),
                     reads=[s.b] + list(t["bdeps"]), writes=[p.b])
                P.op("pe", lambda e: t["pv"](e, p.t[0:nk, 0:nq]), reads=[p.b, kb.cst.b] + list(t["pvr"]), writes=list(t["pvw"]))
                if t.get("after"):
                    t["after"]()
            DEPTH = 4
            for i in range(min(DEPTH, n)):
                qk(i)
            for i in range(n):
                ex_pv(i)
                if i + DEPTH < n:
                    qk(i + DEPTH)

        def fin_fm(u_ap, d_ap, ub, n, zslot, z0, chunk, c0, esk_col=None):
            a, b2 = tf[0], tf[1]
            if esk_col is not None:
                P.op("dve", lambda e: e.tensor_scalar(out=a.t[:, 0:n], in0=d_ap, scalar1=esk_col, scalar2=None, op0=ALU.add), reads=ub + [esk.b], writes=[a.b])
                P.op("dve", lambda e: e.reciprocal(out=a.t[:, 0:n], in_=a.t[:, 0:n]), reads=[a.b], writes=[a.b])
            else:
                P.op("dve", lambda e: e.reciprocal(out=a.t[:, 0:n], in_=d_ap), reads=ub, writes=[a.b])
            P.op("dve", lambda e: e.tensor_tensor(out=b2.t[:, 0:n], in0=u_ap, in1=a.t[:, 0:n], op=ALU.mult), reads=ub + [a.b], writes=[b2.b])
            dbg_y(b2.t[:, 0:n], b2.b, chunk * 128, c0, n)
            P.op("dve", lambda e: e.tensor_tensor(out=uT.t[:, chunk, c0:c0 + n], in0=b2.t[:, 0:n], in1=zs.t[:, zslot, z0:z0 + n], op=ALU.mult),
                 reads=[b2.b, zs.b], writes=[uT.b])

        for hh in range(4 if stage >= 1 else 0):
            wv, wb = wchunk()
            do_q(wv, wb, 0, 6, 0)
            do_z(wv, wb, 128, 0)
            for th in range(2):
                tl = []
                for mt in range(2):
                    def pv(e, p_ap, mt=mt, hh=hh):
                        e.matmul(kb.ud0.t[:, 0:512], lhsT=mv.t[:, mt, hh * 128:(hh + 1) * 128], rhs=p_ap, start=(mt == 0), stop=(mt == 1))
                        return e.matmul(kb.ud1.t[:, 0:512], lhsT=kb.ones, rhs=p_ap, start=(mt == 0), stop=(mt == 1))
                    tl.append(dict(kT=mkT.t[:, hh, mt * 128:(mt + 1) * 128], q=qT.t[:, 0, th * 512:(th + 1) * 512], nk=128, nq=512, bias=None,
                                   bcol=0.0, bdeps=[], pv=pv, pvr=[mv.b], pvw=[kb.ud0.b, kb.ud1.b], kdeps=[mkT.b, qT.b]))
                tiles_pipeline(tl)
                fin_fm(kb.ud0.t[:, 0:512], kb.ud1.t[:, 0:512], [kb.ud0.b, kb.ud1.b], 512, 0, th * 512, 20 + hh, th * 512)
        P.barrier()
        udl = [kb.ud0, kb.ud1]
        udi = [0]

        def band_block(kT_t, kTb, k0, nk1, q_ap, nq, bt, vt0, vt1, vb, hcol, after):
            ud = udl[udi[0] % 2]; udi[0] += 1
            tl = []
            for ti in range(2):
                nk = 128 if ti == 0 else nk1
                vv = vt0 if ti == 0 else vt1

                def pv(e, p_ap, ti=ti, nk=nk, vv=vv):
                    e.matmul(ud.t[:, 0:nq], lhsT=vv, rhs=p_ap, start=(ti == 0), stop=(ti == 1), skip_group_check=True)
                    return e.matmul(ud.t[:, 128:128 + nq], lhsT=kb.ones[0:nk, :], rhs=p_ap, start=False, stop=(ti == 1), skip_group_check=True)
                tl.append(dict(kT=kT_t[:, k0 + ti * 128:k0 + ti * 128 + nk], q=q_ap, nk=nk, nq=nq, bias=bt[ti][0:nk, 0:nq],
                               bcol=(hcol if (ti == 0 and hcol is not None) else 0.0), bdeps=[hv.b, ab.b], pv=pv, pvr=[vb], pvw=[ud.b],
                               kdeps=[kTb, qT.b, ab.b], after=(lambda: after(ud)) if ti == 1 else None))
            return tl

        for kv in range(2 if (stage >= 2 and stage not in (30, 31)) else 0):
            R.reset()
            bk = R.take(1152, BF16); bvv = R.take(9 * 128, BF16)
            P.dma("sp", bk.t[:], bkT_d[kv * 128:(kv + 1) * 128, :], writes=[bk.b])
            bv3 = bvv.t[:].rearrange("p (a c) -> p a c", a=9)
            P.dma("sp", bv3, bV_d[:, kv * 128:(kv + 1) * 128].rearrange("(a p) c -> p a c", p=128), writes=[bvv.b])
            for gi in range(4):
                hd = kv * 4 + gi
                wv, wb = wchunk()
                do_q(wv, wb, 0, 2, 0)
                do_z(wv, wb, 128, 0)
                P.dma("sp", ab.t[:, 0:2, :], abias_d[:, (24 + hd * 2) * 128:(26 + hd * 2) * 128].rearrange("p (a c) -> p a c", a=2), writes=[ab.b])
                tls = []
                for qb in range(8):
                    def after(ud, qb=qb, hd=hd):
                        fin_fm(ud.t[:, 0:128], ud.t[:, 128:256], [ud.b], 128, 0, qb * 128, 4 + hd, qb * 128, esk_col=esk.t[:, hd:hd + 1])
                    tls += band_block(bk.t, bk.b, qb * 128, 128, qT.t[:, 0, qb * 128:(qb + 1) * 128], 128, (ab.t[:, 0, :], ab.t[:, 1, :]),
                                      bv3[:, qb, :], bv3[:, qb + 1, :], bvv.b, hv.t[:, 0:1] if qb == 0 else None, after)
                tiles_pipeline(tls)
            P.barrier()
        for h in range(4 if stage >= 3 else 0):
            R.reset()
            ak = [R.take(AW[g] + (64 if g == 2 else 0), BF16) for g in range(3)]
            av0 = R.take(9 * 128, BF16); av1 = R.take(12 * 128, BF16); av2h = R.take(16 * 128, BF16); av2o = R.take(16 * 128, BF16)
            Ua = R.take(T, F32); Sa = R.take(T, F32)
            for g in range(3):
                if g >= 1 and 'k' in _ASK:
                    continue
                if g == 2:
                    P.op("dve", lambda e: e.memset(ak[2].t[:, 3072:3136], 0.0), writes=[ak[2].b])
                P.dma("sp", ak[g].t[:, 0:AW[g]], akT_d[g][h * 128:(h + 1) * 128, :], writes=[ak[g].b])
            v0 = av0.t[:].rearrange("p (a c) -> p a c", a=9)
            v1 = av1.t[:].rearrange("p (a c) -> p a c", a=12)
            v2h = av2h.t[:].rearrange("p (a c) -> p a c", a=16)
            v2o = av2o.t[:].rearrange("p (a c) -> p a c", a=16)
            P.dma("sp", v0, aV_d[0][:, h * 128:(h + 1) * 128].rearrange("(a p) c -> p a c", p=128), writes=[av0.b])
            if 'k' not in _ASK:
                P.dma("sp", v1, aV_d[1][:, h * 128:(h + 1) * 128].rearrange("(a p) c -> p a c", p=128), writes=[av1.b])
            a2 = aV_d[2][:, h * 128:(h + 1) * 128].rearrange("(r w) c -> w r c", w=192)
            if 'v' not in _ASK:
                P.dma("sp", v2h[:, 0:8, :], a2[0:128, 0:8, :], writes=[av2h.b])
                P.dma("sp", v2h[:, 8:16, :], a2[0:128, 8:16, :], writes=[av2h.b])
                P.op("dve", lambda e: e.memset(av2o.t[:], 0.0), writes=[av2o.b])
                P.dma("sp", v2o[0:64, :, :], a2[128:192, :, :], writes=[av2o.b])
            wv, wb = wchunk()
            do_q(wv, wb, 0, 0, 0)
            do_q(wv, wb, 128, 0, 1, 1 if 'q' in _ASK else 4)
            wv, wb = wchunk()
            do_q(wv, wb, 0, 0, 2, 1 if 'q' in _ASK else 16)
            do_z(wv, wb, 128, 0)
            for g in range(3):
                P.dma("sp", ab.t[:, 2 * g:2 * g + 2, :], abias_d[:, ((g * 4 + h) * 2) * 128:((g * 4 + h) * 2 + 2) * 128].rearrange("p (a c) -> p a c", a=2),
                      writes=[ab.b])

            def evac(ud, nq, dst_u, dst_s, first):
                if first:
                    P.op("dve", lambda e: e.tensor_copy(out=dst_u, in_=ud.t[:, 0:nq]), reads=[ud.b], writes=[Ua.b])
                    P.op("dve", lambda e: e.tensor_copy(out=dst_s, in_=ud.t[:, 128:128 + nq]), reads=[ud.b], writes=[Sa.b])
                else:
                    P.op("dve", lambda e: e.tensor_tensor(out=dst_u, in0=dst_u, in1=ud.t[:, 0:nq], op=ALU.add), reads=[ud.b, Ua.b], writes=[Ua.b])
                    P.op("dve", lambda e: e.tensor_tensor(out=dst_s, in0=dst_s, in1=ud.t[:, 128:128 + nq], op=ALU.add), reads=[ud.b, Sa.b], writes=[Sa.b])
            tls = []
            for qb in range(0 if 'g' in _ASK else 8):
                tls += band_block(ak[0].t, ak[0].b, qb * 128, 128, qT.t[:, 0, qb * 128:(qb + 1) * 128], 128, (ab.t[:, 0, :], ab.t[:, 1, :]),
                           v0[:, qb, :], v0[:, qb + 1, :], av0.b, hv.t[:, 0:1] if qb == 0 else None,
                           lambda ud, qb=qb: evac(ud, 128, Ua.t[:, qb * 128:(qb + 1) * 128], Sa.t[:, qb * 128:(qb + 1) * 128], True))
            for r in range(4 if stage != 31 else 0):
                for sbk in range(2):
                    st0 = 4 * sbk * 128 + r
                    tls += band_block(ak[1].t, ak[1].b, r * 384 + sbk * 128, 128, qT.t[:, 1, r * 256 + sbk * 128:r * 256 + sbk * 128 + 128], 128,
                               (ab.t[:, 2, :], ab.t[:, 3, :]), v1[:, r * 3 + sbk, :], v1[:, r * 3 + sbk + 1, :], av1.b,
                               hv.t[:, 0:1] if sbk == 0 else None,
                               lambda ud, st0=st0: evac(ud, 128, Ua.t[:, st0:st0 + 509:4], Sa.t[:, st0:st0 + 509:4], False))
            for r in range(16 if stage != 31 else 0):
                tls += band_block(ak[2].t, ak[2].b, r * 192, 128, qT.t[:, 2, r * 64:(r + 1) * 64], 64, (ab.t[:, 4, :], ab.t[:, 5, :]),
                           v2h[:, r, :], v2o[:, r, :], av2h.b, hv.t[:, 1:2],
                           lambda ud, r=r: evac(ud, 64, Ua.t[:, r:1024:16], Sa.t[:, r:1024:16], False))
            tiles_pipeline(tls)
            for th in range(0 if 'f' in _ASK else 2):
                fin_fm(Ua.t[:, th * 512:(th + 1) * 512], Sa.t[:, th * 512:(th + 1) * 512], [Ua.b, Sa.b, av2o.b], 512, 0, th * 512, h, th * 512)
            P.barrier()
        for h in range(4 if (stage >= 4 and stage not in (30, 31)) else 0):
            R.reset()
            ck = R.take(2 * 4096, BF16); cva = R.take(32 * 257, BF16); cko = R.take(2 * 1024, BF16); cvo = R.take(8 * 257, BF16)
            o0s = R.take(2 * 257, F32); yf = R.take(256, F32); ysq = R.take(256, F32); yn = R.take(256, BF16); sm = R.take(8, F32)
            ck3 = ck.t[:].rearrange("p (a c) -> p a c", a=2); cko3 = cko.t[:].rearrange("p (a c) -> p a c", a=2)
            cva3 = cva.t[:].rearrange("p (a c) -> p a c", a=32); cvo3 = cvo.t[:].rearrange("p (a c) -> p a c", a=8)
            o03 = o0s.t[:].rearrange("p (a c) -> p a c", a=2)
            for c_ in range(2):
                for r_ in range(4):
                    P.dma("sp", ck3[:, c_, r_ * 1024:(r_ + 1) * 1024],
                          kTa_rows(r_, (14 + h * 2 + c_) * 128, 128), writes=[ck.b])
                P.dma("sp", cko3[:, c_, :], kTl_rows((14 + h * 2 + c_) * 128, 128), writes=[cko.b])
            for q4 in range(4):
                P.dma("sp", cva3[:, q4 * 8:(q4 + 1) * 8, 0:256],
                      va_cols(1792 + h * 256, 256)[q4 * 1024:(q4 + 1) * 1024, :].rearrange("(a p) c -> p a c", p=128), writes=[cva.b])
            P.dma("sp", cvo3[:, :, 0:256], vl_cols(1792 + h * 256, 256).rearrange("(a p) c -> p a c", p=128), writes=[cvo.b])
            P.op("dve", lambda e: e.memset(cva3[:, :, 256:257], 1.0), writes=[cva.b])
            P.op("dve", lambda e: e.memset(cvo3[:, :, 256:257], 1.0), writes=[cvo.b])
            wv, wb = wchunk()
            do_q(wv, wb, 0, 4, 0)
            do_q(wv, wb, 128, 4, 1)
            wv, wb = wchunk()
            do_z(wv, wb, 0, 0)
            do_z(wv, wb, 128, 1)
            for g in range(4):
                for c_ in range(2):
                    tl = []
                    nt = 32 + 2 * g + 2
                    for ti in range(nt):
                        own = ti >= 32
                        lk = ti - 32

                        def pv(e, p_ap, ti=ti, own=own, lk=lk, nt=nt):
                            vv = cvo3[:, lk, :] if own else cva3[:, ti, :]
                            e.matmul(kb.ud0.t[:, 0:257], lhsT=p_ap[:, 0:128], rhs=vv, start=(ti == 0), stop=(ti == nt - 1))
                            return e.matmul(kb.ud1.t[:, 0:257], lhsT=p_ap[:, 128:256], rhs=vv, start=(ti == 0), stop=(ti == nt - 1))
                        bias = None
                        if own and lk == 2 * g:
                            bias = cmask.t[:, 0:256]
                        elif own and lk == 2 * g + 1:
                            bias = cmask.t[:, 256:512]
                        kT_ap = cko3[:, c_, lk * 128:(lk + 1) * 128] if own else ck3[:, c_, ti * 128:(ti + 1) * 128]
                        bcol = cbo.t[:, (h * 4 + g) * 8 + lk:(h * 4 + g) * 8 + lk + 1] if own else cb.t[:, (h * 4 + g) * 32 + ti:(h * 4 + g) * 32 + ti + 1]
                        tl.append(dict(kT=kT_ap, q=qT.t[:, c_, g * 256:(g + 1) * 256], nk=128, nq=256, bias=bias, bcol=bcol, bdeps=[cb.b, cbo.b, cmask.b],
                                       pv=pv, pvr=[cva.b, cvo.b], pvw=[kb.ud0.b, kb.ud1.b], kdeps=[ck.b, cko.b, qT.b, cmask.b]))
                    tiles_pipeline(tl)
                    if c_ == 0:
                        P.op("act", lambda e: e.activation(out=o03[:, 0, :], in_=kb.ud0.t[:, 0:257], func=AF.Copy), reads=[kb.ud0.b], writes=[o0s.b])
                        P.op("act", lambda e: e.activation(out=o03[:, 1, :], in_=kb.ud1.t[:, 0:257], func=AF.Copy), reads=[kb.ud1.b], writes=[o0s.b])
                for j in range(2):
                    ud = udl[j]
                    qblk = 2 * g + j
                    P.op("dve", lambda e, j=j: e.reciprocal(out=sm.t[:, 0:1], in_=o03[:, j, 256:257]), reads=[o0s.b], writes=[sm.b])
                    P.op("dve", lambda e, ud=ud: e.reciprocal(out=sm.t[:, 1:2], in_=ud.t[:, 256:257]), reads=[ud.b], writes=[sm.b])
                    P.op("dve", lambda e: e.tensor_tensor(out=sm.t[:, 1:2], in0=sm.t[:, 1:2], in1=nlam.t[:, 0:1], op=ALU.mult), reads=[sm.b, nlam.b], writes=[sm.b])
                    P.op("dve", lambda e, ud=ud: e.tensor_scalar(out=ysq.t[:], in0=ud.t[:, 0:256], scalar1=sm.t[:, 1:2], scalar2=None, op0=ALU.mult),
                         reads=[ud.b, sm.b], writes=[ysq.b])
                    P.op("dve", lambda e, j=j: e.scalar_tensor_tensor(out=yf.t[:], in0=o03[:, j, 0:256], scalar=sm.t[:, 0:1], in1=ysq.t[:],
                                                                      op0=ALU.mult, op1=ALU.add), reads=[o0s.b, sm.b, ysq.b], writes=[yf.b])
                    P.op("act", lambda e: e.activation(out=ysq.t[:], in_=yf.t[:], func=AF.Square), reads=[yf.b], writes=[ysq.b])
                    P.op("dve", lambda e: e.reduce_sum(out=sm.t[:, 2:3], in_=ysq.t[:], axis=AX.X), reads=[ysq.b], writes=[sm.b])
                    P.op("act", lambda e: e.activation(out=sm.t[:, 3:4], in_=sm.t[:, 2:3], func=AF.Ln, bias=kb.epsc.t[:, 0:1], scale=1.0 / 256),
                         reads=[sm.b, kb.epsc.b], writes=[sm.b])
                    P.op("act", lambda e: e.activation(out=sm.t[:, 3:4], in_=sm.t[:, 3:4], func=AF.Exp, scale=-0.5), reads=[sm.b], writes=[sm.b])
                    P.op("dve", lambda e: e.scalar_tensor_tensor(out=yn.t[:], in0=yf.t[:], scalar=sm.t[:, 3:4], in1=subg.t[:], op0=ALU.mult, op1=ALU.mult),
                         reads=[yf.b, sm.b, subg.b], writes=[yn.b])

                    def tr(e):
                        e.transpose(out=kb.ptr.t[:, 0:128], in_=yn.t[:, 0:128], identity=kb.ident)
                        return e.transpose(out=kb.ptr.t[:, 128:256], in_=yn.t[:, 128:256], identity=kb.ident)
                    P.op("pe", tr, reads=[yn.b, kb.cst.b], writes=[kb.ptr.b])
                    for e2 in range(2):
                        if debug and l == nl - 1:
                            P.op("dve", lambda e, e2=e2: e.tensor_copy(out=tf[2].t[:, 0:128], in_=kb.ptr.t[:, e2 * 128:(e2 + 1) * 128]), reads=[kb.ptr.b], writes=[tf[2].b])
                            dbg_y(tf[2].t[:, 0:128], tf[2].b, (12 + 2 * h + e2) * 128, qblk * 128, 128)
                        P.op("dve", lambda e, e2=e2, qblk=qblk, h=h: e.tensor_tensor(out=uT.t[:, 12 + 2 * h + e2, qblk * 128:(qblk + 1) * 128],
                                                                                in0=kb.ptr.t[:, e2 * 128:(e2 + 1) * 128],
                                                                                in1=zs.t[:, e2, qblk * 128:(qblk + 1) * 128], op=ALU.mult),
                             reads=[kb.ptr.b, zs.b], writes=[uT.b])
            P.barrier()
        R.reset()
        mT = R.take(KC * T, BF16)
        mT3 = TB.__new__(TB); mT3.t = mT.t[:].rearrange("p (k t) -> p k t", k=KC); mT3.b = mT.b
        bch = [(0, 4), (4, 12), (12, 20), (20, 24)]
        pb = [kb.s0, kb.s1, kb.ud0, kb.ud1]
        wcol[0] = 15360 - 8192
        for oc in range(16 if (stage >= 5 and stage not in (30, 31)) else 0):
            r = next_ring(kb)
            wbr = r.t[:, 0:24 * 128].rearrange("p (k c) -> p k c", k=24)
            for q2 in range(2):
                P.dma("pool", wbr[:, q2 * 12:(q2 + 1) * 12, :],
                      wbr_d[q2 * 1536:(q2 + 1) * 1536, oc * 128:(oc + 1) * 128].rearrange("(k p) c -> p k c", p=128), writes=[r.b])
            for th in range(2):
                for b_ in range(4):
                    P.dma("sp", gsb[b_].t[:], gates_d[(oc * 4 + b_) * 128:(oc * 4 + b_ + 1) * 128, th * 512:(th + 1) * 512],
                          reads=[Bgates], writes=[gsb[b_].b])
                for b_ in range(4):
                    def f(e, b_=b_, th=th, wbr=wbr):
                        last = None
                        lo, hi = bch[b_]
                        for k in range(lo, hi):
                            last = e.matmul(pb[b_].t[:, 0:512], lhsT=wbr[:, k, :], rhs=uT.t[:, k, th * 512:(th + 1) * 512], start=(k == lo), stop=(k == hi - 1))
                        return last
                    P.op("pe", f, reads=[r.b, uT.b], writes=[pb[b_].b])
                P.op("dve", lambda e: e.tensor_tensor(out=tf[0].t[:], in0=pb[0].t[:, 0:512], in1=gsb[0].t[:], op=ALU.mult), reads=[pb[0].b, gsb[0].b], writes=[tf[0].b])
                for b_ in range(1, 4):
                    P.op("dve", lambda e, b_=b_: e.tensor_tensor(out=tf[1].t[:], in0=pb[b_].t[:, 0:512], in1=gsb[b_].t[:], op=ALU.mult),
                         reads=[pb[b_].b, gsb[b_].b], writes=[tf[1].b])
                    if b_ < 3:
                        P.op("dve", lambda e: e.tensor_tensor(out=tf[0].t[:], in0=tf[0].t[:], in1=tf[1].t[:], op=ALU.add), reads=[tf[0].b, tf[1].b], writes=[tf[0].b])
                    else:
                        P.op("dve", lambda e, oc=oc, th=th: e.tensor_tensor(out=mT3.t[:, oc, th * 512:(th + 1) * 512], in0=tf[0].t[:], in1=tf[1].t[:], op=ALU.add),
                             reads=[tf[0].b, tf[1].b], writes=[mT.b])
        if debug and l == nl - 1:
            finals.append(P.dma("sp", md_d.rearrange("(k p) t -> p k t", p=128), mT3.t[:], reads=[mT.b]))
        i_ = 0
        for cc in range(8):
            wv, wb = load_w(kb, wout_d, cc * 256, 256)
            for t in range(8):
                pp = next_pp(kb)
                proj_tm(kb, wv, wb, 0, 256, mT3, t * 128, pp)
                x1, o1 = xs[i_ % 2], ot[i_ % 2]
                i_ += 1
                P.dma("sp", x1.t[:], x_src[t * 128:(t + 1) * 128, cc * 256:(cc + 1) * 256], reads=list(xdep), writes=[x1.b])
                P.op("dve", lambda e, pp=pp, x1=x1, o1=o1: e.tensor_tensor(out=o1.t[:], in0=pp.t[:, 0:256], in1=x1.t[:], op=ALU.add),
                     reads=[pp.b, x1.b], writes=[o1.b])
                fo = P.dma("sp", x_dst[t * 128:(t + 1) * 128, cc * 256:(cc + 1) * 256], o1.t[:], reads=[o1.b], writes=list(xdst_b))
                if l == nl - 1:
                    finals.append(fo)
    P.emit(kb.nc, finals)
    kb.st.close()
    return kb.nc


_BF = ml_dtypes.bfloat16
_KCOLS = list(range(1536, 3072)) + list(range(5632, 5888)) + list(range(7168, 8192))
_VCOLS = list(range(3072, 4608)) + list(range(5888, 6144)) + list(range(8192, 9216))


def _main_cols():
    cols = []
    blk = lambda s: list(range(s, s + 128))
    for hh in range(4):
        cols += blk(9216 + hh * 128) + blk(12288 + hh * 128)
    for hd in range(8):
        cols += blk(4608 + hd * 128) + blk(10240 + hd * 128)
    for h in range(4):
        cols += blk(h * 128) + blk(512 + h * 128) + blk(1024 + h * 128) + blk(9728 + h * 128)
    for h in range(4):
        cols += blk(6144 + h * 256) + blk(6144 + h * 256 + 128) + blk(11264 + h * 256) + blk(11264 + h * 256 + 128)
    for oc in range(16):
        for b in range(4):
            cols += blk(12800 + b * 2048 + oc * 128)
    return cols


def _tables():
    consts = np.concatenate([np.eye(128), np.ones((128, 128)), np.full((128, 128), 1 / 128)], 1).astype(_BF)
    w = np.arange(128)[:, None].astype(np.float64)
    qi = np.arange(128)[None, :].astype(np.float64)
    ab = np.zeros((128, 40, 128), np.float32)
    for g in range(3):
        for h in range(4):
            sl = 2.0 ** (-2.0 * (h + 1)) * A_DD[g]
            ab[:, (g * 4 + h) * 2 + 0, :] = np.where(w >= qi, -sl * (qi + 128 - w), NEG)
            ab[:, (g * 4 + h) * 2 + 1, :] = np.where(w <= qi, -sl * (qi - w), NEG)
    for hd in range(8):
        sl = 2.0 ** (-(hd + 1.0))
        ab[:, 24 + hd * 2 + 0, :] = np.where(w >= qi + 1, -sl * (qi + 128 - w), NEG)
        ab[:, 24 + hd * 2 + 1, :] = np.where(w <= qi, -sl * (qi - w), NEG)
    abias = ab.reshape(128, 40 * 128).astype(_BF)
    tri = np.where(w <= qi, 0.0, NEG)
    cmask = np.concatenate([tri, np.zeros((128, 128)), np.full((128, 128), NEG), tri], 1).astype(_BF)
    jp = np.arange(128).astype(np.float64)
    cbo = np.zeros((128, 128), np.float32)
    cbs = []
    for j in range(4):
        cb = np.zeros((128, 512), np.float32)
        for h in range(4):
            sl = 2.0 ** (-2.0 * (h + 1))
            for g in range(4):
                for kb_ in range(32):
                    if kb_ < 8 * j:
                        cb[:, (h * 4 + g) * 32 + kb_] = sl * ((kb_ * 128 + jp) - (j * 1024 + g * 256 + 255))
                    else:
                        cb[:, (h * 4 + g) * 32 + kb_] = NEG
                for lk in range(8):
                    cbo[:, (h * 4 + g) * 8 + lk] = sl * ((lk * 128 + jp) - (g * 256 + 255))
        cbs.append(cb)
    hvs = []
    for j in range(4):
        hv = np.zeros((128, 2), np.float32)
        hv[:, 0] = 0.0 if j >= 1 else NEG
        hv[:64, 1] = 0.0 if j >= 2 else NEG
        hv[64:, 1] = 0.0 if j >= 1 else NEG
        hvs.append(hv)
    return consts, abias, cmask, cbs, cbo, hvs


def _exchange(kTs, vs):
    outs = []
    for b in range(2):
        kc = [np.asarray(kTs[4 * b + j]) for j in range(4)]
        v_all = np.concatenate([np.asarray(vs[4 * b + j]) for j in range(4)], 0)
        ckT = np.concatenate([k[14 * 128:22 * 128] for k in kc], 1)
        cV = np.ascontiguousarray(v_all[:, 1792:2816])
        for j in range(4):
            m = {}
            for g in range(3):
                dd = A_DD[g]
                L = 1024 // dd
                rows = slice(g * 512, (g + 1) * 512)
                KG = np.concatenate([k[rows].reshape(512, dd, L) for k in kc], 2)
                lo = j * L - 128
                if lo < 0:
                    seg = np.concatenate([np.zeros((512, dd, -lo), KG.dtype), KG[:, :, 0:(j + 1) * L]], 2)
                else:
                    seg = KG[:, :, lo:(j + 1) * L]
                m["akT%d" % g] = np.ascontiguousarray(seg.reshape(512, dd * (128 + L)))
                VG = v_all[:, g * 512:(g + 1) * 512].reshape(4096 // dd, dd, 512).transpose(1, 0, 2)
                if lo < 0:
                    segv = np.concatenate([np.zeros((dd, -lo, 512), VG.dtype), VG[:, 0:(j + 1) * L]], 1)
                else:
                    segv = VG[:, lo:(j + 1) * L]
                m["aV%d" % g] = np.ascontiguousarray(segv.reshape(dd * (128 + L), 512))
            KB_ = np.concatenate([k[12 * 128:14 * 128] for k in kc], 1)
            VB_ = v_all[:, 1536:1792]
            lo = j * 1024 - 128
            if lo < 0:
                m["bkT"] = np.ascontiguousarray(np.concatenate([np.zeros((256, 128), KB_.dtype), KB_[:, 0:1024]], 1))
                m["bV"] = np.ascontiguousarray(np.concatenate([np.zeros((128, 256), VB_.dtype), VB_[0:1024]], 0))
            else:
                m["bkT"] = np.ascontiguousarray(KB_[:, lo:lo + 1152])
                m["bV"] = np.ascontiguousarray(VB_[lo:lo + 1152])
            m["ckT"] = ckT
            m["cV"] = cV
            m["ckTo"] = np.ascontiguousarray(kc[j][14 * 128:22 * 128])
            m["cVo"] = np.ascontiguousarray(v_all[j * 1024:(j + 1) * 1024, 1792:2816])
            outs.append(m)
    return outs


_CACHE = {}


def run_layer(l, xs, P, debug=False):
    consts, abias, cmask, cbs, cbo, hvs = _tables()
    w_in = P["w_in"][l]
    wkv = np.ascontiguousarray(w_in[:, _KCOLS + _VCOLS])
    gk = np.ascontiguousarray(P["qk_gain"][l][[1, 3, 5]].T)
    ng = np.ascontiguousarray(P["norm_g"][l][None, :])
    if "kv" not in _CACHE:
        _CACHE["kv"] = build_kv()
    res = run_bass_kernel_spmd(_CACHE["kv"], [{"x": xs[c], "ng": ng, "wkv": wkv, "gk": gk, "consts": consts} for c in range(8)],
                               core_ids=list(range(8)))
    ex = _exchange([r["kT"] for r in res.results], [r["v"] for r in res.results])
    lam_init = 0.8 - 0.6 * math.exp(-0.3 * l)
    key = ("main", l, debug)
    if key not in _CACHE:
        _CACHE[key] = build_main(lam_init, debug)
    wm = np.ascontiguousarray(w_in[:, _main_cols()])
    bgm = np.ascontiguousarray(P["b_gate"][l].reshape(4, 16, 128).transpose(2, 1, 0).reshape(128, 64))
    common = {"ng": ng, "w": wm, "bg": bgm, "gq": np.ascontiguousarray(P["qk_gain"][l].T),
              "sk": np.ascontiguousarray(np.broadcast_to(P["sinks"][l][None, :], (128, 8))),
              "lam": np.ascontiguousarray(np.broadcast_to(P["lam"][l].reshape(1, 512), (128, 512))),
              "subg": np.ascontiguousarray(np.broadcast_to(P["subln_g"][l][None, :], (128, 256))),
              "mng": np.ascontiguousarray(P["mem_norm_g"][l][None, :]), "wmem": P["w_mem_kv"][l], "wbr": P["w_branch"][l],
              "wout": P["w_out"][l], "abias": abias, "cmask": cmask, "cbo": cbo, "consts": consts}
    in_maps = []
    for c in range(8):
        m = dict(common)
        m.update(ex[c])
        m["x"] = xs[c]
        m["mem"] = np.ascontiguousarray(P["mem"][c // 4])
        m["cb"] = cbs[c % 4]
        m["hv"] = hvs[c % 4]
        in_maps.append(m)
    res = run_bass_kernel_spmd(_CACHE[key], in_maps, core_ids=list(range(8)))
    if debug:
        return [r["xo"] for r in res.results], [r["ydbg"] for r in res.results], [r["mdbg"] for r in res.results]
    return [r["xo"] for r in res.results]


def _sel_tables():
    sels = []
    for j in range(4):
        sl = np.zeros((128, 12), np.float32)
        for r in range(4):
            sl[:, r] = 1.0 if r == j - 1 else 0.0
            sl[:64, 4 + r] = 1.0 if r == j - 2 else 0.0
            sl[64:, 4 + r] = 1.0 if r == j - 1 else 0.0
            sl[:, 8 + r] = 1.0 if r == j - 2 else 0.0
        sels.append(sl)
    return sels


def run_fused(nl, xs, P, debug=False, l0=0):
    consts, abias, cmask, cbs, cbo, hvs = _tables()
    sels = _sel_tables()
    Ls = list(range(l0, l0 + nl))
    lam_inits = [0.8 - 0.6 * math.exp(-0.3 * l) for l in Ls]
    key = ("fused", tuple(Ls), debug)
    if key not in _CACHE:
        _CACHE[key] = build_fused(lam_inits, debug)
    mc = _main_cols()
    cat = lambda f: np.ascontiguousarray(np.concatenate([f(l) for l in Ls], 0))
    common = {
        "ng": cat(lambda l: P["norm_g"][l][None, :]), "mng": cat(lambda l: P["mem_norm_g"][l][None, :]),
        "wkv": cat(lambda l: P["w_in"][l][:, _KCOLS + _VCOLS]), "w": cat(lambda l: P["w_in"][l][:, mc]),
        "wmem": cat(lambda l: P["w_mem_kv"][l]), "wbr": cat(lambda l: P["w_branch"][l]), "wout": cat(lambda l: P["w_out"][l]),
        "bg": cat(lambda l: P["b_gate"][l].reshape(4, 16, 128).transpose(2, 1, 0).reshape(128, 64)),
        "gq": cat(lambda l: P["qk_gain"][l].T), "gk": cat(lambda l: P["qk_gain"][l][[1, 3, 5]].T),
        "sk": cat(lambda l: np.broadcast_to(P["sinks"][l][None, :], (128, 8))),
        "lam": cat(lambda l: np.broadcast_to(P["lam"][l].reshape(1, 512), (128, 512))),
        "subg": cat(lambda l: np.broadcast_to(P["subln_g"][l][None, :], (128, 256))),
        "abias": abias, "cmask": cmask, "cbo": cbo, "consts": consts}
    in_maps = []
    for c in range(8):
        m = dict(common)
        m["x"] = xs[c]
        m["mem"] = np.ascontiguousarray(P["mem"][c // 4])
        m["cb"] = cbs[c % 4]
        m["hv"] = hvs[c % 4]
        m["sel"] = sels[c % 4]
        in_maps.append(m)
    res = run_bass_kernel_spmd(_CACHE[key], in_maps, core_ids=list(range(8)))
    if debug:
        return [r["xo"] for r in res.results], [r["ydbg"] for r in res.results], [r["mdbg"] for r in res.results]
    return [r["xo"] for r in res.results]


def kernel(x, mem, norm_g, w_in, b_gate, qk_gain, sinks, lam, subln_g, mem_norm_g, w_mem_kv, w_branch, w_out):
    P = dict(mem=np.asarray(mem), norm_g=np.asarray(norm_g), w_in=np.asarray(w_in), b_gate=np.asarray(b_gate),
             qk_gain=np.asarray(qk_gain), sinks=np.asarray(sinks), lam=np.asarray(lam), subln_g=np.asarray(subln_g),
             mem_norm_g=np.asarray(mem_norm_g), w_mem_kv=np.asarray(w_mem_kv), w_branch=np.asarray(w_branch), w_out=np.asarray(w_out))
    x = np.asarray(x)
    xs = [np.ascontiguousarray(x[c // 4, (c % 4) * 1024:(c % 4 + 1) * 1024]) for c in range(8)]
    xs = run_fused(4, xs, P)
    out = np.zeros_like(x)
    for c in range(8):
        out[c // 4, (c % 4) * 1024:(c % 4 + 1) * 1024] = xs[c]
    return out
```

```python
import contextlib
import math
import os
_ASK = os.environ.get('ASKIP', '')
import numpy as np
import ml_dtypes
import concourse.bass as bass
import concourse.mybir as mybir
from concourse.bass_utils import run_bass_kernel_spmd

F32 = mybir.dt.float32
BF16 = mybir.dt.bfloat16
AF = mybir.ActivationFunctionType
ALU = mybir.AluOpType
AX = mybir.AxisListType
EPS = 1e-6
NEG = -30000.0
D = 2048
T = 1024
KC = 16
NCORE = 8


class Buf:
    __slots__ = ("last_w", "readers", "dw")

    def __init__(self):
        self.last_w = None
        self.readers = []
        self.dw = []


class Op:
    __slots__ = ("eng", "fn", "deps", "signal", "seq", "idx", "dma", "sem", "val", "inc", "ring", "epoch")


class Prog:
    def __init__(self):
        self.ops = []
        self.ring = {"sp": 16, "pool": 8, "act": 2}
        self.fence = []
        self.since = []
        self.last = {}
        self.epoch = 0

    def op(self, eng, fn, reads=(), writes=(), dma=False, inc=16, ring=None):
        o = Op()
        o.ring = ring or eng
        o.epoch = self.epoch
        o.eng, o.fn, o.idx, o.signal, o.dma, o.seq, o.sem, o.val, o.inc = eng, fn, len(self.ops), False, dma, None, None, None, inc
        deps = {}
        for b in reads:
            if b.last_w is not None:
                deps[b.last_w.idx] = b.last_w
            for w in b.dw:
                deps[w.idx] = w
        for b in writes:
            if b.last_w is not None:
                deps[b.last_w.idx] = b.last_w
            if not dma:
                for w in b.dw:
                    deps[w.idx] = w
            for r in b.readers:
                deps[r.idx] = r
        for f in self.fence:
            deps[f.idx] = f
        o.deps = list(deps.values())
        for b in reads:
            b.readers.append(o)
        for b in writes:
            if dma:
                b.dw.append(o)
            else:
                b.last_w = o
                b.dw = []
            b.readers = []
        self.ops.append(o)
        if dma:
            self.since.append(o)
        else:
            self.last[eng] = o
        return o

    def barrier(self):
        self.fence = list(self.last.values()) + list(self.since)
        self.since = []

    def dma(self, eng, out, in_, reads=(), writes=()):
        return self.op(eng, lambda e: e.dma_start(out=out, in_=in_), reads, writes, dma=True)

    def emit(self, nc, final_ops=()):
        ops = self.ops
        engs = ["pe", "act", "dve", "pool", "sp"]
        for o in ops:
            for d in o.deps:
                if d.dma:
                    continue
                if d.eng == "pe" and o.eng == "pe" and not o.dma:
                    continue
                d.signal = True
        seqc = {}
        for o in ops:
            if not o.dma and o.signal:
                k = (o.eng, o.epoch)
                seqc[k] = seqc.get(k, 0) + 1
                o.seq = seqc[k]
        stack = contextlib.ExitStack()
        prog_sem = {k: stack.enter_context(nc.semaphore("prg_%s%d" % k)) for k in seqc}
        ring_sems = {q: [stack.enter_context(nc.semaphore("rg_%s%d" % (q, i))) for i in range(n)]
                     for q, n in self.ring.items()}
        ring_cnt = {q: [0] * n for q, n in self.ring.items()}
        ring_last = {q: [None] * n for q, n in self.ring.items()}
        ring_pos = {q: 0 for q in self.ring}
        for o in ops:
            if o.dma:
                q = o.ring
                i = ring_pos[q]
                ring_pos[q] = (i + 1) % self.ring[q]
                prev = ring_last[q][i]
                if prev is not None:
                    o.deps.append(prev)
                ring_cnt[q][i] += o.inc
                o.sem = ring_sems[q][i]
                o.val = ring_cnt[q][i]
                ring_last[q][i] = o
            elif o.signal:
                o.sem = prog_sem[(o.eng, o.epoch)]
                o.val = o.seq
        per_eng = {e: [o for o in ops if o.eng == e] for e in engs}
        block = stack.enter_context(nc.Block())

        def run(eng_name, handle):
            waited = {}
            for o in per_eng[eng_name]:
                need = {}
                for d in o.deps:
                    if (not d.dma) and d.eng == "pe" and eng_name == "pe" and not o.dma:
                        continue
                    k = id(d.sem)
                    if k not in need or need[k][1] < d.val:
                        need[k] = (d.sem, d.val)
                for k, (s, v) in need.items():
                    if waited.get(k, 0) >= v:
                        continue
                    handle.wait_ge(s, v)
                    waited[k] = v
                ins = o.fn(handle)
                if o.dma:
                    ins.then_inc(o.sem, o.inc)
                elif o.signal:
                    ins.then_inc(o.sem, 1)
            if eng_name == "sp":
                for o in final_ops:
                    handle.wait_ge(o.sem, o.val)

        block.tensor(lambda e: run("pe", e))
        block.scalar(lambda e: run("act", e))
        block.vector(lambda e: run("dve", e))
        block.gpsimd(lambda e: run("pool", e))
        block.sync(lambda e: run("sp", e))
        stack.close()


class TB:
    def __init__(self, t):
        self.t = t
        self.b = Buf()


class KB:
    def __init__(self):
        self.nc = bass.Bass("TRN2", target_bir_lowering=False)
        self.P = Prog()
        self.st = contextlib.ExitStack()
        self.n = 0

    def sb(self, shape, dt):
        self.n += 1
        return TB(self.st.enter_context(self.nc.sbuf_tensor("sb%d" % self.n, list(shape), dt)))

    def ps(self, shape, dt=F32):
        self.n += 1
        return TB(self.st.enter_context(self.nc.psum_tensor("ps%d" % self.n, list(shape), dt)))

    def din(self, name, shape, dt=F32):
        return self.nc.dram_tensor(name, list(shape), dt, kind="ExternalInput").ap()

    def dout(self, name, shape, dt=F32):
        return self.nc.dram_tensor(name, list(shape), dt, kind="ExternalOutput").ap()


def common_setup(kb, consts_d):
    P = kb.P
    c = kb.sb([128, 384], BF16)
    P.dma("sp", c.t[:], consts_d, writes=[c.b])
    kb.cst = c
    kb.ident = c.t[:, 0:128]
    kb.ones = c.t[:, 128:256]
    kb.ones128 = c.t[:, 256:384]
    kb.p0 = kb.ps([128, 512])
    kb.p1 = kb.ps([128, 512])
    kb.pss = kb.ps([128, 512])
    kb.s0 = kb.ps([128, 512])
    kb.s1 = kb.ps([128, 512])
    kb.ud0 = kb.ps([128, 512])
    kb.ud1 = kb.ps([128, 512])
    kb.ptr = kb.ps([128, 1024], BF16)
    kb.ring = [kb.sb([128, 8192], BF16) for _ in range(2)]
    kb.ring_i = 0
    kb.sq = [kb.sb([128, 512], BF16) for _ in range(2)]
    kb.rstd = [kb.sb([128, 512], F32) for _ in range(2)]
    kb.qn_i = 0
    kb.pp_i = 0
    kb.epsc = kb.sb([128, 1], F32)
    P.op("dve", lambda e: e.memset(kb.epsc.t[:], EPS), writes=[kb.epsc.b])


def next_ring(kb):
    r = kb.ring[kb.ring_i % len(kb.ring)]
    kb.ring_i += 1
    return r


def next_pp(kb):
    lst = getattr(kb, "pp_list", None) or (kb.p0, kb.p1)
    p = lst[kb.pp_i % len(lst)]
    kb.pp_i += 1
    return p


def load_w(kb, w_d, c0, nc_, nk=KC):
    r = next_ring(kb)
    view = r.t[:, 0:nk * nc_].rearrange("p (k c) -> p k c", k=nk)
    src = w_d[:, c0:c0 + nc_].rearrange("(k p) c -> p k c", p=128)
    kb.P.dma("pool", view, src, writes=[r.b])
    return view, r.b


def norm_to_hT(kb, x_d, ngb, hT, ntile, scr, xdep=()):
    P = kb.P
    xt, sqf, hb, ss, rs = scr
    for t in range(ntile):
        xs = xt[t % 2]
        P.dma("sp", xs.t[:], x_d[t * 128:(t + 1) * 128, :], reads=list(xdep), writes=[xs.b])
        P.op("act", lambda e, xs=xs: e.activation(out=sqf.t[:], in_=xs.t[:], func=AF.Square), reads=[xs.b], writes=[sqf.b])
        P.op("dve", lambda e: e.reduce_sum(out=ss.t[:], in_=sqf.t[:], axis=AX.X), reads=[sqf.b], writes=[ss.b])
        P.op("act", lambda e: e.activation(out=rs.t[:], in_=ss.t[:], func=AF.Ln, bias=kb.epsc.t[:, 0:1], scale=1.0 / D),
             reads=[ss.b, kb.epsc.b], writes=[rs.b])
        P.op("act", lambda e: e.activation(out=rs.t[:], in_=rs.t[:], func=AF.Exp, scale=-0.5), reads=[rs.b], writes=[rs.b])
        P.op("dve", lambda e, xs=xs: e.scalar_tensor_tensor(out=hb.t[:], in0=xs.t[:], scalar=rs.t[:, 0:1], in1=ngb.t[:],
                                                            op0=ALU.mult, op1=ALU.mult),
             reads=[xs.b, rs.b, ngb.b], writes=[hb.b])
        for k0 in range(0, KC, 4):
            def tr(e, k0=k0):
                last = None
                for i in range(4):
                    last = e.transpose(out=kb.ptr.t[:, i * 128:(i + 1) * 128], in_=hb.t[:, (k0 + i) * 128:(k0 + i + 1) * 128],
                                       identity=kb.ident)
                return last
            P.op("pe", tr, reads=[hb.b, kb.cst.b], writes=[kb.ptr.b])
            src = kb.ptr.t[:, 0:512].rearrange("p (a b) -> p a b", a=4)
            dst = hT.t[:, k0:k0 + 4, t * 128:(t + 1) * 128]
            if (k0 // 4) % 2 == 0:
                P.op("act", lambda e, src=src, dst=dst: e.activation(out=dst, in_=src, func=AF.Copy), reads=[kb.ptr.b], writes=[hT.b])
            else:
                P.op("dve", lambda e, src=src, dst=dst: e.tensor_copy(out=dst, in_=src), reads=[kb.ptr.b], writes=[hT.b])


def proj_fm(kb, wv, wb, col, ncol, hT, tok0, ntok, pp):
    def f(e):
        last = None
        for k in range(KC):
            last = e.matmul(pp.t[0:ncol, 0:ntok], lhsT=wv[:, k, col:col + ncol], rhs=hT.t[:, k, tok0:tok0 + ntok],
                            start=(k == 0), stop=(k == KC - 1))
        return last
    kb.P.op("pe", f, reads=[wb, hT.b], writes=[pp.b])


def proj_tm(kb, wv, wb, col, ncol, hT, tok0, pp):
    def f(e):
        last = None
        for k in range(KC):
            last = e.matmul(pp.t[:, 0:ncol], lhsT=hT.t[:, k, tok0:tok0 + 128], rhs=wv[:, k, col:col + ncol],
                            start=(k == 0), stop=(k == KC - 1))
        return last
    kb.P.op("pe", f, reads=[wb, hT.b], writes=[pp.b])


def qknorm(kb, pp, ntok, gcol, gb, dst, dstb, dd=1):
    P = kb.P
    i = kb.qn_i % 2
    kb.qn_i += 1
    sq, rstd = kb.sq[i], kb.rstd[i]
    P.op("act", lambda e: e.activation(out=sq.t[:, 0:ntok], in_=pp.t[:, 0:ntok], func=AF.Square), reads=[pp.b], writes=[sq.b])
    P.op("pe", lambda e: e.matmul(kb.pss.t[:, 0:ntok], lhsT=kb.ones128, rhs=sq.t[:, 0:ntok], start=True, stop=True),
         reads=[sq.b, kb.cst.b], writes=[kb.pss.b])
    P.op("act", lambda e: e.activation(out=rstd.t[:, 0:ntok], in_=kb.pss.t[:, 0:ntok], func=AF.Ln, bias=kb.epsc.t[:, 0:1]),
         reads=[kb.pss.b, kb.epsc.b], writes=[rstd.b])
    P.op("act", lambda e: e.activation(out=rstd.t[:, 0:ntok], in_=rstd.t[:, 0:ntok], func=AF.Exp, scale=-0.5),
         reads=[rstd.b], writes=[rstd.b])
    a = pp.t[:, 0:ntok]
    b = rstd.t[:, 0:ntok]
    if dd > 1:
        a = a.rearrange("p (i r) -> p i r", r=dd)
        b = b.rearrange("p (i r) -> p i r", r=dd)
    P.op("dve", lambda e: e.scalar_tensor_tensor(out=dst, in0=a, scalar=gcol, in1=b, op0=ALU.mult, op1=ALU.mult),
         reads=[pp.b, rstd.b, gb], writes=[dstb])


A_DD = (1, 4, 16)


def build_kv():
    kb = KB()
    P = kb.P
    x_d = kb.din("x", [T, D])
    ng_d = kb.din("ng", [1, D])
    w_d = kb.din("wkv", [D, 5632])
    gk_d = kb.din("gk", [128, 3])
    c_d = kb.din("consts", [128, 384], BF16)
    kT_d = kb.dout("kT", [22 * 128, T], BF16)
    v_d = kb.dout("v", [T, 2816], BF16)
    common_setup(kb, c_d)
    hT = kb.sb([128, KC, T], BF16)
    ngb = kb.sb([128, D], F32)
    gk = kb.sb([128, 3], F32)
    P.dma("sp", ngb.t[:], ng_d[0, :].partition_broadcast(128), writes=[ngb.b])
    P.dma("sp", gk.t[:], gk_d, writes=[gk.b])
    scr = ([kb.sb([128, D], F32) for _ in range(2)], kb.sb([128, D], F32), kb.sb([128, D], BF16),
           kb.sb([128, 1], F32), kb.sb([128, 1], F32))
    norm_to_hT(kb, x_d, ngb, hT, 8, scr)
    finals = []
    kst = [kb.sb([128, T], BF16) for _ in range(2)]
    for c0 in range(0, 2816, 512):
        ncol = min(512, 2816 - c0)
        wv, wb = load_w(kb, w_d, c0, ncol)
        for hh in range(ncol // 128):
            hd = c0 // 128 + hh
            if hd < 12:
                dd, gi = A_DD[hd // 4], 0
            elif hd < 14:
                dd, gi = 1, 1
            else:
                dd, gi = 1, 2
            ks = kst[hd % 2]
            for th in range(2):
                pp = next_pp(kb)
                proj_fm(kb, wv, wb, hh * 128, 128, hT, th * 512, 512, pp)
                if dd == 1:
                    dst = ks.t[:, th * 512:(th + 1) * 512]
                else:
                    n = 512 // dd
                    dst = ks.t[:].rearrange("p (r i) -> p i r", r=dd)[:, th * n:(th + 1) * n, :]
                qknorm(kb, pp, 512, gk.t[:, gi:gi + 1], gk.b, dst, ks.b, dd)
            finals.append(P.dma("sp", kT_d[hd * 128:(hd + 1) * 128, :], ks.t[:], reads=[ks.b]))
    vst = [kb.sb([128, 8, 512], BF16) for _ in range(2)]
    ci = 0
    for c0 in range(0, 2816, 512):
        ncol = min(512, 2816 - c0)
        wv, wb = load_w(kb, w_d, 2816 + c0, ncol)
        vs = vst[ci % 2]
        ci += 1
        for t in range(8):
            pp = next_pp(kb)
            proj_tm(kb, wv, wb, 0, ncol, hT, t * 128, pp)
            if t % 2 == 0:
                P.op("act", lambda e, pp=pp, t=t, vs=vs, ncol=ncol: e.activation(out=vs.t[:, t, 0:ncol], in_=pp.t[:, 0:ncol], func=AF.Copy),
                     reads=[pp.b], writes=[vs.b])
            else:
                P.op("dve", lambda e, pp=pp, t=t, vs=vs, ncol=ncol: e.tensor_copy(out=vs.t[:, t, 0:ncol], in_=pp.t[:, 0:ncol]),
                     reads=[pp.b], writes=[vs.b])
        finals.append(P.dma("sp", v_d[:, c0:c0 + ncol].rearrange("(t p) c -> p t c", p=128), vs.t[:, :, 0:ncol], reads=[vs.b]))
    P.emit(kb.nc, finals)
    kb.st.close()
    return kb.nc


class Region:
    def __init__(self, kb, nf32):
        self.tb = kb.sb([128, nf32], F32)
        self.n = nf32
        self.off = 0

    def reset(self):
        self.off = 0

    def take(self, nelem, dt):
        w = nelem if dt == F32 else (nelem + 1) // 2
        a = self.tb.t[:, self.off:self.off + w]
        self.off += w
        assert self.off <= self.n, (self.off, self.n)
        o = TB.__new__(TB)
        o.t = a if dt == F32 else a.bitcast(BF16)
        o.b = Buf()
        return o


def build_main(lam_init, debug=False, stage=9):
    kb = KB()
    P = kb.P
    x_d = kb.din("x", [T, D]); ng_d = kb.din("ng", [1, D]); w_d = kb.din("w", [D, 15360])
    bg_d = kb.din("bg", [128, 64]); gq_d = kb.din("gq", [128, 8]); sk_d = kb.din("sk", [128, 8])
    lam_d = kb.din("lam", [128, 512]); subg_d = kb.din("subg", [128, 256])
    mem_d = kb.din("mem", [256, D]); mng_d = kb.din("mng", [1, D]); wmem_d = kb.din("wmem", [D, 1024])
    wbr_d = kb.din("wbr", [3072, D]); wout_d = kb.din("wout", [D, D])
    AW = (1152, 1536, 3072)
    akT_d = [kb.din("akT%d" % g, [512, AW[g]], BF16) for g in range(3)]
    aV_d = [kb.din("aV%d" % g, [AW[g], 512], BF16) for g in range(3)]
    bkT_d = kb.din("bkT", [256, 1152], BF16); bV_d = kb.din("bV", [1152, 256], BF16)
    ckT_d = kb.din("ckT", [1024, 4096], BF16); cV_d = kb.din("cV", [4096, 1024], BF16)
    ckTo_d = kb.din("ckTo", [1024, 1024], BF16); cVo_d = kb.din("cVo", [1024, 1024], BF16)
    abias_d = kb.din("abias", [128, 40 * 128], BF16); cmask_d = kb.din("cmask", [128, 512], BF16)
    cb_d = kb.din("cb", [128, 512]); cbo_d = kb.din("cbo", [128, 128]); hv_d = kb.din("hv", [128, 2])
    c_d = kb.din("consts", [128, 384], BF16)
    xo_d = kb.dout("xo", [T, D])
    yd_d = kb.dout("ydbg", [3072, T]) if debug else None
    md_d = kb.dout("mdbg", [D, T], BF16) if debug else None
    finals = []
    c = kb.sb([128, 384], BF16)
    P.dma("sp", c.t[:], c_d, writes=[c.b])
    kb.cst = c; kb.ident = c.t[:, 0:128]; kb.ones = c.t[:, 128:256]; kb.ones128 = c.t[:, 256:384]
    kb.p0 = kb.ps([128, 512]); kb.p1 = kb.ps([128, 512]); kb.pss = kb.ps([128, 512])
    kb.s0 = kb.ps([128, 512]); kb.s1 = kb.ps([128, 512]); kb.ud0 = kb.ps([128, 512]); kb.ud1 = kb.ps([128, 512])
    kb.ptr = kb.ps([128, 1024], BF16)
    kb.ring = [kb.sb([128, 4096], BF16) for _ in range(3)]
    kb.ring_i = 0
    kb.sq = [kb.sb([128, 512], BF16) for _ in range(2)]
    kb.rstd = [kb.sb([128, 512], F32) for _ in range(2)]
    kb.qn_i = 0; kb.pp_i = 0
    kb.epsc = kb.sb([128, 1], F32)
    P.op("dve", lambda e: e.memset(kb.epsc.t[:], EPS), writes=[kb.epsc.b])
    hT = kb.sb([128, KC, T], BF16)
    uT = kb.sb([128, 24, T], BF16)
    R = Region(kb, 11500)
    gq = kb.sb([128, 8], F32); bg = kb.sb([128, 64], F32); esk = kb.sb([128, 8], F32)
    cb = kb.sb([128, 512], F32); cbo = kb.sb([128, 128], F32); hv = kb.sb([128, 2], F32)
    cmask = kb.sb([128, 512], BF16); subg = kb.sb([128, 256], F32); nlam = kb.sb([128, 1], F32)
    qT = kb.sb([128, 3, T], BF16); zs = kb.sb([128, 2, T], BF16)
    pT = [kb.sb([128, 512], BF16) for _ in range(2)]
    tf = [kb.sb([128, 512], F32) for _ in range(3)]
    mkT = kb.sb([128, 4, 256], BF16); mv = kb.sb([128, 2, 512], BF16)
    for t_, d_ in ((gq, gq_d), (bg, bg_d), (esk, sk_d), (cb, cb_d), (cbo, cbo_d), (hv, hv_d), (cmask, cmask_d), (subg, subg_d)):
        P.dma("sp", t_.t[:], d_, writes=[t_.b])
    for col in (0, 2, 4, 6):
        P.op("dve", lambda e, col=col: e.tensor_scalar(out=gq.t[:, col:col + 1], in0=gq.t[:, col:col + 1], scalar1=128 ** -0.5,
                                                       scalar2=None, op0=ALU.mult), reads=[gq.b], writes=[gq.b])
    P.op("act", lambda e: e.activation(out=esk.t[:], in_=esk.t[:], func=AF.Exp), reads=[esk.b], writes=[esk.b])
    P.op("dve", lambda e: e.tensor_scalar(out=subg.t[:], in0=subg.t[:], scalar1=1.0 - lam_init, scalar2=None, op0=ALU.mult),
         reads=[subg.b], writes=[subg.b])
    ngb = R.take(D, F32); xt = [R.take(D, F32) for _ in range(2)]; sqf = R.take(D, F32); hb = R.take(D, BF16)
    ss = R.take(1, F32); rs = R.take(1, F32); mnT = kb.sb([128, KC, 256], BF16)
    P.dma("sp", ngb.t[:], ng_d[0, :].partition_broadcast(128), writes=[ngb.b])
    norm_to_hT(kb, x_d, ngb, hT, 8, (xt, sqf, hb, ss, rs))
    P.dma("sp", ngb.t[:], mng_d[0, :].partition_broadcast(128), writes=[ngb.b])
    norm_to_hT(kb, mem_d, ngb, mnT, 2, (xt, sqf, hb, ss, rs))
    lm = xt[0]
    P.dma("sp", lm.t[:, 0:512], lam_d, writes=[lm.b])
    P.op("dve", lambda e: e.tensor_tensor(out=lm.t[:, 512:640], in0=lm.t[:, 0:128], in1=lm.t[:, 128:256], op=ALU.mult), reads=[lm.b], writes=[lm.b])
    P.op("dve", lambda e: e.tensor_tensor(out=lm.t[:, 640:768], in0=lm.t[:, 256:384], in1=lm.t[:, 384:512], op=ALU.mult), reads=[lm.b], writes=[lm.b])
    P.op("dve", lambda e: e.reduce_sum(out=lm.t[:, 800:801], in_=lm.t[:, 512:640], axis=AX.X), reads=[lm.b], writes=[lm.b])
    P.op("dve", lambda e: e.reduce_sum(out=lm.t[:, 801:802], in_=lm.t[:, 640:768], axis=AX.X), reads=[lm.b], writes=[lm.b])
    P.op("act", lambda e: e.activation(out=lm.t[:, 800:802], in_=lm.t[:, 800:802], func=AF.Exp), reads=[lm.b], writes=[lm.b])
    P.op("dve", lambda e: e.tensor_tensor(out=nlam.t[:], in0=lm.t[:, 801:802], in1=lm.t[:, 800:801], op=ALU.subtract), reads=[lm.b], writes=[nlam.b])
    P.op("dve", lambda e: e.tensor_scalar(out=nlam.t[:], in0=nlam.t[:], scalar1=-lam_init, scalar2=None, op0=ALU.add), reads=[nlam.b], writes=[nlam.b])
    wv, wb = load_w(kb, wmem_d, 0, 256)
    wv2, wb2 = load_w(kb, wmem_d, 256, 256)
    for hh in range(4):
        pp = next_pp(kb)
        proj_fm(kb, (wv, wv2)[hh // 2], (wb, wb2)[hh // 2], (hh % 2) * 128, 128, mnT, 0, 256, pp)
        qknorm(kb, pp, 256, gq.t[:, 7:8], gq.b, mkT.t[:, hh, :], mkT.b)
    for cc in range(2):
        wv, wb = load_w(kb, wmem_d, 512 + cc * 256, 256)
        for mt in range(2):
            pp = next_pp(kb)
            proj_tm(kb, wv, wb, 0, 256, mnT, mt * 128, pp)
            P.op("act", lambda e, pp=pp, mt=mt, cc=cc: e.activation(out=mv.t[:, mt, cc * 256:(cc + 1) * 256], in_=pp.t[:, 0:256], func=AF.Copy),
                 reads=[pp.b], writes=[mv.b])
    P.barrier()
    wcol = [0]

    def wchunk():
        v = load_w(kb, w_d, wcol[0], 256)
        wcol[0] += 256
        return v

    def do_q(wv, wb, col, gcol, slot, dd=1):
        for th in range(2):
            pp = next_pp(kb)
            proj_fm(kb, wv, wb, col, 128, hT, th * 512, 512, pp)
            if dd == 1:
                dst = qT.t[:, slot, th * 512:(th + 1) * 512]
            else:
                n = 512 // dd
                dst = qT.t[:, slot, :].rearrange("p (r i) -> p i r", r=dd)[:, th * n:(th + 1) * n, :]
            qknorm(kb, pp, 512, gq.t[:, gcol:gcol + 1], gq.b, dst, qT.b, dd)

    def do_z(wv, wb, col, slot):
        for th in range(2):
            pp = next_pp(kb)
            proj_fm(kb, wv, wb, col, 128, hT, th * 512, 512, pp)
            P.op("act", lambda e, pp=pp, th=th: e.activation(out=zs.t[:, slot, th * 512:(th + 1) * 512], in_=pp.t[:, 0:512], func=AF.Silu),
                 reads=[pp.b], writes=[zs.b])

    def dbg_y(y_ap, yb, row0, c0, n):
        if debug:
            finals.append(P.dma("sp", yd_d[row0:row0 + 128, c0:c0 + n], y_ap, reads=[yb]))

    sc = [kb.s0, kb.s1]
    sci = [0]

    def tiles_pipeline(tiles):
        n = len(tiles)
        slots = []

        def qk(i):
            t = tiles[i]
            s = sc[sci[0] % 2]; p = pT[sci[0] % 2]; sci[0] += 1
            slots.append((s, p))
            nk, nq = t["nk"], t["nq"]

            def f(e):
                last = e.matmul(s.t[0:nk, 0:nq], lhsT=t["kT"], rhs=t["q"], start=True, stop=(t["bias"] is None))
                if t["bias"] is not None:
                    last = e.matmul(s.t[0:nk, 0:nq], lhsT=kb.ident[0:nk, 0:nk], rhs=t["bias"], start=False, stop=True)
                return last
            P.op("pe", f, reads=list(t["kdeps"]) + [kb.cst.b], writes=[s.b])

        def ex_pv(i):
            t = tiles[i]
            s, p = slots[i]
            nk, nq = t["nk"], t["nq"]
            P.op("act", lambda e: e.activation(out=p.t[0:nk, 0:nq], in_=s.t[0:nk, 0:nq], func=AF.Exp, bias=t["bcol"]),
                 reads=[s.b] + list(t["bdeps"]), writes=[p.b])
            P.op("pe", lambda e: t["pv"](e, p.t[0:nk, 0:nq]), reads=[p.b, kb.cst.b] + list(t["pvr"]), writes=list(t["pvw"]))
            if t.get("after"):
                t["after"]()
        for i in range(n):
            if i == 0:
                qk(0)
            if i + 1 < n:
                qk(i + 1)
            ex_pv(i)

    def fin_fm(u_ap, d_ap, ub, n, zslot, z0, chunk, c0, esk_col=None):
        a, b2 = tf[0], tf[1]
        if esk_col is not None:
            P.op("dve", lambda e: e.tensor_scalar(out=a.t[:, 0:n], in0=d_ap, scalar1=esk_col, scalar2=None, op0=ALU.add), reads=ub + [esk.b], writes=[a.b])
            P.op("dve", lambda e: e.reciprocal(out=a.t[:, 0:n], in_=a.t[:, 0:n]), reads=[a.b], writes=[a.b])
        else:
            P.op("dve", lambda e: e.reciprocal(out=a.t[:, 0:n], in_=d_ap), reads=ub, writes=[a.b])
        P.op("dve", lambda e: e.tensor_tensor(out=b2.t[:, 0:n], in0=u_ap, in1=a.t[:, 0:n], op=ALU.mult), reads=ub + [a.b], writes=[b2.b])
        dbg_y(b2.t[:, 0:n], b2.b, chunk * 128, c0, n)
        P.op("dve", lambda e: e.tensor_tensor(out=uT.t[:, chunk, c0:c0 + n], in0=b2.t[:, 0:n], in1=zs.t[:, zslot, z0:z0 + n], op=ALU.mult),
             reads=[b2.b, zs.b], writes=[uT.b])

    for hh in range(4 if stage >= 1 else 0):
        wv, wb = wchunk()
        do_q(wv, wb, 0, 6, 0)
        do_z(wv, wb, 128, 0)
        for th in range(2):
            tl = []
            for mt in range(2):
                def pv(e, p_ap, mt=mt, hh=hh):
                    e.matmul(kb.ud0.t[:, 0:512], lhsT=mv.t[:, mt, hh * 128:(hh + 1) * 128], rhs=p_ap, start=(mt == 0), stop=(mt == 1))
                    return e.matmul(kb.ud1.t[:, 0:512], lhsT=kb.ones, rhs=p_ap, start=(mt == 0), stop=(mt == 1))
                tl.append(dict(kT=mkT.t[:, hh, mt * 128:(mt + 1) * 128], q=qT.t[:, 0, th * 512:(th + 1) * 512], nk=128, nq=512, bias=None,
                               bcol=0.0, bdeps=[], pv=pv, pvr=[mv.b], pvw=[kb.ud0.b, kb.ud1.b], kdeps=[mkT.b, qT.b]))
            tiles_pipeline(tl)
            fin_fm(kb.ud0.t[:, 0:512], kb.ud1.t[:, 0:512], [kb.ud0.b, kb.ud1.b], 512, 0, th * 512, 20 + hh, th * 512)
    P.barrier()
    ab = kb.sb([128, 6, 128], BF16)
    udl = [kb.ud0, kb.ud1]
    udi = [0]

    def band_block(kT_t, kTb, k0, nk1, q_ap, nq, bt, vt0, vt1, vb, hcol, after):
        ud = udl[udi[0] % 2]; udi[0] += 1
        tl = []
        for ti in range(2):
            nk = 128 if ti == 0 else nk1
            vv = vt0 if ti == 0 else vt1

            def pv(e, p_ap, ti=ti, nk=nk, vv=vv):
                e.matmul(ud.t[:, 0:nq], lhsT=vv, rhs=p_ap, start=(ti == 0), stop=(ti == 1), skip_group_check=True)
                return e.matmul(ud.t[:, 128:128 + nq], lhsT=kb.ones[0:nk, :], rhs=p_ap, start=False, stop=(ti == 1), skip_group_check=True)
            tl.append(dict(kT=kT_t[:, k0 + ti * 128:k0 + ti * 128 + nk], q=q_ap, nk=nk, nq=nq, bias=bt[ti][0:nk, 0:nq],
                           bcol=(hcol if (ti == 0 and hcol is not None) else 0.0), bdeps=[hv.b, ab.b], pv=pv, pvr=[vb], pvw=[ud.b],
                           kdeps=[kTb, qT.b, ab.b], after=(lambda: after(ud)) if ti == 1 else None))
        tiles_pipeline(tl)

    for kv in range(2 if (stage >= 2 and stage not in (30, 31)) else 0):
        R.reset()
        bk = R.take(1152, BF16); bvv = R.take(9 * 128, BF16)
        P.dma("sp", bk.t[:], bkT_d[kv * 128:(kv + 1) * 128, :], writes=[bk.b])
        bv3 = bvv.t[:].rearrange("p (a c) -> p a c", a=9)
        P.dma("sp", bv3, bV_d[:, kv * 128:(kv + 1) * 128].rearrange("(a p) c -> p a c", p=128), writes=[bvv.b])
        for gi in range(4):
            hd = kv * 4 + gi
            wv, wb = wchunk()
            do_q(wv, wb, 0, 2, 0)
            do_z(wv, wb, 128, 0)
            P.dma("sp", ab.t[:, 0:2, :], abias_d[:, (24 + hd * 2) * 128:(26 + hd * 2) * 128].rearrange("p (a c) -> p a c", a=2), writes=[ab.b])
            for qb in range(8):
                def after(ud, qb=qb, hd=hd):
                    fin_fm(ud.t[:, 0:128], ud.t[:, 128:256], [ud.b], 128, 0, qb * 128, 4 + hd, qb * 128, esk_col=esk.t[:, hd:hd + 1])
                band_block(bk.t, bk.b, qb * 128, 128, qT.t[:, 0, qb * 128:(qb + 1) * 128], 128, (ab.t[:, 0, :], ab.t[:, 1, :]),
                           bv3[:, qb, :], bv3[:, qb + 1, :], bvv.b, hv.t[:, 0:1] if qb == 0 else None, after)
        P.barrier()
    for h in range(4 if stage >= 3 else 0):
        R.reset()
        ak = [R.take(AW[g] + (64 if g == 2 else 0), BF16) for g in range(3)]
        av0 = R.take(9 * 128, BF16); av1 = R.take(12 * 128, BF16); av2h = R.take(16 * 128, BF16); av2o = R.take(16 * 128, BF16)
        Ua = R.take(T, F32); Sa = R.take(T, F32)
        for g in range(3):
            if g >= 1 and 'k' in _ASK:
                continue
            if g == 2:
                P.op("dve", lambda e: e.memset(ak[2].t[:, 3072:3136], 0.0), writes=[ak[2].b])
            P.dma("sp", ak[g].t[:, 0:AW[g]], akT_d[g][h * 128:(h + 1) * 128, :], writes=[ak[g].b])
        v0 = av0.t[:].rearrange("p (a c) -> p a c", a=9)
        v1 = av1.t[:].rearrange("p (a c) -> p a c", a=12)
        v2h = av2h.t[:].rearrange("p (a c) -> p a c", a=16)
        v2o = av2o.t[:].rearrange("p (a c) -> p a c", a=16)
        P.dma("sp", v0, aV_d[0][:, h * 128:(h + 1) * 128].rearrange("(a p) c -> p a c", p=128), writes=[av0.b])
        if 'k' not in _ASK:
            P.dma("sp", v1, aV_d[1][:, h * 128:(h + 1) * 128].rearrange("(a p) c -> p a c", p=128), writes=[av1.b])
        a2 = aV_d[2][:, h * 128:(h + 1) * 128].rearrange("(r w) c -> w r c", w=192)
        if 'v' not in _ASK:
            P.dma("sp", v2h[:, 0:8, :], a2[0:128, 0:8, :], writes=[av2h.b])
            P.dma("sp", v2h[:, 8:16, :], a2[0:128, 8:16, :], writes=[av2h.b])
            P.op("dve", lambda e: e.memset(av2o.t[:], 0.0), writes=[av2o.b])
            P.dma("sp", v2o[0:64, :, :], a2[128:192, :, :], writes=[av2o.b])
        wv, wb = wchunk()
        do_q(wv, wb, 0, 0, 0)
        do_q(wv, wb, 128, 0, 1, 1 if 'q' in _ASK else 4)
        wv, wb = wchunk()
        do_q(wv, wb, 0, 0, 2, 1 if 'q' in _ASK else 16)
        do_z(wv, wb, 128, 0)
        for g in range(3):
            P.dma("sp", ab.t[:, 2 * g:2 * g + 2, :], abias_d[:, ((g * 4 + h) * 2) * 128:((g * 4 + h) * 2 + 2) * 128].rearrange("p (a c) -> p a c", a=2),
                  writes=[ab.b])

        def evac(ud, nq, dst_u, dst_s, first):
            if first:
                P.op("dve", lambda e: e.tensor_copy(out=dst_u, in_=ud.t[:, 0:nq]), reads=[ud.b], writes=[Ua.b])
                P.op("dve", lambda e: e.tensor_copy(out=dst_s, in_=ud.t[:, 128:128 + nq]), reads=[ud.b], writes=[Sa.b])
            else:
                P.op("dve", lambda e: e.tensor_tensor(out=dst_u, in0=dst_u, in1=ud.t[:, 0:nq], op=ALU.add), reads=[ud.b, Ua.b], writes=[Ua.b])
                P.op("dve", lambda e: e.tensor_tensor(out=dst_s, in0=dst_s, in1=ud.t[:, 128:128 + nq], op=ALU.add), reads=[ud.b, Sa.b], writes=[Sa.b])
        for qb in range(0 if 'g' in _ASK else 8):
            band_block(ak[0].t, ak[0].b, qb * 128, 128, qT.t[:, 0, qb * 128:(qb + 1) * 128], 128, (ab.t[:, 0, :], ab.t[:, 1, :]),
                       v0[:, qb, :], v0[:, qb + 1, :], av0.b, hv.t[:, 0:1] if qb == 0 else None,
                       lambda ud, qb=qb: evac(ud, 128, Ua.t[:, qb * 128:(qb + 1) * 128], Sa.t[:, qb * 128:(qb + 1) * 128], True))
        for r in range(4 if stage != 31 else 0):
            for sbk in range(2):
                st0 = 4 * sbk * 128 + r
                band_block(ak[1].t, ak[1].b, r * 384 + sbk * 128, 128, qT.t[:, 1, r * 256 + sbk * 128:r * 256 + sbk * 128 + 128], 128,
                           (ab.t[:, 2, :], ab.t[:, 3, :]), v1[:, r * 3 + sbk, :], v1[:, r * 3 + sbk + 1, :], av1.b,
                           hv.t[:, 0:1] if sbk == 0 else None,
                           lambda ud, st0=st0: evac(ud, 128, Ua.t[:, st0:st0 + 509:4], Sa.t[:, st0:st0 + 509:4], False))
        for r in range(16 if stage != 31 else 0):
            band_block(ak[2].t, ak[2].b, r * 192, 128, qT.t[:, 2, r * 64:(r + 1) * 64], 64, (ab.t[:, 4, :], ab.t[:, 5, :]),
                       v2h[:, r, :], v2o[:, r, :], av2h.b, hv.t[:, 1:2],
                       lambda ud, r=r: evac(ud, 64, Ua.t[:, r:1024:16], Sa.t[:, r:1024:16], False))
        for th in range(0 if 'f' in _ASK else 2):
            fin_fm(Ua.t[:, th * 512:(th + 1) * 512], Sa.t[:, th * 512:(th + 1) * 512], [Ua.b, Sa.b, av2o.b], 512, 0, th * 512, h, th * 512)
        P.barrier()
    for h in range(4 if (stage >= 4 and stage not in (30, 31)) else 0):
        R.reset()
        ck = R.take(2 * 4096, BF16); cva = R.take(32 * 257, BF16); cko = R.take(2 * 1024, BF16); cvo = R.take(8 * 257, BF16)
        o0s = R.take(2 * 257, F32); yf = R.take(256, F32); ysq = R.take(256, F32); yn = R.take(256, BF16); sm = R.take(8, F32)
        ck3 = ck.t[:].rearrange("p (a c) -> p a c", a=2); cko3 = cko.t[:].rearrange("p (a c) -> p a c", a=2)
        cva3 = cva.t[:].rearrange("p (a c) -> p a c", a=32); cvo3 = cvo.t[:].rearrange("p (a c) -> p a c", a=8)
        o03 = o0s.t[:].rearrange("p (a c) -> p a c", a=2)
        for c_ in range(2):
            P.dma("sp", ck3[:, c_, :], ckT_d[(h * 2 + c_) * 128:(h * 2 + c_ + 1) * 128, :], writes=[ck.b])
            P.dma("sp", cko3[:, c_, :], ckTo_d[(h * 2 + c_) * 128:(h * 2 + c_ + 1) * 128, :], writes=[cko.b])
        for q4 in range(4):
            P.dma("sp", cva3[:, q4 * 8:(q4 + 1) * 8, 0:256],
                  cV_d[q4 * 1024:(q4 + 1) * 1024, h * 256:(h + 1) * 256].rearrange("(a p) c -> p a c", p=128), writes=[cva.b])
        P.dma("sp", cvo3[:, :, 0:256], cVo_d[:, h * 256:(h + 1) * 256].rearrange("(a p) c -> p a c", p=128), writes=[cvo.b])
        P.op("dve", lambda e: e.memset(cva3[:, :, 256:257], 1.0), writes=[cva.b])
        P.op("dve", lambda e: e.memset(cvo3[:, :, 256:257], 1.0), writes=[cvo.b])
        wv, wb = wchunk()
        do_q(wv, wb, 0, 4, 0)
        do_q(wv, wb, 128, 4, 1)
        wv, wb = wchunk()
        do_z(wv, wb, 0, 0)
        do_z(wv, wb, 128, 1)
        for g in range(4):
            for c_ in range(2):
                tl = []
                nt = 32 + 2 * g + 2
                for ti in range(nt):
                    own = ti >= 32
                    lk = ti - 32

                    def pv(e, p_ap, ti=ti, own=own, lk=lk, nt=nt):
                        vv = cvo3[:, lk, :] if own else cva3[:, ti, :]
                        e.matmul(kb.ud0.t[:, 0:257], lhsT=p_ap[:, 0:128], rhs=vv, start=(ti == 0), stop=(ti == nt - 1))
                        return e.matmul(kb.ud1.t[:, 0:257], lhsT=p_ap[:, 128:256], rhs=vv, start=(ti == 0), stop=(ti == nt - 1))
                    bias = None
                    if own and lk == 2 * g:
                        bias = cmask.t[:, 0:256]
                    elif own and lk == 2 * g + 1:
                        bias = cmask.t[:, 256:512]
                    kT_ap = cko3[:, c_, lk * 128:(lk + 1) * 128] if own else ck3[:, c_, ti * 128:(ti + 1) * 128]
                    bcol = cbo.t[:, (h * 4 + g) * 8 + lk:(h * 4 + g) * 8 + lk + 1] if own else cb.t[:, (h * 4 + g) * 32 + ti:(h * 4 + g) * 32 + ti + 1]
                    tl.append(dict(kT=kT_ap, q=qT.t[:, c_, g * 256:(g + 1) * 256], nk=128, nq=256, bias=bias, bcol=bcol, bdeps=[cb.b, cbo.b, cmask.b],
                                   pv=pv, pvr=[cva.b, cvo.b], pvw=[kb.ud0.b, kb.ud1.b], kdeps=[ck.b, cko.b, qT.b, cmask.b]))
                tiles_pipeline(tl)
                if c_ == 0:
                    P.op("act", lambda e: e.activation(out=o03[:, 0, :], in_=kb.ud0.t[:, 0:257], func=AF.Copy), reads=[kb.ud0.b], writes=[o0s.b])
                    P.op("act", lambda e: e.activation(out=o03[:, 1, :], in_=kb.ud1.t[:, 0:257], func=AF.Copy), reads=[kb.ud1.b], writes=[o0s.b])
            for j in range(2):
                ud = udl[j]
                qblk = 2 * g + j
                P.op("dve", lambda e, j=j: e.reciprocal(out=sm.t[:, 0:1], in_=o03[:, j, 256:257]), reads=[o0s.b], writes=[sm.b])
                P.op("dve", lambda e, ud=ud: e.reciprocal(out=sm.t[:, 1:2], in_=ud.t[:, 256:257]), reads=[ud.b], writes=[sm.b])
                P.op("dve", lambda e: e.tensor_tensor(out=sm.t[:, 1:2], in0=sm.t[:, 1:2], in1=nlam.t[:, 0:1], op=ALU.mult), reads=[sm.b, nlam.b], writes=[sm.b])
                P.op("dve", lambda e, ud=ud: e.tensor_scalar(out=ysq.t[:], in0=ud.t[:, 0:256], scalar1=sm.t[:, 1:2], scalar2=None, op0=ALU.mult),
                     reads=[ud.b, sm.b], writes=[ysq.b])
                P.op("dve", lambda e, j=j: e.scalar_tensor_tensor(out=yf.t[:], in0=o03[:, j, 0:256], scalar=sm.t[:, 0:1], in1=ysq.t[:],
                                                                  op0=ALU.mult, op1=ALU.add), reads=[o0s.b, sm.b, ysq.b], writes=[yf.b])
                P.op("act", lambda e: e.activation(out=ysq.t[:], in_=yf.t[:], func=AF.Square), reads=[yf.b], writes=[ysq.b])
                P.op("dve", lambda e: e.reduce_sum(out=sm.t[:, 2:3], in_=ysq.t[:], axis=AX.X), reads=[ysq.b], writes=[sm.b])
                P.op("act", lambda e: e.activation(out=sm.t[:, 3:4], in_=sm.t[:, 2:3], func=AF.Ln, bias=kb.epsc.t[:, 0:1], scale=1.0 / 256),
                     reads=[sm.b, kb.epsc.b], writes=[sm.b])
                P.op("act", lambda e: e.activation(out=sm.t[:, 3:4], in_=sm.t[:, 3:4], func=AF.Exp, scale=-0.5), reads=[sm.b], writes=[sm.b])
                P.op("dve", lambda e: e.scalar_tensor_tensor(out=yn.t[:], in0=yf.t[:], scalar=sm.t[:, 3:4], in1=subg.t[:], op0=ALU.mult, op1=ALU.mult),
                     reads=[yf.b, sm.b, subg.b], writes=[yn.b])

                def tr(e):
                    e.transpose(out=kb.ptr.t[:, 0:128], in_=yn.t[:, 0:128], identity=kb.ident)
                    return e.transpose(out=kb.ptr.t[:, 128:256], in_=yn.t[:, 128:256], identity=kb.ident)
                P.op("pe", tr, reads=[yn.b, kb.cst.b], writes=[kb.ptr.b])
                for e2 in range(2):
                    if debug:
                        P.op("dve", lambda e, e2=e2: e.tensor_copy(out=tf[2].t[:, 0:128], in_=kb.ptr.t[:, e2 * 128:(e2 + 1) * 128]), reads=[kb.ptr.b], writes=[tf[2].b])
                        dbg_y(tf[2].t[:, 0:128], tf[2].b, (12 + 2 * h + e2) * 128, qblk * 128, 128)
                    P.op("dve", lambda e, e2=e2, qblk=qblk, h=h: e.tensor_tensor(out=uT.t[:, 12 + 2 * h + e2, qblk * 128:(qblk + 1) * 128],
                                                                            in0=kb.ptr.t[:, e2 * 128:(e2 + 1) * 128],
                                                                            in1=zs.t[:, e2, qblk * 128:(qblk + 1) * 128], op=ALU.mult),
                         reads=[kb.ptr.b, zs.b], writes=[uT.b])
        P.barrier()
    R.reset()
    mT = R.take(KC * T, BF16)
    mT3 = TB.__new__(TB); mT3.t = mT.t[:].rearrange("p (k t) -> p k t", k=KC); mT3.b = mT.b
    gsb = [kb.sb([128, 512], BF16) for _ in range(4)]
    bch = [(0, 4), (4, 12), (12, 20), (20, 24)]
    pb = [kb.s0, kb.s1, kb.ud0, kb.ud1]
    wcol[0] = 15360 - 8192
    for oc in range(16 if (stage >= 5 and stage not in (30, 31)) else 0):
        wa = wchunk(); wb_ = wchunk()
        r = next_ring(kb)
        wbr = r.t[:, 0:24 * 128].rearrange("p (k c) -> p k c", k=24)
        for q2 in range(2):
            P.dma("pool", wbr[:, q2 * 12:(q2 + 1) * 12, :],
                  wbr_d[q2 * 1536:(q2 + 1) * 1536, oc * 128:(oc + 1) * 128].rearrange("(k p) c -> p k c", p=128), writes=[r.b])
        for th in range(2):
            for b_ in range(4):
                pp = next_pp(kb)
                wv, wb = (wa, wb_)[b_ // 2]
                proj_fm(kb, wv, wb, (b_ % 2) * 128, 128, hT, th * 512, 512, pp)
                P.op("act", lambda e, pp=pp, b_=b_, oc=oc: e.activation(out=gsb[b_].t[:], in_=pp.t[:, 0:512], func=AF.Sigmoid,
                                                                       bias=bg.t[:, oc * 4 + b_:oc * 4 + b_ + 1]),
                     reads=[pp.b, bg.b], writes=[gsb[b_].b])
            for b_ in range(4):
                def f(e, b_=b_, th=th, wbr=wbr):
                    last = None
                    lo, hi = bch[b_]
                    for k in range(lo, hi):
                        last = e.matmul(pb[b_].t[:, 0:512], lhsT=wbr[:, k, :], rhs=uT.t[:, k, th * 512:(th + 1) * 512], start=(k == lo), stop=(k == hi - 1))
                    return last
                P.op("pe", f, reads=[r.b, uT.b], writes=[pb[b_].b])
            P.op("dve", lambda e: e.tensor_tensor(out=tf[0].t[:], in0=pb[0].t[:, 0:512], in1=gsb[0].t[:], op=ALU.mult), reads=[pb[0].b, gsb[0].b], writes=[tf[0].b])
            for b_ in range(1, 4):
                P.op("dve", lambda e, b_=b_: e.tensor_tensor(out=tf[1].t[:], in0=pb[b_].t[:, 0:512], in1=gsb[b_].t[:], op=ALU.mult),
                     reads=[pb[b_].b, gsb[b_].b], writes=[tf[1].b])
                if b_ < 3:
                    P.op("dve", lambda e: e.tensor_tensor(out=tf[0].t[:], in0=tf[0].t[:], in1=tf[1].t[:], op=ALU.add), reads=[tf[0].b, tf[1].b], writes=[tf[0].b])
                else:
                    P.op("dve", lambda e, oc=oc, th=th: e.tensor_tensor(out=mT3.t[:, oc, th * 512:(th + 1) * 512], in0=tf[0].t[:], in1=tf[1].t[:], op=ALU.add),
                         reads=[tf[0].b, tf[1].b], writes=[mT.b])
    if debug:
        finals.append(P.dma("sp", md_d.rearrange("(k p) t -> p k t", p=128), mT3.t[:], reads=[mT.b]))
    xs = [kb.sb([128, 256], F32) for _ in range(2)]
    ot = [kb.sb([128, 256], F32) for _ in range(2)]
    i_ = 0
    for cc in range(8):
        wv, wb = load_w(kb, wout_d, cc * 256, 256)
        for t in range(8):
            pp = next_pp(kb)
            proj_tm(kb, wv, wb, 0, 256, mT3, t * 128, pp)
            x1, o1 = xs[i_ % 2], ot[i_ % 2]
            i_ += 1
            P.dma("sp", x1.t[:], x_d[t * 128:(t + 1) * 128, cc * 256:(cc + 1) * 256], writes=[x1.b])
            P.op("dve", lambda e, pp=pp, x1=x1, o1=o1: e.tensor_tensor(out=o1.t[:], in0=pp.t[:, 0:256], in1=x1.t[:], op=ALU.add),
                 reads=[pp.b, x1.b], writes=[o1.b])
            finals.append(P.dma("sp", xo_d[t * 128:(t + 1) * 128, cc * 256:(cc + 1) * 256], o1.t[:], reads=[o1.b]))
    P.emit(kb.nc, finals)
    kb.st.close()
    return kb.nc


def build_fused(lam_inits, debug=False, stage=9):
    kb = KB()
    P = kb.P
    P.ring["cc"] = 12
    P.ring["sp"] = 32
    P.ring["act"] = 12
    nl = len(lam_inits)
    x_d = kb.din("x", [T, D]); mem_d = kb.din("mem", [256, D])
    ng_all = kb.din("ng", [nl, D]); mng_all = kb.din("mng", [nl, D])
    wkv_all = kb.din("wkv", [nl * D, 5632]); w_all = kb.din("w", [nl * D, 15360])
    wmem_all = kb.din("wmem", [nl * D, 1024]); wbr_all = kb.din("wbr", [nl * 3072, D]); wout_all = kb.din("wout", [nl * D, D])
    bg_all = kb.din("bg", [nl * 128, 64]); gq_all = kb.din("gq", [nl * 128, 8]); sk_all = kb.din("sk", [nl * 128, 8])
    gk_all = kb.din("gk", [nl * 128, 3])
    lam_all = kb.din("lam", [nl * 128, 512]); subg_all = kb.din("subg", [nl * 128, 256])
    abias_d = kb.din("abias", [128, 40 * 128], BF16); cmask_d = kb.din("cmask", [128, 512], BF16)
    cb_d = kb.din("cb", [128, 512]); cbo_d = kb.din("cbo", [128, 128]); hv_d = kb.din("hv", [128, 2]); sel_d = kb.din("sel", [128, 12])
    c_d = kb.din("consts", [128, 384], BF16)
    xo_d = kb.dout("xo", [T, D])
    yd_d = kb.dout("ydbg", [3072, T]) if debug else None
    md_d = kb.dout("mdbg", [D, T], BF16) if debug else None
    nc = kb.nc
    xb = [nc.dram_tensor("xb%d" % i, [T, D], F32).ap() for i in range(2)]
    Bxb = [Buf(), Buf()]
    KP = [512] * 5 + [256]
    kTl_p = [nc.dram_tensor("kTl%d" % p, [KP[p], T], BF16) for p in range(6)]
    kTa_p = [nc.dram_tensor("kTa%d" % p, [4 * KP[p], T], BF16) for p in range(6)]
    vl_p = [nc.dram_tensor("vl%d" % p, [T, KP[p]], BF16) for p in range(6)]
    va_p = [nc.dram_tensor("va%d" % p, [4 * T, KP[p]], BF16) for p in range(6)]
    BkTl = [Buf() for _ in range(6)]; Bvl = [Buf() for _ in range(6)]; BkTa = [Buf() for _ in range(6)]; Bva = [Buf() for _ in range(6)]

    def kTl_rows(q0, n):
        p = q0 // 512
        return kTl_p[p].ap()[q0 - p * 512:q0 - p * 512 + n, :]

    def kTa_rows(r, q0, n):
        p = q0 // 512
        o = r * KP[p] + q0 - p * 512
        return kTa_p[p].ap()[o:o + n, :]

    def vl_cols(c0, n):
        p = c0 // 512
        return vl_p[p].ap()[:, c0 - p * 512:c0 - p * 512 + n]

    def va_cols(c0, n):
        p = c0 // 512
        return va_p[p].ap()[:, c0 - p * 512:c0 - p * 512 + n]
    RG = [[0, 1, 2, 3], [4, 5, 6, 7]]
    AW = (1152, 1536, 3072)
    gates_d = nc.dram_tensor("gates_scr", [8192, T], BF16).ap()
    Bgates = Buf()
    akT_d = [nc.dram_tensor("akT_rel%d" % g, [512, AW[g]], BF16).ap() for g in range(3)]
    aV_d = [nc.dram_tensor("aV_rel%d" % g, [AW[g], 512], BF16).ap() for g in range(3)]
    bkT_d = nc.dram_tensor("bkT_rel", [256, 1152], BF16).ap(); bV_d = nc.dram_tensor("bV_rel", [1152, 256], BF16).ap()
    finals = []
    c = kb.sb([128, 384], BF16)
    P.dma("sp", c.t[:], c_d, writes=[c.b])
    kb.cst = c; kb.ident = c.t[:, 0:128]; kb.ones = c.t[:, 128:256]; kb.ones128 = c.t[:, 256:384]
    kb.p0 = kb.ps([128, 512]); kb.p1 = kb.ps([128, 512]); kb.pss = kb.ps([128, 512])
    kb.s0 = kb.ps([128, 512]); kb.s1 = kb.ps([128, 512]); kb.ud0 = kb.ps([128, 512]); kb.ud1 = kb.ps([128, 512])
    kb.ptr = kb.ps([128, 1024], BF16)
    kb.ring = [kb.sb([128, 4096], BF16) for _ in range(4)]
    kb.ring_i = 0
    kb.sq = [kb.sb([128, 512], BF16) for _ in range(2)]
    kb.rstd = [kb.sb([128, 512], F32) for _ in range(2)]
    kb.qn_i = 0; kb.pp_i = 0
    kb.pp_list = [kb.p0, kb.p1, kb.s0, kb.s1]
    pend = [None]

    def push(projf, epif):
        pp = next_pp(kb)
        projf(pp)
        epif(pp)

    def drain():
        if pend[0] is not None:
            pend[0][0](pend[0][1])
            pend[0] = None
    kb.epsc = kb.sb([128, 1], F32)
    P.op("dve", lambda e: e.memset(kb.epsc.t[:], EPS), writes=[kb.epsc.b])
    hT = kb.sb([128, KC, T], BF16)
    uT = kb.sb([128, 24, T], BF16)
    R = Region(kb, 11500)
    gq = kb.sb([128, 8], F32); bg = kb.sb([128, 64], F32); esk = kb.sb([128, 8], F32)
    cb = kb.sb([128, 512], F32); cbo = kb.sb([128, 128], F32); hv = kb.sb([128, 2], F32)
    cmask = kb.sb([128, 512], BF16); subg = kb.sb([128, 256], F32); nlam = kb.sb([128, 1], F32)
    qT = kb.sb([128, 3, T], BF16); zs = kb.sb([128, 2, T], BF16)
    pT = [kb.sb([128, 512], BF16) for _ in range(4)]
    tf = [kb.sb([128, 512], F32) for _ in range(3)]
    mkT = kb.sb([128, 4, 256], BF16); mv = kb.sb([128, 2, 512], BF16)
    gk = kb.sb([128, 3], F32); sel = kb.sb([128, 12], F32)
    ab = kb.sb([128, 6, 128], BF16)
    gsb = [kb.sb([128, 512], BF16) for _ in range(4)]
    xs = [kb.sb([128, 256], F32) for _ in range(2)]
    ot = [kb.sb([128, 256], F32) for _ in range(2)]
    for t_, d_ in ((cb, cb_d), (cbo, cbo_d), (hv, hv_d), (cmask, cmask_d), (sel, sel_d)):
        P.dma("sp", t_.t[:], d_, writes=[t_.b])
    for l in range(nl):
        if l > 0:
            P.barrier()
            P.epoch = l
        lam_init = lam_inits[l]
        x_src = x_d if l == 0 else xb[(l - 1) % 2]
        xdep = [] if l == 0 else [Bxb[(l - 1) % 2]]
        x_dst = xo_d if l == nl - 1 else xb[l % 2]
        xdst_b = [] if l == nl - 1 else [Bxb[l % 2]]
        w_d = w_all[l * D:(l + 1) * D, :]; wkv_d = wkv_all[l * D:(l + 1) * D, :]
        wmem_d = wmem_all[l * D:(l + 1) * D, :]; wbr_d = wbr_all[l * 3072:(l + 1) * 3072, :]; wout_d = wout_all[l * D:(l + 1) * D, :]
        R.reset()
        for t_, d_ in ((gq, gq_all), (bg, bg_all), (esk, sk_all), (subg, subg_all), (gk, gk_all)):
            P.dma("sp", t_.t[:], d_[l * 128:(l + 1) * 128, :], writes=[t_.b])
        for col in (0, 2, 4, 6):
            P.op("dve", lambda e, col=col: e.tensor_scalar(out=gq.t[:, col:col + 1], in0=gq.t[:, col:col + 1], scalar1=128 ** -0.5,
                                                           scalar2=None, op0=ALU.mult), reads=[gq.b], writes=[gq.b])
        P.op("act", lambda e: e.activation(out=esk.t[:], in_=esk.t[:], func=AF.Exp), reads=[esk.b], writes=[esk.b])
        P.op("dve", lambda e, li=lam_init: e.tensor_scalar(out=subg.t[:], in0=subg.t[:], scalar1=1.0 - li, scalar2=None, op0=ALU.mult),
             reads=[subg.b], writes=[subg.b])
        ngb = R.take(D, F32); xt = [R.take(D, F32) for _ in range(2)]; sqf = R.take(D, F32); hb = R.take(D, BF16)
        ss = R.take(1, F32); rs = R.take(1, F32)
        mnT_ = R.take(KC * 256, BF16)
        mnT = TB.__new__(TB); mnT.t = mnT_.t[:].rearrange("p (k t) -> p k t", k=KC); mnT.b = mnT_.b
        P.dma("sp", ngb.t[:], ng_all[l, :].partition_broadcast(128), writes=[ngb.b])
        norm_to_hT(kb, x_src, ngb, hT, 8, (xt, sqf, hb, ss, rs), xdep)
        P.dma("sp", ngb.t[:], mng_all[l, :].partition_broadcast(128), writes=[ngb.b])
        norm_to_hT(kb, mem_d, ngb, mnT, 2, (xt, sqf, hb, ss, rs))
        lm = xt[0]
        P.dma("sp", lm.t[:, 0:512], lam_all[l * 128:(l + 1) * 128, :], writes=[lm.b])
        P.op("dve", lambda e: e.tensor_tensor(out=lm.t[:, 512:640], in0=lm.t[:, 0:128], in1=lm.t[:, 128:256], op=ALU.mult), reads=[lm.b], writes=[lm.b])
        P.op("dve", lambda e: e.tensor_tensor(out=lm.t[:, 640:768], in0=lm.t[:, 256:384], in1=lm.t[:, 384:512], op=ALU.mult), reads=[lm.b], writes=[lm.b])
        P.op("dve", lambda e: e.reduce_sum(out=lm.t[:, 800:801], in_=lm.t[:, 512:640], axis=AX.X), reads=[lm.b], writes=[lm.b])
        P.op("dve", lambda e: e.reduce_sum(out=lm.t[:, 801:802], in_=lm.t[:, 640:768], axis=AX.X), reads=[lm.b], writes=[lm.b])
        P.op("act", lambda e: e.activation(out=lm.t[:, 800:802], in_=lm.t[:, 800:802], func=AF.Exp), reads=[lm.b], writes=[lm.b])
        P.op("dve", lambda e: e.tensor_tensor(out=nlam.t[:], in0=lm.t[:, 801:802], in1=lm.t[:, 800:801], op=ALU.subtract), reads=[lm.b], writes=[nlam.b])
        P.op("dve", lambda e, li=lam_init: e.tensor_scalar(out=nlam.t[:], in0=nlam.t[:], scalar1=-li, scalar2=None, op0=ALU.add), reads=[nlam.b], writes=[nlam.b])
        wv, wb = load_w(kb, wmem_d, 0, 256)
        wv2, wb2 = load_w(kb, wmem_d, 256, 256)
        for hh in range(4):
            pp = next_pp(kb)
            proj_fm(kb, (wv, wv2)[hh // 2], (wb, wb2)[hh // 2], (hh % 2) * 128, 128, mnT, 0, 256, pp)
            qknorm(kb, pp, 256, gq.t[:, 7:8], gq.b, mkT.t[:, hh, :], mkT.b)
        for cc in range(2):
            wv, wb = load_w(kb, wmem_d, 512 + cc * 256, 256)
            for mt in range(2):
                pp = next_pp(kb)
                proj_tm(kb, wv, wb, 0, 256, mnT, mt * 128, pp)
                P.op("act", lambda e, pp=pp, mt=mt, cc=cc: e.activation(out=mv.t[:, mt, cc * 256:(cc + 1) * 256], in_=pp.t[:, 0:256], func=AF.Copy),
                     reads=[pp.b], writes=[mv.b])
        P.barrier()
        R.reset()
        kst = [R.take(T, BF16) for _ in range(2)]
        vst = [R.take(8 * 256, BF16) for _ in range(2)]
        for c0 in range(0, 2816, 256):
            wv, wb = load_w(kb, wkv_d, c0, 256)
            for hh in range(2):
                hd = c0 // 128 + hh
                if hd < 12:
                    dd, gi = A_DD[hd // 4], 0
                elif hd < 14:
                    dd, gi = 1, 1
                else:
                    dd, gi = 1, 2
                ks = kst[hd % 2]
                for th in range(2):
                    if dd == 1:
                        dst = ks.t[:, th * 512:(th + 1) * 512]
                    else:
                        n = 512 // dd
                        dst = ks.t[:].rearrange("p (r i) -> p i r", r=dd)[:, th * n:(th + 1) * n, :]

                    def epi(pp, gi=gi, dst=dst, ks=ks, dd=dd, th=th, hd=hd):
                        qknorm(kb, pp, 512, gk.t[:, gi:gi + 1], gk.b, dst, ks.b, dd)
                        if th == 1:
                            P.dma("sp", kTl_rows(hd * 128, 128), ks.t[:], reads=[ks.b], writes=[BkTl[hd // 4]])
                    push(lambda pp, wv=wv, wb=wb, hh=hh, th=th: proj_fm(kb, wv, wb, hh * 128, 128, hT, th * 512, 512, pp), epi)
        drain()
        for p_ in range(6):
            P.op("pool", lambda e, p_=p_: e.collective_compute("AllGather", ALU.bypass, replica_groups=RG, ins=[kTl_p[p_].ap().opt()],
                                                               outs=[kTa_p[p_].ap().opt()]),
                 reads=[BkTl[p_]], writes=[BkTa[p_]], dma=True, inc=1, ring="cc")
        ci = 0
        for c0 in range(0, 2816, 256):
            wv, wb = load_w(kb, wkv_d, 2816 + c0, 256)
            vs = vst[ci % 2]
            vs3 = vs.t[:].rearrange("p (a c) -> p a c", a=8)
            ci += 1
            for t in range(8):
                pp = next_pp(kb)
                proj_tm(kb, wv, wb, 0, 256, hT, t * 128, pp)
                if t % 2 == 0:
                    P.op("act", lambda e, pp=pp, t=t, vs3=vs3: e.activation(out=vs3[:, t, :], in_=pp.t[:, 0:256], func=AF.Copy),
                         reads=[pp.b], writes=[vs.b])
                else:
                    P.op("dve", lambda e, pp=pp, t=t, vs3=vs3: e.tensor_copy(out=vs3[:, t, :], in_=pp.t[:, 0:256]),
                         reads=[pp.b], writes=[vs.b])
            P.dma("sp", vl_cols(c0, 256).rearrange("(t p) c -> p t c", p=128), vs3, reads=[vs.b], writes=[Bvl[c0 // 512]])
        for p_ in range(6):
            P.op("pool", lambda e, p_=p_: e.collective_compute("AllGather", ALU.bypass, replica_groups=RG, ins=[vl_p[p_].ap().opt()],
                                                               outs=[va_p[p_].ap().opt()]),
                 reads=[Bvl[p_]], writes=[Bva[p_]], dma=True, inc=1, ring="cc")
        stq = [R.take(2048, BF16) for _ in range(4)]
        sti = [0]
        accs = [R.take(4096, BF16) for _ in range(2)]
        acci = [0]

        def next_acc():
            acci[0] += 1
            return accs[acci[0] % 2]

        def select(dst_ap, cands, scols, shape, acc, dep):
            for r in range(4):
                st = stq[sti[0] % 4]; sti[0] += 1
                for fn, src in cands[r]:
                    P.dma("sp", fn(st.t), src, reads=list(dep), writes=[st.b])
                for (accv, stv, scol) in shape(st.t):
                    sc_ = sel.t[:, scol + r:scol + r + 1]
                    if r == 0:
                        P.op("dve", lambda e, accv=accv, stv=stv, sc_=sc_: e.tensor_scalar(out=accv, in0=stv, scalar1=sc_, scalar2=None, op0=ALU.mult),
                             reads=[st.b, sel.b], writes=[acc.b])
                    else:
                        P.op("dve", lambda e, accv=accv, stv=stv, sc_=sc_: e.scalar_tensor_tensor(out=accv, in0=stv, scalar=sc_, in1=accv, op0=ALU.mult, op1=ALU.add),
                             reads=[st.b, sel.b, acc.b], writes=[acc.b])
            for dd_, ss_ in (dst_ap if isinstance(dst_ap, list) else [dst_ap]):
                P.dma("sp", dd_, ss_, reads=[acc.b])

        acc = next_acc()
        a3 = acc.t[:, 0:512].rearrange("p (h c) -> p h c", h=4)
        select((akT_d[0][:, 0:128].rearrange("(h d) c -> d h c", d=128), a3),
               [[(lambda st: st[:, 0:512].rearrange("p (h c) -> p h c", h=4), kTa_rows(r, 0, 512)[:, 896:1024].rearrange("(h d) c -> d h c", d=128))] for r in range(4)],
               None, lambda st: [(a3, st[:, 0:512].rearrange("p (h c) -> p h c", h=4), 0)], acc, BkTa)
        acc = next_acc()
        a3 = acc.t[:, 0:256].rearrange("p (h c) -> p h c", h=2)
        select((bkT_d[:, 0:128].rearrange("(h d) c -> d h c", d=128), a3),
               [[(lambda st: st[:, 0:256].rearrange("p (h c) -> p h c", h=2), kTa_rows(r, 1536, 256)[:, 896:1024].rearrange("(h d) c -> d h c", d=128))] for r in range(4)],
               None, lambda st: [(a3, st[:, 0:256].rearrange("p (h c) -> p h c", h=2), 0)], acc, BkTa)
        acc = next_acc()
        a2_ = acc.t[:, 0:512]
        select((aV_d[0][0:128, :], a2_),
               [[(lambda st: st[:, 0:512], va_cols(0, 512)[r * 1024 + 896:r * 1024 + 1024, :])] for r in range(4)],
               None, lambda st: [(a2_, st[:, 0:512], 0)], acc, Bva)
        acc = next_acc()
        a2_ = acc.t[:, 0:256]
        select((bV_d[0:128, :], a2_),
               [[(lambda st: st[:, 0:256], va_cols(1536, 256)[r * 1024 + 896:r * 1024 + 1024, :])] for r in range(4)],
               None, lambda st: [(a2_, st[:, 0:256], 0)], acc, Bva)
        acc = next_acc()
        a4 = acc.t[:, 0:2048].rearrange("p (h rr c) -> p h rr c", h=4, rr=4)
        select([(akT_d[1][h_ * 128:(h_ + 1) * 128, :].rearrange("d (rr w) -> d rr w", rr=4)[:, :, 0:128], a4[:, h_]) for h_ in range(4)],
               [[(lambda st, h_=h_: st[:, h_ * 512:(h_ + 1) * 512].rearrange("p (rr c) -> p rr c", rr=4),
                  kTa_rows(r, 512 + h_ * 128, 128).rearrange("d (rr i) -> d rr i", rr=4)[:, :, 128:256])
                 for h_ in range(4)] for r in range(4)],
               None, lambda st: [(a4, st[:].rearrange("p (h rr c) -> p h rr c", h=4, rr=4), 0)], acc, BkTa)
        acc = next_acc()
        a3 = acc.t[:, 0:2048].rearrange("p (rr c) -> p rr c", rr=4)
        select((aV_d[1].rearrange("(rr w) c -> w rr c", rr=4)[0:128, :, :], a3),
               [[(lambda st: st[:].rearrange("p (rr c) -> p rr c", rr=4),
                  va_cols(512, 512)[r * 1024 + 512:r * 1024 + 1024, :].rearrange("(i rr) c -> i rr c", rr=4))] for r in range(4)],
               None, lambda st: [(a3, st[:].rearrange("p (rr c) -> p rr c", rr=4), 0)], acc, Bva)
        for hp in range(2):
            acc = next_acc()
            a4 = acc.t[:].rearrange("p (h rr c) -> p h rr c", h=2, rr=16)
            select([(akT_d[2][(hp * 2 + hh) * 128:(hp * 2 + hh + 1) * 128, :].rearrange("d (rr w) -> d rr w", rr=16)[:, :, 0:128], a4[:, hh]) for hh in range(2)],
                   [[(lambda st, hh=hh: st[:, hh * 1024:(hh + 1) * 1024],
                      kTa_rows(r, 1024 + (hp * 2 + hh) * 128, 128)) for hh in range(2)] for r in range(4)],
                   None, lambda st: [(a4[:, :, :, 0:64], st[:].rearrange("p (h rr c) -> p h rr c", h=2, rr=16), 8),
                                     (a4[:, :, :, 64:128], st[:].rearrange("p (h rr c) -> p h rr c", h=2, rr=16), 0)], acc, BkTa)
        for q4 in range(4):
            acc = next_acc()
            a3 = acc.t[:, 0:2048].rearrange("p (rr c) -> p rr c", rr=4)
            cands = []
            for r in range(4):
                src = va_cols(1024, 512)[r * 1024:(r + 1) * 1024, :].rearrange("(i rr) c -> i rr c", rr=16)[:, q4 * 4:(q4 + 1) * 4, :]
                cands.append([(lambda st: st[0:64, :].rearrange("p (rr c) -> p rr c", rr=4), src),
                              (lambda st: st[64:128, :].rearrange("p (rr c) -> p rr c", rr=4), src)])
            select((aV_d[2].rearrange("(rr w) c -> w rr c", rr=16)[0:128, q4 * 4:(q4 + 1) * 4, :], a3), cands,
                   None, lambda st: [(a3, st[:].rearrange("p (rr c) -> p rr c", rr=4), 4)], acc, Bva)
        for g in range(3):
            dd = A_DD[g]; L = 1024 // dd
            P.dma("sp", akT_d[g].rearrange("q (rr w) -> q rr w", rr=dd)[:, :, 128:128 + L],
                  kTl_rows(g * 512, 512).rearrange("q (rr i) -> q rr i", rr=dd), reads=BkTl)
            P.dma("sp", aV_d[g].rearrange("(rr w) c -> rr w c", rr=dd)[:, 128:128 + L, :],
                  vl_cols(g * 512, 512).rearrange("(i rr) c -> rr i c", rr=dd), reads=Bvl)
        P.dma("sp", bkT_d[:, 128:1152], kTl_rows(1536, 256), reads=BkTl)
        P.dma("sp", bV_d[128:1152, :], vl_cols(1536, 256), reads=Bvl)
        gsi = 0
        for oc in range(16):
            wa = load_w(kb, w_d, 7168 + oc * 512, 256); wb_ = load_w(kb, w_d, 7168 + oc * 512 + 256, 256)
            for th in range(2):
                for b_ in range(4):
                    pp = next_pp(kb)
                    wv, wb = (wa, wb_)[b_ // 2]
                    proj_fm(kb, wv, wb, (b_ % 2) * 128, 128, hT, th * 512, 512, pp)
                    g_ = gsb[gsi % 4]; gsi += 1
                    P.op("act", lambda e, pp=pp, b_=b_, oc=oc, g_=g_: e.activation(out=g_.t[:], in_=pp.t[:, 0:512], func=AF.Sigmoid,
                                                                                  bias=bg.t[:, oc * 4 + b_:oc * 4 + b_ + 1]),
                         reads=[pp.b, bg.b], writes=[g_.b])
                    P.dma("act", gates_d[(oc * 4 + b_) * 128:(oc * 4 + b_ + 1) * 128, th * 512:(th + 1) * 512], g_.t[:], reads=[g_.b], writes=[Bgates])
        P.barrier()
        wcol = [0]

        def wchunk():
            v = load_w(kb, w_d, wcol[0], 256)
            wcol[0] += 256
            return v

        def do_q(wv, wb, col, gcol, slot, dd=1):
            for th in range(2):
                if dd == 1:
                    dst = qT.t[:, slot, th * 512:(th + 1) * 512]
                else:
                    n = 512 // dd
                    dst = qT.t[:, slot, :].rearrange("p (r i) -> p i r", r=dd)[:, th * n:(th + 1) * n, :]
                push(lambda pp, wv=wv, wb=wb, col=col, th=th: proj_fm(kb, wv, wb, col, 128, hT, th * 512, 512, pp),
                     lambda pp, gcol=gcol, dst=dst, dd=dd: qknorm(kb, pp, 512, gq.t[:, gcol:gcol + 1], gq.b, dst, qT.b, dd))

        def do_z(wv, wb, col, slot):
            for th in range(2):
                pp = next_pp(kb)
                proj_fm(kb, wv, wb, col, 128, hT, th * 512, 512, pp)
                P.op("act", lambda e, pp=pp, th=th: e.activation(out=zs.t[:, slot, th * 512:(th + 1) * 512], in_=pp.t[:, 0:512], func=AF.Silu),
                     reads=[pp.b], writes=[zs.b])

        def dbg_y(y_ap, yb, row0, c0, n):
            if debug and l == nl - 1:
                finals.append(P.dma("sp", yd_d[row0:row0 + 128, c0:c0 + n], y_ap, reads=[yb]))

        sc = [kb.s0, kb.s1, kb.p0, kb.p1]
        sci = [0]

        def tiles_pipeline(tiles):
            drain()
            n = len(tiles)
            slots = []

            def qk(i):
                t = tiles[i]
                s = sc[sci[0] % 4]; p = pT[sci[0] % 4]; sci[0] += 1
                slots.append((s, p))
                nk, nq = t["nk"], t["nq"]

                def f(e):
                    last = e.matmul(s.t[0:nk, 0:nq], lhsT=t["kT"], rhs=t["q"], start=True, stop=(t["bias"] is None))
                    if t["bias"] is not None:
                        last = e.matmul(s.t[0:nk, 0:nq], lhsT=kb.ident[0:nk, 0:nk], rhs=t["bias"], start=False, stop=True)
                    return last
                P.op("pe", f, reads=list(t["kdeps"]) + [kb.cst.b], writes=[s.b])

            def ex_pv(i):
                t = tiles[i]
                s, p = slots[i]
                nk, nq = t["nk"], t["nq"]
                P.op("act", lambda e: e.activation(out=p.t[0:nk, 0:nq], in_=s.t[0:nk, 0:nq], func=AF.Exp, bias=t["bcol"]),
                     reads=[s.b] + list(t["bdeps"]), writes=[p.b])
                P.op("pe", lambda e: t["pv"](e, p.t[0:nk, 0:nq]), reads=[p.b, kb.cst.b] + list(t["pvr"]), writes=list(t["pvw"]))
                if t.get("after"):
                    t["after"]()
            DEPTH = 4
            for i in range(min(DEPTH, n)):
                qk(i)
            for i in range(n):
                ex_pv(i)
                if i + DEPTH < n:
                    qk(i + DEPTH)

        def fin_fm(u_ap, d_ap, ub, n, zslot, z0, chunk, c0, esk_col=None):
            a, b2 = tf[0], tf[1]
            if esk_col is not None:
                P.op("dve", lambda e: e.tensor_scalar(out=a.t[:, 0:n], in0=d_ap, scalar1=esk_col, scalar2=None, op0=ALU.add), reads=ub + [esk.b], writes=[a.b])
                P.op("dve", lambda e: e.reciprocal(out=a.t[:, 0:n], in_=a.t[:, 0:n]), reads=[a.b], writes=[a.b])
            else:
                P.op("dve", lambda e: e.reciprocal(out=a.t[:, 0:n], in_=d_ap), reads=ub, writes=[a.b])
            P.op("dve", lambda e: e.tensor_tensor(out=b2.t[:, 0:n], in0=u_ap, in1=a.t[:, 0:n], op=ALU.mult), reads=ub + [a.b], writes=[b2.b])
            dbg_y(b2.t[:, 0:n], b2.b, chunk * 128, c0, n)
            P.op("dve", lambda e: e.tensor_tensor(out=uT.t[:, chunk, c0:c0 + n], in0=b2.t[:, 0:n], in1=zs.t[:, zslot, z0:z0 + n], op=ALU.mult),
                 reads=[b2.b, zs.b], writes=[uT.b])

        for hh in range(4 if stage >= 1 else 0):
            wv, wb = wchunk()
            do_q(wv, wb, 0, 6, 0)
            do_z(wv, wb, 128, 0)
            for th in range(2):
                tl = []
                for mt in range(2):
                    def pv(e, p_ap, mt=mt, hh=hh):
                        e.matmul(kb.ud0.t[:, 0:512], lhsT=mv.t[:, mt, hh * 128:(hh + 1) * 128], rhs=p_ap, start=(mt == 0), stop=(mt == 1))
                        return e.matmul(kb.ud1.t[:, 0:512], lhsT=kb.ones, rhs=p_ap, start=(mt == 0), stop=(mt == 1))
                    tl.append(dict(kT=mkT.t[:, hh, mt * 128:(mt + 1) * 128], q=qT.t[:, 0, th * 512:(th + 1) * 512], nk=128, nq=512, bias=None,
                                   bcol=0.0, bdeps=[], pv=pv, pvr=[mv.b], pvw=[kb.ud0.b, kb.ud1.b], kdeps=[mkT.b, qT.b]))
                tiles_pipeline(tl)
                fin_fm(kb.ud0.t[:, 0:512], kb.ud1.t[:, 0:512], [kb.ud0.b, kb.ud1.b], 512, 0, th * 512, 20 + hh, th * 512)
        P.barrier()
        udl = [kb.ud0, kb.ud1]
        udi = [0]

        def band_block(kT_t, kTb, k0, nk1, q_ap, nq, bt, vt0, vt1, vb, hcol, after):
            ud = udl[udi[0] % 2]; udi[0] += 1
            tl = []
            for ti in range(2):
                nk = 128 if ti == 0 else nk1
                vv = vt0 if ti == 0 else vt1

                def pv(e, p_ap, ti=ti, nk=nk, vv=vv):
                    e.matmul(ud.t[:, 0:nq], lhsT=vv, rhs=p_ap, start=(ti == 0), stop=(ti == 1), skip_group_check=True)
                    return e.matmul(ud.t[:, 128:128 + nq], lhsT=kb.ones[0:nk, :], rhs=p_ap, start=False, stop=(ti == 1), skip_group_check=True)
                tl.append(dict(kT=kT_t[:, k0 + ti * 128:k0 + ti * 128 + nk], q=q_ap, nk=nk, nq=nq, bias=bt[ti][0:nk, 0:nq],
                               bcol=(hcol if (ti == 0 and hcol is not None) else 0.0), bdeps=[hv.b, ab.b], pv=pv, pvr=[vb], pvw=[ud.b],
                               kdeps=[kTb, qT.b, ab.b], after=(lambda: after(ud)) if ti == 1 else None))
            return tl

        for kv in range(2 if (stage >= 2 and stage not in (30, 31)) else 0):
            R.reset()
            bk = R.take(1152, BF16); bvv = R.take(9 * 128, BF16)
            P.dma("sp", bk.t[:], bkT_d[kv * 128:(kv + 1) * 128, :], writes=[bk.b])
            bv3 = bvv.t[:].rearrange("p (a c) -> p a c", a=9)
            P.dma("sp", bv3, bV_d[:, kv * 128:(kv + 1) * 128].rearrange("(a p) c -> p a c", p=128), writes=[bvv.b])
            for gi in range(4):
                hd = kv * 4 + gi
                wv, wb = wchunk()
                do_q(wv, wb, 0, 2, 0)
                do_z(wv, wb, 128, 0)
                P.dma("sp", ab.t[:, 0:2, :], abias_d[:, (24 + hd * 2) * 128:(26 + hd * 2) * 128].rearrange("p (a c) -> p a c", a=2), writes=[ab.b])
                tls = []
                for qb in range(8):
                    def after(ud, qb=qb, hd=hd):
                        fin_fm(ud.t[:, 0:128], ud.t[:, 128:256], [ud.b], 128, 0, qb * 128, 4 + hd, qb * 128, esk_col=esk.t[:, hd:hd + 1])
                    tls += band_block(bk.t, bk.b, qb * 128, 128, qT.t[:, 0, qb * 128:(qb + 1) * 128], 128, (ab.t[:, 0, :], ab.t[:, 1, :]),
                                      bv3[:, qb, :], bv3[:, qb + 1, :], bvv.b, hv.t[:, 0:1] if qb == 0 else None, after)
                tiles_pipeline(tls)
            P.barrier()
        for h in range(4 if stage >= 3 else 0):
            R.reset()
            ak = [R.take(AW[g] + (64 if g == 2 else 0), BF16) for g in range(3)]
            av0 = R.take(9 * 128, BF16); av1 = R.take(12 * 128, BF16); av2h = R.take(16 * 128, BF16); av2o = R.take(16 * 128, BF16)
            Ua = R.take(T, F32); Sa = R.take(T, F32)
            for g in range(3):
                if g >= 1 and 'k' in _ASK:
                    continue
                if g == 2:
                    P.op("dve", lambda e: e.memset(ak[2].t[:, 3072:3136], 0.0), writes=[ak[2].b])
                P.dma("sp", ak[g].t[:, 0:AW[g]], akT_d[g][h * 128:(h + 1) * 128, :], writes=[ak[g].b])
            v0 = av0.t[:].rearrange("p (a c) -> p a c", a=9)
            v1 = av1.t[:].rearrange("p (a c) -> p a c", a=12)
            v2h = av2h.t[:].rearrange("p (a c) -> p a c", a=16)
            v2o = av2o.t[:].rearrange("p (a c) -> p a c", a=16)
            P.dma("sp", v0, aV_d[0][:, h * 128:(h + 1) * 128].rearrange("(a p) c -> p a c", p=128), writes=[av0.b])
            if 'k' not in _ASK:
                P.dma("sp", v1, aV_d[1][:, h * 128:(h + 1) * 128].rearrange("(a p) c -> p a c", p=128), writes=[av1.b])
            a2 = aV_d[2][:, h * 128:(h + 1) * 128].rearrange("(r w) c -> w r c", w=192)
            if 'v' not in _ASK:
                P.dma("sp", v2h[:, 0:8, :], a2[0:128, 0:8, :], writes=[av2h.b])
                P.dma("sp", v2h[:, 8:16, :], a2[0:128, 8:16, :], writes=[av2h.b])
                P.op("dve", lambda e: e.memset(av2o.t[:], 0.0), writes=[av2o.b])
                P.dma("sp", v2o[0:64, :, :], a2[128:192, :, :], writes=[av2o.b])
            wv, wb = wchunk()
            do_q(wv, wb, 0, 0, 0)
            do_q(wv, wb, 128, 0, 1, 1 if 'q' in _ASK else 4)
            wv, wb = wchunk()
            do_q(wv, wb, 0, 0, 2, 1 if 'q' in _ASK else 16)
            do_z(wv, wb, 128, 0)
            for g in range(3):
                P.dma("sp", ab.t[:, 2 * g:2 * g + 2, :], abias_d[:, ((g * 4 + h) * 2) * 128:((g * 4 + h) * 2 + 2) * 128].rearrange("p (a c) -> p a c", a=2),
                      writes=[ab.b])

            def evac(ud, nq, dst_u, dst_s, first):
                if first:
                    P.op("dve", lambda e: e.tensor_copy(out=dst_u, in_=ud.t[:, 0:nq]), reads=[ud.b], writes=[Ua.b])
                    P.op("dve", lambda e: e.tensor_copy(out=dst_s, in_=ud.t[:, 128:128 + nq]), reads=[ud.b], writes=[Sa.b])
                else:
                    P.op("dve", lambda e: e.tensor_tensor(out=dst_u, in0=dst_u, in1=ud.t[:, 0:nq], op=ALU.add), reads=[ud.b, Ua.b], writes=[Ua.b])
                    P.op("dve", lambda e: e.tensor_tensor(out=dst_s, in0=dst_s, in1=ud.t[:, 128:128 + nq], op=ALU.add), reads=[ud.b, Sa.b], writes=[Sa.b])
            tls = []
            for qb in range(0 if 'g' in _ASK else 8):
                tls += band_block(ak[0].t, ak[0].b, qb * 128, 128, qT.t[:, 0, qb * 128:(qb + 1) * 128], 128, (ab.t[:, 0, :], ab.t[:, 1, :]),
                           v0[:, qb, :], v0[:, qb + 1, :], av0.b, hv.t[:, 0:1] if qb == 0 else None,
                           lambda ud, qb=qb: evac(ud, 128, Ua.t[:, qb * 128:(qb + 1) * 128], Sa.t[:, qb * 128:(qb + 1) * 128], True))
            for r in range(4 if stage != 31 else 0):
                for sbk in range(2):
                    st0 = 4 * sbk * 128 + r
                    tls += band_block(ak[1].t, ak[1].b, r * 384 + sbk * 128, 128, qT.t[:, 1, r * 256 + sbk * 128:r * 256 + sbk * 128 + 128], 128,
                               (ab.t[:, 2, :], ab.t[:, 3, :]), v1[:, r * 3 + sbk, :], v1[:, r * 3 + sbk + 1, :], av1.b,
                               hv.t[:, 0:1] if sbk == 0 else None,
                               lambda ud, st0=st0: evac(ud, 128, Ua.t[:, st0:st0 + 509:4], Sa.t[:, st0:st0 + 509:4], False))
            for r in range(16 if stage != 31 else 0):
                tls += band_block(ak[2].t, ak[2].b, r * 192, 128, qT.t[:, 2, r * 64:(r + 1) * 64], 64, (ab.t[:, 4, :], ab.t[:, 5, :]),
                           v2h[:, r, :], v2o[:, r, :], av2h.b, hv.t[:, 1:2],
                           lambda ud, r=r: evac(ud, 64, Ua.t[:, r:1024:16], Sa.t[:, r:1024:16], False))
            tiles_pipeline(tls)
            for th in range(0 if 'f' in _ASK else 2):
                fin_fm(Ua.t[:, th * 512:(th + 1) * 512], Sa.t[:, th * 512:(th + 1) * 512], [Ua.b, Sa.b, av2o.b], 512, 0, th * 512, h, th * 512)
            P.barrier()
        for h in range(4 if (stage >= 4 and stage not in (30, 31)) else 0):
            R.reset()
            ck = R.take(2 * 4096, BF16); cva = R.take(32 * 257, BF16); cko = R.take(2 * 1024, BF16); cvo = R.take(8 * 257, BF16)
            o0s = R.take(2 * 257, F32); yf = R.take(256, F32); ysq = R.take(256, F32); yn = R.take(256, BF16); sm = R.take(8, F32)
            ck3 = ck.t[:].rearrange("p (a c) -> p a c", a=2); cko3 = cko.t[:].rearrange("p (a c) -> p a c", a=2)
            cva3 = cva.t[:].rearrange("p (a c) -> p a c", a=32); cvo3 = cvo.t[:].rearrange("p (a c) -> p a c", a=8)
            o03 = o0s.t[:].rearrange("p (a c) -> p a c", a=2)
            for c_ in range(2):
                for r_ in range(4):
                    P.dma("sp", ck3[:, c_, r_ * 1024:(r_ + 1) * 1024],
                          kTa_rows(r_, (14 + h * 2 + c_) * 128, 128), writes=[ck.b])
                P.dma("sp", cko3[:, c_, :], kTl_rows((14 + h * 2 + c_) * 128, 128), writes=[cko.b])
            for q4 in range(4):
                P.dma("sp", cva3[:, q4 * 8:(q4 + 1) * 8, 0:256],
                      va_cols(1792 + h * 256, 256)[q4 * 1024:(q4 + 1) * 1024, :].rearrange("(a p) c -> p a c", p=128), writes=[cva.b])
            P.dma("sp", cvo3[:, :, 0:256], vl_cols(1792 + h * 256, 256).rearrange("(a p) c -> p a c", p=128), writes=[cvo.b])
            P.op("dve", lambda e: e.memset(cva3[:, :, 256:257], 1.0), writes=[cva.b])
            P.op("dve", lambda e: e.memset(cvo3[:, :, 256:257], 1.0), writes=[cvo.b])
            wv, wb = wchunk()
            do_q(wv, wb, 0, 4, 0)
            do_q(wv, wb, 128, 4, 1)
            wv, wb = wchunk()
            do_z(wv, wb, 0, 0)
            do_z(wv, wb, 128, 1)
            for g in range(4):
                for c_ in range(2):
                    tl = []
                    nt = 32 + 2 * g + 2
                    for ti in range(nt):
                        own = ti >= 32
                        lk = ti - 32

                        def pv(e, p_ap, ti=ti, own=own, lk=lk, nt=nt):
                            vv = cvo3[:, lk, :] if own else cva3[:, ti, :]
                            e.matmul(kb.ud0.t[:, 0:257], lhsT=p_ap[:, 0:128], rhs=vv, start=(ti == 0), stop=(ti == nt - 1))
                            return e.matmul(kb.ud1.t[:, 0:257], lhsT=p_ap[:, 128:256], rhs=vv, start=(ti == 0), stop=(ti == nt - 1))
                        bias = None
                        if own and lk == 2 * g:
                            bias = cmask.t[:, 0:256]
                        elif own and lk == 2 * g + 1:
                            bias = cmask.t[:, 256:512]
                        kT_ap = cko3[:, c_, lk * 128:(lk + 1) * 128] if own else ck3[:, c_, ti * 128:(ti + 1) * 128]
                        bcol = cbo.t[:, (h * 4 + g) * 8 + lk:(h * 4 + g) * 8 + lk + 1] if own else cb.t[:, (h * 4 + g) * 32 + ti:(h * 4 + g) * 32 + ti + 1]
                        tl.append(dict(kT=kT_ap, q=qT.t[:, c_, g * 256:(g + 1) * 256], nk=128, nq=256, bias=bias, bcol=bcol, bdeps=[cb.b, cbo.b, cmask.b],
                                       pv=pv, pvr=[cva.b, cvo.b], pvw=[kb.ud0.b, kb.ud1.b], kdeps=[ck.b, cko.b, qT.b, cmask.b]))
                    tiles_pipeline(tl)
                    if c_ == 0:
                        P.op("act", lambda e: e.activation(out=o03[:, 0, :], in_=kb.ud0.t[:, 0:257], func=AF.Copy), reads=[kb.ud0.b], writes=[o0s.b])
                        P.op("act", lambda e: e.activation(out=o03[:, 1, :], in_=kb.ud1.t[:, 0:257], func=AF.Copy), reads=[kb.ud1.b], writes=[o0s.b])
                for j in range(2):
                    ud = udl[j]
                    qblk = 2 * g + j
                    P.op("dve", lambda e, j=j: e.reciprocal(out=sm.t[:, 0:1], in_=o03[:, j, 256:257]), reads=[o0s.b], writes=[sm.b])
                    P.op("dve", lambda e, ud=ud: e.reciprocal(out=sm.t[:, 1:2], in_=ud.t[:, 256:257]), reads=[ud.b], writes=[sm.b])
                    P.op("dve", lambda e: e.tensor_tensor(out=sm.t[:, 1:2], in0=sm.t[:, 1:2], in1=nlam.t[:, 0:1], op=ALU.mult), reads=[sm.b, nlam.b], writes=[sm.b])
                    P.op("dve", lambda e, ud=ud: e.tensor_scalar(out=ysq.t[:], in0=ud.t[:, 0:256], scalar1=sm.t[:, 1:2], scalar2=None, op0=ALU.mult),
                         reads=[ud.b, sm.b], writes=[ysq.b])
                    P.op("dve", lambda e, j=j: e.scalar_tensor_tensor(out=yf.t[:], in0=o03[:, j, 0:256], scalar=sm.t[:, 0:1], in1=ysq.t[:],
                                                                      op0=ALU.mult, op1=ALU.add), reads=[o0s.b, sm.b, ysq.b], writes=[yf.b])
                    P.op("act", lambda e: e.activation(out=ysq.t[:], in_=yf.t[:], func=AF.Square), reads=[yf.b], writes=[ysq.b])
                    P.op("dve", lambda e: e.reduce_sum(out=sm.t[:, 2:3], in_=ysq.t[:], axis=AX.X), reads=[ysq.b], writes=[sm.b])
                    P.op("act", lambda e: e.activation(out=sm.t[:, 3:4], in_=sm.t[:, 2:3], func=AF.Ln, bias=kb.epsc.t[:, 0:1], scale=1.0 / 256),
                         reads=[sm.b, kb.epsc.b], writes=[sm.b])
                    P.op("act", lambda e: e.activation(out=sm.t[:, 3:4], in_=sm.t[:, 3:4], func=AF.Exp, scale=-0.5), reads=[sm.b], writes=[sm.b])
                    P.op("dve", lambda e: e.scalar_tensor_tensor(out=yn.t[:], in0=yf.t[:], scalar=sm.t[:, 3:4], in1=subg.t[:], op0=ALU.mult, op1=ALU.mult),
                         reads=[yf.b, sm.b, subg.b], writes=[yn.b])

                    def tr(e):
                        e.transpose(out=kb.ptr.t[:, 0:128], in_=yn.t[:, 0:128], identity=kb.ident)
                        return e.transpose(out=kb.ptr.t[:, 128:256], in_=yn.t[:, 128:256], identity=kb.ident)
                    P.op("pe", tr, reads=[yn.b, kb.cst.b], writes=[kb.ptr.b])
                    for e2 in range(2):
                        if debug and l == nl - 1:
                            P.op("dve", lambda e, e2=e2: e.tensor_copy(out=tf[2].t[:, 0:128], in_=kb.ptr.t[:, e2 * 128:(e2 + 1) * 128]), reads=[kb.ptr.b], writes=[tf[2].b])
                            dbg_y(tf[2].t[:, 0:128], tf[2].b, (12 + 2 * h + e2) * 128, qblk * 128, 128)
                        P.op("dve", lambda e, e2=e2, qblk=qblk, h=h: e.tensor_tensor(out=uT.t[:, 12 + 2 * h + e2, qblk * 128:(qblk + 1) * 128],
                                                                                in0=kb.ptr.t[:, e2 * 128:(e2 + 1) * 128],
                                                                                in1=zs.t[:, e2, qblk * 128:(qblk + 1) * 128], op=ALU.mult),
                             reads=[kb.ptr.b, zs.b], writes=[uT.b])
            P.barrier()
        R.reset()
        mT = R.take(KC * T, BF16)
        mT3 = TB.__new__(TB); mT3.t = mT.t[:].rearrange("p (k t) -> p k t", k=KC); mT3.b = mT.b
        bch = [(0, 4), (4, 12), (12, 20), (20, 24)]
        pb = [kb.s0, kb.s1, kb.ud0, kb.ud1]
        wcol[0] = 15360 - 8192
        for oc in range(16 if (stage >= 5 and stage not in (30, 31)) else 0):
            r = next_ring(kb)
            wbr = r.t[:, 0:24 * 128].rearrange("p (k c) -> p k c", k=24)
            for q2 in range(2):
                P.dma("pool", wbr[:, q2 * 12:(q2 + 1) * 12, :],
                      wbr_d[q2 * 1536:(q2 + 1) * 1536, oc * 128:(oc + 1) * 128].rearrange("(k p) c -> p k c", p=128), writes=[r.b])
            for th in range(2):
                for b_ in range(4):
                    P.dma("sp", gsb[b_].t[:], gates_d[(oc * 4 + b_) * 128:(oc * 4 + b_ + 1) * 128, th * 512:(th + 1) * 512],
                          reads=[Bgates], writes=[gsb[b_].b])
                for b_ in range(4):
                    def f(e, b_=b_, th=th, wbr=wbr):
                        last = None
                        lo, hi = bch[b_]
                        for k in range(lo, hi):
                            last = e.matmul(pb[b_].t[:, 0:512], lhsT=wbr[:, k, :], rhs=uT.t[:, k, th * 512:(th + 1) * 512], start=(k == lo), stop=(k == hi - 1))
                        return last
                    P.op("pe", f, reads=[r.b, uT.b], writes=[pb[b_].b])
                P.op("dve", lambda e: e.tensor_tensor(out=tf[0].t[:], in0=pb[0].t[:, 0:512], in1=gsb[0].t[:], op=ALU.mult), reads=[pb[0].b, gsb[0].b], writes=[tf[0].b])
                for b_ in range(1, 4):
                    P.op("dve", lambda e, b_=b_: e.tensor_tensor(out=tf[1].t[:], in0=pb[b_].t[:, 0:512], in1=gsb[b_].t[:], op=ALU.mult),
                         reads=[pb[b_].b, gsb[b_].b], writes=[tf[1].b])
                    if b_ < 3:
                        P.op("dve", lambda e: e.tensor_tensor(out=tf[0].t[:], in0=tf[0].t[:], in1=tf[1].t[:], op=ALU.add), reads=[tf[0].b, tf[1].b], writes=[tf[0].b])
                    else:
                        P.op("dve", lambda e, oc=oc, th=th: e.tensor_tensor(out=mT3.t[:, oc, th * 512:(th + 1) * 512], in0=tf[0].t[:], in1=tf[1].t[:], op=ALU.add),
                             reads=[tf[0].b, tf[1].b], writes=[mT.b])
        if debug and l == nl - 1:
            finals.append(P.dma("sp", md_d.rearrange("(k p) t -> p k t", p=128), mT3.t[:], reads=[mT.b]))
        i_ = 0
        for cc in range(8):
            wv, wb = load_w(kb, wout_d, cc * 256, 256)
            for t in range(8):
                pp = next_pp(kb)
                proj_tm(kb, wv, wb, 0, 256, mT3, t * 128, pp)
                x1, o1 = xs[i_ % 2], ot[i_ % 2]
                i_ += 1
                P.dma("sp", x1.t[:], x_src[t * 128:(t + 1) * 128, cc * 256:(cc + 1) * 256], reads=list(xdep), writes=[x1.b])
                P.op("dve", lambda e, pp=pp, x1=x1, o1=o1: e.tensor_tensor(out=o1.t[:], in0=pp.t[:, 0:256], in1=x1.t[:], op=ALU.add),
                     reads=[pp.b, x1.b], writes=[o1.b])
                fo = P.dma("sp", x_dst[t * 128:(t + 1) * 128, cc * 256:(cc + 1) * 256], o1.t[:], reads=[o1.b], writes=list(xdst_b))
                if l == nl - 1:
                    finals.append(fo)
    P.emit(kb.nc, finals)
    kb.st.close()
    return kb.nc


_BF = ml_dtypes.bfloat16
_KCOLS = list(range(1536, 3072)) + list(range(5632, 5888)) + list(range(7168, 8192))
_VCOLS = list(range(3072, 4608)) + list(range(5888, 6144)) + list(range(8192, 9216))


def _main_cols():
    cols = []
    blk = lambda s: list(range(s, s + 128))
    for hh in range(4):
        cols += blk(9216 + hh * 128) + blk(12288 + hh * 128)
    for hd in range(8):
        cols += blk(4608 + hd * 128) + blk(10240 + hd * 128)
    for h in range(4):
        cols += blk(h * 128) + blk(512 + h * 128) + blk(1024 + h * 128) + blk(9728 + h * 128)
    for h in range(4):
        cols += blk(6144 + h * 256) + blk(6144 + h * 256 + 128) + blk(11264 + h * 256) + blk(11264 + h * 256 + 128)
    for oc in range(16):
        for b in range(4):
            cols += blk(12800 + b * 2048 + oc * 128)
    return cols


def _tables():
    consts = np.concatenate([np.eye(128), np.ones((128, 128)), np.full((128, 128), 1 / 128)], 1).astype(_BF)
    w = np.arange(128)[:, None].astype(np.float64)
    qi = np.arange(128)[None, :].astype(np.float64)
    ab = np.zeros((128, 40, 128), np.float32)
    for g in range(3):
        for h in range(4):
            sl = 2.0 ** (-2.0 * (h + 1)) * A_DD[g]
            ab[:, (g * 4 + h) * 2 + 0, :] = np.where(w >= qi, -sl * (qi + 128 - w), NEG)
            ab[:, (g * 4 + h) * 2 + 1, :] = np.where(w <= qi, -sl * (qi - w), NEG)
    for hd in range(8):
        sl = 2.0 ** (-(hd + 1.0))
        ab[:, 24 + hd * 2 + 0, :] = np.where(w >= qi + 1, -sl * (qi + 128 - w), NEG)
        ab[:, 24 + hd * 2 + 1, :] = np.where(w <= qi, -sl * (qi - w), NEG)
    abias = ab.reshape(128, 40 * 128).astype(_BF)
    tri = np.where(w <= qi, 0.0, NEG)
    cmask = np.concatenate([tri, np.zeros((128, 128)), np.full((128, 128), NEG), tri], 1).astype(_BF)
    jp = np.arange(128).astype(np.float64)
    cbo = np.zeros((128, 128), np.float32)
    cbs = []
    for j in range(4):
        cb = np.zeros((128, 512), np.float32)
        for h in range(4):
            sl = 2.0 ** (-2.0 * (h + 1))
            for g in range(4):
                for kb_ in range(32):
                    if kb_ < 8 * j:
                        cb[:, (h * 4 + g) * 32 + kb_] = sl * ((kb_ * 128 + jp) - (j * 1024 + g * 256 + 255))
                    else:
                        cb[:, (h * 4 + g) * 32 + kb_] = NEG
                for lk in range(8):
                    cbo[:, (h * 4 + g) * 8 + lk] = sl * ((lk * 128 + jp) - (g * 256 + 255))
        cbs.append(cb)
    hvs = []
    for j in range(4):
        hv = np.zeros((128, 2), np.float32)
        hv[:, 0] = 0.0 if j >= 1 else NEG
        hv[:64, 1] = 0.0 if j >= 2 else NEG
        hv[64:, 1] = 0.0 if j >= 1 else NEG
        hvs.append(hv)
    return consts, abias, cmask, cbs, cbo, hvs


def _exchange(kTs, vs):
    outs = []
    for b in range(2):
        kc = [np.asarray(kTs[4 * b + j]) for j in range(4)]
        v_all = np.concatenate([np.asarray(vs[4 * b + j]) for j in range(4)], 0)
        ckT = np.concatenate([k[14 * 128:22 * 128] for k in kc], 1)
        cV = np.ascontiguousarray(v_all[:, 1792:2816])
        for j in range(4):
            m = {}
            for g in range(3):
                dd = A_DD[g]
                L = 1024 // dd
                rows = slice(g * 512, (g + 1) * 512)
                KG = np.concatenate([k[rows].reshape(512, dd, L) for k in kc], 2)
                lo = j * L - 128
                if lo < 0:
                    seg = np.concatenate([np.zeros((512, dd, -lo), KG.dtype), KG[:, :, 0:(j + 1) * L]], 2)
                else:
                    seg = KG[:, :, lo:(j + 1) * L]
                m["akT%d" % g] = np.ascontiguousarray(seg.reshape(512, dd * (128 + L)))
                VG = v_all[:, g * 512:(g + 1) * 512].reshape(4096 // dd, dd, 512).transpose(1, 0, 2)
                if lo < 0:
                    segv = np.concatenate([np.zeros((dd, -lo, 512), VG.dtype), VG[:, 0:(j + 1) * L]], 1)
                else:
                    segv = VG[:, lo:(j + 1) * L]
                m["aV%d" % g] = np.ascontiguousarray(segv.reshape(dd * (128 + L), 512))
            KB_ = np.concatenate([k[12 * 128:14 * 128] for k in kc], 1)
            VB_ = v_all[:, 1536:1792]
            lo = j * 1024 - 128
            if lo < 0:
                m["bkT"] = np.ascontiguousarray(np.concatenate([np.zeros((256, 128), KB_.dtype), KB_[:, 0:1024]], 1))
                m["bV"] = np.ascontiguousarray(np.concatenate([np.zeros((128, 256), VB_.dtype), VB_[0:1024]], 0))
            else:
                m["bkT"] = np.ascontiguousarray(KB_[:, lo:lo + 1152])
                m["bV"] = np.ascontiguousarray(VB_[lo:lo + 1152])
            m["ckT"] = ckT
            m["cV"] = cV
            m["ckTo"] = np.ascontiguousarray(kc[j][14 * 128:22 * 128])
            m["cVo"] = np.ascontiguousarray(v_all[j * 1024:(j + 1) * 1024, 1792:2816])
            outs.append(m)
    return outs


_CACHE = {}


def run_layer(l, xs, P, debug=False):
    consts, abias, cmask, cbs, cbo, hvs = _tables()
    w_in = P["w_in"][l]
    wkv = np.ascontiguousarray(w_in[:, _KCOLS + _VCOLS])
    gk = np.ascontiguousarray(P["qk_gain"][l][[1, 3, 5]].T)
    ng = np.ascontiguousarray(P["norm_g"][l][None, :])
    if "kv" not in _CACHE:
        _CACHE["kv"] = build_kv()
    res = run_bass_kernel_spmd(_CACHE["kv"], [{"x": xs[c], "ng": ng, "wkv": wkv, "gk": gk, "consts": consts} for c in range(8)],
                               core_ids=list(range(8)))
    ex = _exchange([r["kT"] for r in res.results], [r["v"] for r in res.results])
    lam_init = 0.8 - 0.6 * math.exp(-0.3 * l)
    key = ("main", l, debug)
    if key not in _CACHE:
        _CACHE[key] = build_main(lam_init, debug)
    wm = np.ascontiguousarray(w_in[:, _main_cols()])
    bgm = np.ascontiguousarray(P["b_gate"][l].reshape(4, 16, 128).transpose(2, 1, 0).reshape(128, 64))
    common = {"ng": ng, "w": wm, "bg": bgm, "gq": np.ascontiguousarray(P["qk_gain"][l].T),
              "sk": np.ascontiguousarray(np.broadcast_to(P["sinks"][l][None, :], (128, 8))),
              "lam": np.ascontiguousarray(np.broadcast_to(P["lam"][l].reshape(1, 512), (128, 512))),
              "subg": np.ascontiguousarray(np.broadcast_to(P["subln_g"][l][None, :], (128, 256))),
              "mng": np.ascontiguousarray(P["mem_norm_g"][l][None, :]), "wmem": P["w_mem_kv"][l], "wbr": P["w_branch"][l],
              "wout": P["w_out"][l], "abias": abias, "cmask": cmask, "cbo": cbo, "consts": consts}
    in_maps = []
    for c in range(8):
        m = dict(common)
        m.update(ex[c])
        m["x"] = xs[c]
        m["mem"] = np.ascontiguousarray(P["mem"][c // 4])
        m["cb"] = cbs[c % 4]
        m["hv"] = hvs[c % 4]
        in_maps.append(m)
    res = run_bass_kernel_spmd(_CACHE[key], in_maps, core_ids=list(range(8)))
    if debug:
        return [r["xo"] for r in res.results], [r["ydbg"] for r in res.results], [r["mdbg"] for r in res.results]
    return [r["xo"] for r in res.results]


def _sel_tables():
    sels = []
    for j in range(4):
        sl = np.zeros((128, 12), np.float32)
        for r in range(4):
            sl[:, r] = 1.0 if r == j - 1 else 0.0
            sl[:64, 4 + r] = 1.0 if r == j - 2 else 0.0
            sl[64:, 4 + r] = 1.0 if r == j - 1 else 0.0
            sl[:, 8 + r] = 1.0 if r == j - 2 else 0.0
        sels.append(sl)
    return sels


def run_fused(nl, xs, P, debug=False, l0=0):
    consts, abias, cmask, cbs, cbo, hvs = _tables()
    sels = _sel_tables()
    Ls = list(range(l0, l0 + nl))
    lam_inits = [0.8 - 0.6 * math.exp(-0.3 * l) for l in Ls]
    key = ("fused", tuple(Ls), debug)
    if key not in _CACHE:
        _CACHE[key] = build_fused(lam_inits, debug)
    mc = _main_cols()
    cat = lambda f: np.ascontiguousarray(np.concatenate([f(l) for l in Ls], 0))
    common = {
        "ng": cat(lambda l: P["norm_g"][l][None, :]), "mng": cat(lambda l: P["mem_norm_g"][l][None, :]),
        "wkv": cat(lambda l: P["w_in"][l][:, _KCOLS + _VCOLS]), "w": cat(lambda l: P["w_in"][l][:, mc]),
        "wmem": cat(lambda l: P["w_mem_kv"][l]), "wbr": cat(lambda l: P["w_branch"][l]), "wout": cat(lambda l: P["w_out"][l]),
        "bg": cat(lambda l: P["b_gate"][l].reshape(4, 16, 128).transpose(2, 1, 0).reshape(128, 64)),
        "gq": cat(lambda l: P["qk_gain"][l].T), "gk": cat(lambda l: P["qk_gain"][l][[1, 3, 5]].T),
        "sk": cat(lambda l: np.broadcast_to(P["sinks"][l][None, :], (128, 8))),
        "lam": cat(lambda l: np.broadcast_to(P["lam"][l].reshape(1, 512), (128, 512))),
        "subg": cat(lambda l: np.broadcast_to(P["subln_g"][l][None, :], (128, 256))),
        "abias": abias, "cmask": cmask, "cbo": cbo, "consts": consts}
    in_maps = []
    for c in range(8):
        m = dict(common)
        m["x"] = xs[c]
        m["mem"] = np.ascontiguousarray(P["mem"][c // 4])
        m["cb"] = cbs[c % 4]
        m["hv"] = hvs[c % 4]
        m["sel"] = sels[c % 4]
        in_maps.append(m)
    res = run_bass_kernel_spmd(_CACHE[key], in_maps, core_ids=list(range(8)))
    if debug:
        return [r["xo"] for r in res.results], [r["ydbg"] for r in res.results], [r["mdbg"] for r in res.results]
    return [r["xo"] for r in res.results]


def kernel(x, mem, norm_g, w_in, b_gate, qk_gain, sinks, lam, subln_g, mem_norm_g, w_mem_kv, w_branch, w_out):
    P = dict(mem=np.asarray(mem), norm_g=np.asarray(norm_g), w_in=np.asarray(w_in), b_gate=np.asarray(b_gate),
             qk_gain=np.asarray(qk_gain), sinks=np.asarray(sinks), lam=np.asarray(lam), subln_g=np.asarray(subln_g),
             mem_norm_g=np.asarray(mem_norm_g), w_mem_kv=np.asarray(w_mem_kv), w_branch=np.asarray(w_branch), w_out=np.asarray(w_out))
    x = np.asarray(x)
    xs = [np.ascontiguousarray(x[c // 4, (c % 4) * 1024:(c % 4 + 1) * 1024]) for c in range(8)]
    xs = run_fused(4, xs, P)
    out = np.zeros_like(x)
    for c in range(8):
        out[c // 4, (c % 4) * 1024:(c % 4 + 1) * 1024] = xs[c]
    return out
```
